# Optimizing a Trainium2 kernel written in Bass

```python
import math
import jax, jax.numpy as jnp
from jax import lax
import numpy as np

D_MODEL = 1024
BATCH = 8
SEQ = 4096
DEPTH = 4

PLE_DIM = 256
N_BRANCH = 4
BRANCH_W = D_MODEL // N_BRANCH
CONV_W = 31
MLA_HEADS = 4
MLA_NOPE = 64
MLA_ROPE = 32
MLA_V = 64
MLA_Q_RANK = D_MODEL // 4
MLA_KV_RANK = D_MODEL // 8
ROPE_THETA = 10000.0
ATTN_BLOCK = 128
SSM_GROUP = 16
SSM_GROUPS = BRANCH_W // SSM_GROUP
SSM_STATE = 64
DT_MIN = 1e-3
DT_MAX = 1e-1
SWA_HEADS = 4
SWA_KV_HEADS = 2
SWA_HEAD_DIM = 64
WINDOW = 128
DEEPNORM_ALPHA = (2.0 * DEPTH) ** 0.25
DEEPNORM_BETA = (8.0 * DEPTH) ** -0.25
LN_EPS = 1e-5
RMS_EPS = 1e-6

IN_SPLITS = (
    BRANCH_W, BRANCH_W, BRANCH_W,
    MLA_Q_RANK, MLA_KV_RANK, MLA_ROPE, BRANCH_W,
    BRANCH_W, BRANCH_W,
    SWA_HEADS * SWA_HEAD_DIM, SWA_KV_HEADS * SWA_HEAD_DIM,
    SWA_KV_HEADS * SWA_HEAD_DIM, BRANCH_W,
)
IN_WIDTH = sum(IN_SPLITS)
IN_OFFSETS = tuple(sum(IN_SPLITS[:i + 1]) for i in range(len(IN_SPLITS) - 1))

kernel_name = 'hybrid_gated_parallel_mixers'


def layer_norm(x, g, b):
    xf = x.astype(jnp.float32)
    mu = jnp.mean(xf, axis=-1, keepdims=True)
    var = jnp.mean(jnp.square(xf - mu), axis=-1, keepdims=True)
    return ((xf - mu) * lax.rsqrt(var + LN_EPS) * g.astype(jnp.float32) + b.astype(jnp.float32)).astype(x.dtype)


def rms_norm(x, g):
    xf = x.astype(jnp.float32)
    ms = jnp.mean(jnp.square(xf), axis=-1, keepdims=True)
    return (xf * lax.rsqrt(ms + RMS_EPS) * g.astype(jnp.float32)).astype(x.dtype)


def rope_tables(seq):
    pos = jnp.arange(seq, dtype=jnp.float32)
    inv_freq = ROPE_THETA ** (-jnp.arange(0, MLA_ROPE, 2, dtype=jnp.float32) / MLA_ROPE)
    ang = pos[:, None] * inv_freq[None, :]
    return jnp.cos(ang), jnp.sin(ang)


def rope(x, cos, sin):
    half = x.shape[-1] // 2
    xf = x.astype(jnp.float32)
    x1, x2 = xf[..., :half], xf[..., half:]
    return jnp.concatenate([x1 * cos - x2 * sin, x2 * cos + x1 * sin], axis=-1).astype(x.dtype)


def conv_module(a_val, a_gate, conv_w, conv_b, norm_g, norm_b, w_pw2):
    h = a_val * jax.nn.sigmoid(a_gate)
    h = jnp.pad(h, ((0, 0), (CONV_W - 1, 0), (0, 0)))
    h = lax.conv_general_dilated(h, conv_w[:, None, :].astype(h.dtype), window_strides=(1,),
                                 padding='VALID', dimension_numbers=('NWC', 'WIO', 'NWC'),
                                 feature_group_count=BRANCH_W) + conv_b
    h = jax.nn.silu(layer_norm(h, norm_g, norm_b))
    return h @ w_pw2


def mla(c_q, c_kv, k_r, q_norm_g, kv_norm_g, w_uq, w_ukv, cos, sin):
    B, S, _ = c_q.shape
    q = (rms_norm(c_q, q_norm_g) @ w_uq).reshape(B, S, MLA_HEADS, MLA_NOPE + MLA_ROPE)
    q_nope = q[..., :MLA_NOPE]
    q_rope = rope(q[..., MLA_NOPE:], cos[:, None, :], sin[:, None, :])
    kv = (rms_norm(c_kv, kv_norm_g) @ w_ukv).reshape(B, S, MLA_HEADS, MLA_NOPE + MLA_V)
    k_nope, v = kv[..., :MLA_NOPE], kv[..., MLA_NOPE:]
    k_rope = rope(k_r, cos, sin)
    scale = (MLA_NOPE + MLA_ROPE) ** -0.5
    nblk = S // ATTN_BLOCK
    qn = q_nope.reshape(B, nblk, ATTN_BLOCK, MLA_HEADS, MLA_NOPE).transpose(1, 0, 2, 3, 4)
    qr = q_rope.reshape(B, nblk, ATTN_BLOCK, MLA_HEADS, MLA_ROPE).transpose(1, 0, 2, 3, 4)
    k_pos = jnp.arange(S)

    def block(args):
        qn_b, qr_b, i = args
        s = (jnp.einsum('bqhd,bkhd->bhqk', qn_b, k_nope)
             + jnp.einsum('bqhr,bkr->bhqk', qr_b, k_rope)).astype(jnp.float32) * scale
        q_pos = i * ATTN_BLOCK + jnp.arange(ATTN_BLOCK)
        s = jnp.where(k_pos[None, :] <= q_pos[:, None], s, -jnp.inf)
        prob = jax.nn.softmax(s, axis=-1).astype(v.dtype)
        return jnp.einsum('bhqk,bkhd->bqhd', prob, v)

    o = lax.map(block, (qn, qr, jnp.arange(nblk)))
    return o.transpose(1, 0, 2, 3, 4).reshape(B, S, MLA_HEADS * MLA_V)


def s5_layer(u, a_re, a_im, log_dt, b_re, b_im, c_re, c_im, d, w_glu):
    B, S, W = u.shape
    f32 = jnp.float32
    uf = u.astype(f32)
    ug = uf.reshape(B, S, SSM_GROUPS, SSM_GROUP)
    dt = jnp.exp(log_dt.astype(f32))[:, None]
    lr, li = a_re.astype(f32), a_im.astype(f32)
    mag = jnp.exp(lr * dt)
    lb_re, lb_im = mag * jnp.cos(li * dt), mag * jnp.sin(li * dt)
    den = lr * lr + li * li
    nr, ni = lb_re - 1.0, lb_im
    f_re = ((nr * lr + ni * li) / den)[..., None]
    f_im = ((ni * lr - nr * li) / den)[..., None]
    br, bi = b_re.astype(f32), b_im.astype(f32)
    bb_re = f_re * br - f_im * bi
    bb_im = f_re * bi + f_im * br
    bu_re = jnp.einsum('bsgh,gph->bsgp', ug, bb_re)
    bu_im = jnp.einsum('bsgh,gph->bsgp', ug, bb_im)
    a_r = jnp.broadcast_to(lb_re[None, None], (1, S, SSM_GROUPS, SSM_STATE))
    a_i = jnp.broadcast_to(lb_im[None, None], (1, S, SSM_GROUPS, SSM_STATE))

    def combine(e1, e2):
        a1r, a1i, x1r, x1i = e1
        a2r, a2i, x2r, x2i = e2
        return (a2r * a1r - a2i * a1i, a2r * a1i + a2i * a1r,
                a2r * x1r - a2i * x1i + x2r, a2r * x1i + a2i * x1r + x2i)

    _, _, h_re, h_im = lax.associative_scan(combine, (a_r, a_i, bu_re, bu_im), axis=1)
    y = (jnp.einsum('bsgp,ghp->bsgh', h_re, c_re.astype(f32))
         - jnp.einsum('bsgp,ghp->bsgh', h_im, c_im.astype(f32))).reshape(B, S, W)
    y = jax.nn.gelu(y + d.astype(f32) * uf).astype(u.dtype)
    g = y @ w_glu
    return g[..., :W] * jax.nn.sigmoid(g[..., W:])


def swa(q, k, v, sinks):
    B, S, _ = q.shape
    nb = S // WINDOW
    G = SWA_HEADS // SWA_KV_HEADS
    q = q.reshape(B, nb, WINDOW, SWA_KV_HEADS, G, SWA_HEAD_DIM)
    k = k.reshape(B, nb, WINDOW, SWA_KV_HEADS, SWA_HEAD_DIM)
    v = v.reshape(B, nb, WINDOW, SWA_KV_HEADS, SWA_HEAD_DIM)
    prev = lambda t: jnp.concatenate([jnp.zeros_like(t[:, :1]), t[:, :-1]], axis=1)
    k2 = jnp.concatenate([prev(k), k], axis=2)
    v2 = jnp.concatenate([prev(v), v], axis=2)
    s = jnp.einsum('bnqhgd,bnkhd->bnhgqk', q, k2).astype(jnp.float32) * (SWA_HEAD_DIM ** -0.5)
    qi = jnp.arange(WINDOW)[:, None] + WINDOW
    kj = jnp.arange(2 * WINDOW)[None, :]
    rel = qi - kj
    band = (rel >= 0) & (rel < WINDOW)
    blk = jnp.arange(nb)[:, None, None]
    valid = band[None] & ((blk > 0) | (kj >= WINDOW)[None])
    s = jnp.where(valid[None, :, None, None], s, -jnp.inf)
    sink = sinks.astype(jnp.float32).reshape(SWA_KV_HEADS, G)
    sink_col = jnp.broadcast_to(sink[None, None, :, :, None, None], s.shape[:-1] + (1,))
    prob = jax.nn.softmax(jnp.concatenate([s, sink_col], axis=-1), axis=-1)[..., :-1]
    o = jnp.einsum('bnhgqk,bnkhd->bnqhgd', prob.astype(v.dtype), v2)
    return o.reshape(B, S, SWA_HEADS * SWA_HEAD_DIM)


def hybrid_layer(x, p_i, cos, sin, w_in, w_merge, b_merge, conv_w, conv_b, conv_norm_g, conv_norm_b,
                 w_pw2, mla_q_norm_g, mla_kv_norm_g, w_uq, w_ukv, ssm_a_re, ssm_a_im, ssm_log_dt,
                 ssm_b_re, ssm_b_im, ssm_c_re, ssm_c_im, ssm_d, w_glu, attn_sinks, w_branch, w_out,
                 ln_g, ln_b, w_ple, w_ple_gate, ple_norm_g):
    B, S, D = x.shape
    h = x @ w_in
    (a_val, a_gate, a_z, c_q, c_kv, k_r, b_z, u, c_z, q, k, v, d_z) = jnp.split(h, IN_OFFSETS, axis=-1)
    y_a = conv_module(a_val, a_gate, conv_w, conv_b, conv_norm_g, conv_norm_b, w_pw2) * jax.nn.silu(a_z)
    y_b = mla(c_q, c_kv, k_r, mla_q_norm_g, mla_kv_norm_g, w_uq, w_ukv, cos, sin) * jax.nn.silu(b_z)
    y_c = s5_layer(u, ssm_a_re, ssm_a_im, ssm_log_dt, ssm_b_re, ssm_b_im, ssm_c_re, ssm_c_im,
                   ssm_d, w_glu) * jax.nn.silu(c_z)
    y_d = swa(q, k, v, attn_sinks) * jax.nn.silu(d_z)
    ys = jnp.stack([y_a, y_b, y_c, y_d], axis=2)
    branch = jnp.einsum('bsnw,nwd->bsnd', ys, w_branch)
    gates = jax.nn.sigmoid(x @ w_merge + b_merge).reshape(B, S, N_BRANCH, D)
    merged = jnp.einsum('bsnd,bsnd->bsd', gates, branch)
    x = layer_norm(DEEPNORM_ALPHA * x + merged @ w_out, ln_g, ln_b)
    e = (p_i @ w_ple) * jax.nn.sigmoid(x @ w_ple_gate)
    return x + rms_norm(e, ple_norm_g)


def setup_inputs(seed: int = 0) -> dict:
    key = jax.random.key(seed)
    ks = jax.random.split(key, 31)
    L, D, W = DEPTH, D_MODEL, BRANCH_W
    G, P, H = SSM_GROUPS, SSM_STATE, SSM_GROUP
    nrm = lambda k, shape, scale: jax.random.normal(k, shape, jnp.float32) * scale
    n_idx = jnp.arange(P, dtype=jnp.float32)
    return {
        'x': nrm(ks[0], (BATCH, SEQ, D), 1.0),
        'p': nrm(ks[1], (DEPTH, BATCH, SEQ, PLE_DIM), 1.0),
        'w_in': nrm(ks[2], (L, D, IN_WIDTH), D ** -0.5),
        'w_merge': nrm(ks[3], (L, D, N_BRANCH * D), D ** -0.5),
        'b_merge': nrm(ks[4], (L, N_BRANCH * D), 0.01),
        'conv_w': nrm(ks[5], (L, CONV_W, W), CONV_W ** -0.5),
        'conv_b': nrm(ks[6], (L, W), 0.01),
        'conv_norm_g': 1.0 + nrm(ks[7], (L, W), 0.01),
        'conv_norm_b': nrm(ks[8], (L, W), 0.01),
        'w_pw2': nrm(ks[9], (L, W, W), W ** -0.5),
        'mla_q_norm_g': 1.0 + nrm(ks[10], (L, MLA_Q_RANK), 0.01),
        'mla_kv_norm_g': 1.0 + nrm(ks[11], (L, MLA_KV_RANK), 0.01),
        'w_uq': nrm(ks[12], (L, MLA_Q_RANK, MLA_HEADS * (MLA_NOPE + MLA_ROPE)), MLA_Q_RANK ** -0.5),
        'w_ukv': nrm(ks[13], (L, MLA_KV_RANK, MLA_HEADS * (MLA_NOPE + MLA_V)), MLA_KV_RANK ** -0.5),
        'ssm_a_re': -0.5 + nrm(ks[14], (L, G, P), 0.01),
        'ssm_a_im': math.pi * n_idx + nrm(ks[15], (L, G, P), 0.01),
        'ssm_log_dt': jax.random.uniform(ks[16], (L, G), jnp.float32, math.log(DT_MIN), math.log(DT_MAX)),
        'ssm_b_re': nrm(ks[17], (L, G, P, H), (2 * H) ** -0.5),
        'ssm_b_im': nrm(ks[18], (L, G, P, H), (2 * H) ** -0.5),
        'ssm_c_re': nrm(ks[19], (L, G, H, P), P ** -0.5),
        'ssm_c_im': nrm(ks[20], (L, G, H, P), P ** -0.5),
        'ssm_d': nrm(ks[21], (L, W), 1.0),
        'w_glu': nrm(ks[22], (L, W, 2 * W), W ** -0.5),
        'attn_sinks': nrm(ks[23], (L, SWA_HEADS), 0.5),
        'w_branch': nrm(ks[24], (L, N_BRANCH, W, D), DEEPNORM_BETA * W ** -0.5),
        'w_out': nrm(ks[25], (L, D, D), DEEPNORM_BETA * D ** -0.5),
        'ln_g': 1.0 + nrm(ks[26], (L, D), 0.01),
        'ln_b': nrm(ks[27], (L, D), 0.01),
        'w_ple': nrm(ks[28], (L, PLE_DIM, D), PLE_DIM ** -0.5),
        'w_ple_gate': nrm(ks[29], (L, D, D), D ** -0.5),
        'ple_norm_g': 1.0 + nrm(ks[30], (L, D), 0.01),
    }


def reference(x, p, w_in, w_merge, b_merge, conv_w, conv_b, conv_norm_g, conv_norm_b, w_pw2,
              mla_q_norm_g, mla_kv_norm_g, w_uq, w_ukv, ssm_a_re, ssm_a_im, ssm_log_dt,
              ssm_b_re, ssm_b_im, ssm_c_re, ssm_c_im, ssm_d, w_glu, attn_sinks, w_branch, w_out,
              ln_g, ln_b, w_ple, w_ple_gate, ple_norm_g):
    cos, sin = rope_tables(x.shape[1])
    for i in range(DEPTH):
        x = hybrid_layer(x, p[i], cos, sin, w_in[i], w_merge[i], b_merge[i], conv_w[i], conv_b[i],
                         conv_norm_g[i], conv_norm_b[i], w_pw2[i], mla_q_norm_g[i], mla_kv_norm_g[i],
                         w_uq[i], w_ukv[i], ssm_a_re[i], ssm_a_im[i], ssm_log_dt[i], ssm_b_re[i],
                         ssm_b_im[i], ssm_c_re[i], ssm_c_im[i], ssm_d[i], w_glu[i], attn_sinks[i],
                         w_branch[i], w_out[i], ln_g[i], ln_b[i], w_ple[i], w_ple_gate[i], ple_norm_g[i])
    return x
```

```python
import math
import os
from contextlib import ExitStack
import numpy as np
import concourse.bass as bass
import concourse.mybir as mybir
from concourse.bass_utils import run_bass_kernel_spmd

F32 = mybir.dt.float32
BF16 = mybir.dt.bfloat16
I32 = mybir.dt.int32
ALU = mybir.AluOpType
AF = mybir.ActivationFunctionType

ENGS = ("pe", "dve", "act", "pool", "sp")
N_DMA_SEMS = 24

D = 1024
TB = 512
LN_EPS = 1e-5
RMS_EPS = 1e-6
TWO_PI = 2.0 * math.pi


def REC(name, *args, **kwargs):
    def fn(e):
        return getattr(e, name)(*args, **kwargs)
    return fn


class Sched:
    def __init__(self, nc):
        self.nc = nc
        self.ops = []
        self.last_w = {}
        self.readers = {}

    par = 0
    dbl = ()

    def _k(self, t):
        if self.dbl and isinstance(t, str) and t.rstrip("0123456789_") in self.dbl:
            return t + "#%d" % self.par
        return t

    def op(self, eng, fn, reads=(), writes=(), dma=False, final=False):
        reads = [self._k(t) for t in reads]
        writes = [self._k(t) for t in writes]
        idx = len(self.ops)
        deps = set()
        for t in reads:
            w = self.last_w.get(t)
            if w is not None:
                deps.add((w, "raw"))
        for t in writes:
            w = self.last_w.get(t)
            if w is not None:
                deps.add((w, "waw"))
            for r in self.readers.get(t, ()):
                deps.add((r, "war"))
        for t in writes:
            self.last_w[t] = idx
            self.readers[t] = []
        for t in reads:
            self.readers.setdefault(t, []).append(idx)
        self.ops.append(dict(eng=eng, fn=fn, deps=deps, dma=dma, final=final))
        return idx

    def pe(self, fn, r=(), w=()):
        return self.op("pe", fn, r, w)

    def dve(self, fn, r=(), w=()):
        return self.op("dve", fn, r, w)

    def act(self, fn, r=(), w=()):
        return self.op("act", fn, r, w)

    def pool(self, fn, r=(), w=()):
        return self.op("pool", fn, r, w)

    def dma(self, fn, r=(), w=(), q="sp", final=False):
        return self.op(q, fn, r, w, dma=True, final=final)

    def barrier(self):
        self.ops.append(dict(eng=None, fn=None, deps=set(), dma=False, final=False, barrier=True))
        self.last_w.clear()
        self.readers.clear()

    def emit(self, stack):
        nc = self.nc
        ops = self.ops
        n = len(ops)
        CE = ("pe", "dve", "act", "pool")
        seen = {e: {f: -1 for f in ENGS} for e in ENGS}
        seen_dma = {e: set() for e in ENGS}
        need = [[] for _ in range(n)]
        is_prod = [False] * n
        last_c = {e: -1 for e in CE}
        bar_need = {}
        for i, o in enumerate(ops):
            if o.get("barrier"):
                for e in ENGS:
                    lst = []
                    for e2 in CE:
                        j = last_c[e2]
                        if j >= 0 and e2 != e and j > seen[e][e2]:
                            lst.append(j)
                            is_prod[j] = True
                            seen[e][e2] = j
                    bar_need[(i, e)] = lst
                continue
            e = o["eng"]
            if not o["dma"]:
                last_c[e] = i
            best = {}
            for (d, kind) in o["deps"]:
                po = ops[d]
                pe_ = po["eng"]
                if po["dma"]:
                    if d in seen_dma[e]:
                        continue
                    best[("dma", d)] = d
                    continue
                if pe_ == e and not o["dma"]:
                    if kind != "raw" or e == "pe":
                        continue
                if d <= seen[e][pe_]:
                    continue
                k = ("c", pe_)
                if k not in best or best[k] < d:
                    best[k] = d
            for k, d in best.items():
                need[i].append(d)
                is_prod[d] = True
                if k[0] == "dma":
                    seen_dma[e].add(d)
                else:
                    seen[e][k[1]] = d
        csem = {e: stack.enter_context(nc.semaphore("c_" + e)) for e in ENGS}
        dsem = [stack.enter_context(nc.semaphore("d%d" % k)) for k in range(N_DMA_SEMS)]
        bsem = stack.enter_context(nc.semaphore("bar"))
        cnt = {e: 0 for e in ENGS}
        dcount = [0] * N_DMA_SEMS
        ndma = 0
        waitval = [None] * n
        dma_slot_prev = {}
        extra_wait = [None] * n
        slot_last = {}
        bar_idx = {}
        nbar = 0
        for i, o in enumerate(ops):
            if o.get("barrier"):
                slot_last[i] = dict(dma_slot_prev)
                nbar += 1
                bar_idx[i] = nbar
                continue
            if o["dma"]:
                slot = ndma % N_DMA_SEMS
                ndma += 1
                dcount[slot] += 16
                waitval[i] = (dsem[slot], dcount[slot])
                if slot in dma_slot_prev:
                    extra_wait[i] = dma_slot_prev[slot]
                dma_slot_prev[slot] = i
                o["dsem"] = dsem[slot]
            elif is_prod[i]:
                cnt[o["eng"]] += 1
                waitval[i] = (csem[o["eng"]], cnt[o["eng"]])
        per_eng = {e: [i for i, o in enumerate(ops) if o["eng"] == e or o.get("barrier")] for e in ENGS}
        block = stack.enter_context(nc.Block())
        self.n_waits = 0
        self.n_ins = {e: len(per_eng[e]) for e in ENGS}

        def run(engobj, ename):
            for i in per_eng[ename]:
                o = ops[i]
                if o.get("barrier"):
                    for d in bar_need[(i, ename)]:
                        s_, v = waitval[d]
                        engobj.wait_ge(s_, v)
                    if ename == "sp":
                        for slot, d in slot_last[i].items():
                            s_, v = waitval[d]
                            engobj.wait_ge(s_, v)
                        engobj.dma_start(out=self.bar_dst, in_=self.bar_src).then_inc(bsem, 16)
                    else:
                        engobj.wait_ge(bsem, 16 * bar_idx[i])
                    continue
                ws = list(need[i])
                if extra_wait[i] is not None:
                    ws.append(extra_wait[i])
                for d in ws:
                    s_, v = waitval[d]
                    engobj.wait_ge(s_, v)
                    self.n_waits += 1
                ins = o["fn"](engobj)
                if o["dma"]:
                    ins.then_inc(o["dsem"], 16)
                elif is_prod[i]:
                    ins.then_inc(csem[ename], 1)
            for i in per_eng[ename]:
                if ops[i]["dma"] and ops[i]["final"] and ops[i]["eng"] == ename:
                    s_, v = waitval[i]
                    engobj.wait_ge(s_, v)

        @block.tensor
        def _(e):
            run(e, "pe")

        @block.vector
        def _(e):
            run(e, "dve")

        @block.scalar
        def _(e):
            run(e, "act")

        @block.gpsimd
        def _(e):
            run(e, "pool")

        @block.sync
        def _(e):
            run(e, "sp")


V_CONVW = 0
V_CONVB = 62
V_CONVG = 64
V_CONVBETA = 66
V_QNG = 68
V_KVNG = 70
V_SSMD = 71
V_BMERGE = 73
V_LNG = 105
V_LNB = 113
V_PLEG = 121
V_LR = 129
V_LI = 137
V_LOGDT = 145
V_SINK = 153
NV = 160

C_IDENT = 0
C_TRI = 128
C_SWM = 256
C_IOTA = 512
NCM = 1024


def build(S, L, debug=False):
    NB = S // TB
    NT = S // 128
    nc = bass.Bass("TRN2", target_bir_lowering=False)

    def din(name, shape):
        return nc.dram_tensor(name, list(shape), F32, kind="ExternalInput").ap()

    xT = din("xT", [D, S])
    pT = din("pT", [L, 256, S])
    w1a = din("w1a", [L, D, 10 * 128])
    w1b = din("w1b", [L, D, 13 * 128])
    wuq = din("wuq", [L, 256, 8 * 128])
    wukv = din("wukv", [L, 128, 512])
    wpw2 = din("wpw2", [L, 256, 256])
    wglu = din("wglu", [L, 256, 512])
    vecs = din("vecs", [L, 128, NV])
    s5b = din("s5b", [L, 128, 2 * 8 * 16])
    s5c = din("s5c", [L, 128, 2 * 8 * 16])
    wmerge = din("wmerge", [L, D, 4 * D])
    wbranch = din("wbranch", [L, 4 * 256, D])
    wout = din("wout", [L, D, D])
    wple = din("wple", [L, 256, D])
    wpleg = din("wpleg", [L, D, D])
    ropec = din("ropec", [128, S])
    ropes = din("ropes", [128, S])
    cmat = din("cmat", [128, NCM])
    yT = nc.dram_tensor("yT", [D, S], F32, kind="ExternalOutput").ap()
    ysd = nc.dram_tensor("ysd", [D, S], BF16, kind=("ExternalOutput" if debug else "Internal")).ap()
    mgd = nc.dram_tensor("mgd", [D, S], BF16, kind=("ExternalOutput" if debug else "Internal")).ap()
    xs = nc.dram_tensor("xs", [D, S], F32, kind="Internal").ap()
    rotd = nc.dram_tensor("rotd", [128, 16 * TB], F32, kind="Internal").ap()

    st = ExitStack()
    with st:
        s = Sched(nc)
        uid = [0]

        def sb(shape, dt=F32, name=None):
            uid[0] += 1
            return st.enter_context(nc.sbuf_tensor("%s_%d" % (name or "t", uid[0]), list(shape), dt))

        psum = st.enter_context(nc.psum_tensor("psum", [128, 8 * 512], F32))
        psi = [0]

        def nps(nbanks=1):
            b = psi[0] % 5
            if nbanks == 2:
                while b % 2 == 1 or b + 2 > 5:
                    psi[0] += 1
                    b = psi[0] % 5
            psi[0] += nbanks
            key = tuple("ps%d" % (b + i) for i in range(nbanks))
            return psum[:, b * 512:(b + nbanks) * 512], key

        acci = [0]

        def nacc():
            b = 5 + acci[0] % 3
            acci[0] += 1
            return psum[:, b * 512:(b + 1) * 512], ("ps%d" % b,)

        cm = sb([128, NCM], F32, "cm")
        s.dma(REC("dma_start", out=cm[:], in_=cmat), w=["cm"])
        cmb = sb([128, 512], BF16, "cmb")
        s.dve(REC("tensor_copy", out=cmb[:], in_=cm[:, 0:512]), r=["cm"], w=["cmb"])
        ident_f = cm[:, C_IDENT:C_IDENT + 128]
        tri_b = cmb[:, C_TRI:C_TRI + 128]
        swm_b = cmb[:, C_SWM:C_SWM + 256]
        iota_f = cm[:, C_IOTA:C_IOTA + 512]
        bard = nc.dram_tensor("bard", [1, 16], F32, kind="Internal").ap()
        s.bar_dst = bard
        s.bar_src = cm[0:1, 0:16]
        ones_b = sb([128, 128], BF16, "ones")
        s.dve(REC("memset", ones_b[:], 1.0), w=["ones"])
        vec = sb([128, NV], F32, "vec")

        def V(c, n=1):
            return vec[:, c:c + n]

        def mm(ps_ap, pskey, pairs, extra_r=()):
            n = len(pairs)
            for i, (l, r, keys) in enumerate(pairs):
                M_ = l.shape[1]
                o_ap = ps_ap if M_ == 128 else ps_ap[0:M_, :]
                s.pe(REC("matmul", o_ap, lhsT=l, rhs=r, start=(i == 0), stop=(i == n - 1)),
                     r=list(keys) + list(extra_r), w=list(pskey))

        def rsqrt(out, in_, scale, eps, rk, wk, eng="dve"):
            s.dve(REC("tensor_scalar", out=out, in0=in_, scalar1=float(scale), scalar2=float(eps), op0=ALU.mult, op1=ALU.add), r=rk, w=wk)
            s.act(REC("activation", out=out, in_=out, func=AF.Sqrt), r=wk, w=wk)
            s.dve(REC("reciprocal", out=out, in_=out), r=wk, w=wk)

        WSL = {}

        def load_w(tile, dram2d, K, key, bounds=None, order=None, by_k=False):
            nk = max(1, K // 128)
            N = tile.shape[2]
            if by_k:
                WSL[key] = ["k"]
                for kt in range(nk):
                    s.dma(REC("dma_start", out=tile[:, kt, :], in_=dram2d[kt * 128:(kt + 1) * 128, :]), w=["%s@k%d" % (key, kt)], q="pool")
                return
            bounds = list(bounds or [0, N])
            WSL[key] = bounds
            for si in (order if order is not None else range(len(bounds) - 1)):
                c0, c1 = bounds[si], bounds[si + 1]
                for kt in range(nk):
                    s.dma(REC("dma_start", out=tile[:, kt, c0:c1], in_=dram2d[kt * 128:(kt + 1) * 128, c0:c1]), w=["%s@%d" % (key, si)], q="pool")

        def wkey(key, col=0, kt=None):
            bd = WSL[key]
            if bd[0] == "k":
                return "%s@k%d" % (key, kt)
            si = 0
            while si + 1 < len(bd) - 1 and col >= bd[si + 1]:
                si += 1
            return "%s@%d" % (key, si)

        xf = sb([128, 8, TB], F32, "xf")
        xb = sb([128, 8, TB], BF16, "xb")

        def load_x(l, b, xb=xb, xf=xf, cast=True):
            src = xT if l == 0 else xs
            for kt in range(8):
                s.dma(REC("dma_start", out=xf[:, kt, :], in_=src[kt * 128:(kt + 1) * 128, b * TB:(b + 1) * TB]),
                      w=["xf%d" % kt], r=(["xs_%d" % b] if l > 0 else []))
            if not cast:
                return
            for kt in range(8):
                if kt % 2 == 0:
                    s.act(REC("activation", out=xb[:, kt, :], in_=xf[:, kt, :], func=AF.Copy), r=["xf%d" % kt], w=["xb%d" % kt])
                else:
                    s.pool(REC("tensor_copy", out=xb[:, kt, :], in_=xf[:, kt, :]), r=["xf%d" % kt], w=["xb%d" % kt])

        XB = ["xb%d" % k for k in range(8)]

        def run_pipe(gen_fn, nblocks, lag, dbl):
            s.dbl = tuple(dbl)
            active = []
            nxt = 0
            while nxt < nblocks or active:
                if nxt < nblocks and len(active) < 2 and (not active or active[-1]["n"] >= lag):
                    active.append(dict(g=gen_fn(nxt), b=nxt, n=0, done=set(), blocked=None))
                    nxt += 1
                for a in list(active):
                    older = active[0] if (a is not active[0]) else None
                    if a["blocked"] is not None:
                        if older is None or a["blocked"] in older["done"]:
                            a["blocked"] = None
                        else:
                            continue
                    s.par = a["b"] % 2
                    try:
                        r = next(a["g"])
                        a["n"] += 1
                        if isinstance(r, tuple):
                            if r[0] == "done":
                                a["done"].add(r[1])
                            elif r[0] == "wait" and older is not None and r[1] not in older["done"]:
                                a["blocked"] = r[1]
                    except StopIteration:
                        active.remove(a)
            s.par = 0
            s.dbl = ()

        for l in range(L):
            s.dma(REC("dma_start", out=vec[:], in_=vecs[l]), w=["vec"])
            scA = ExitStack()
            SW = os.environ.get('KSWEEPS', 'A,B,C1,C2').split(',')
            with scA:
              if 'A' in SW:
                def sa(shape, dt=F32, name=None):
                    uid[0] += 1
                    return scA.enter_context(nc.sbuf_tensor("%s_%d" % (name or "a", uid[0]), list(shape), dt))
                scP = ExitStack()

                def sp2(shape, dt=F32, name=None):
                    uid[0] += 1
                    return scP.enter_context(nc.sbuf_tensor("%s_%d" % (name or "p", uid[0]), list(shape), dt))
                wA = sa([128, 8, 1280], BF16, "wA")
                if 'lw' not in os.environ.get('KA_SKIP', ''):
                    load_w(wA, w1a[l], D, "wA", bounds=[0, 512, 768, 1280], order=[1, 2, 0])
                wp2 = sa([128, 2, 256], BF16, "wp2")
                if 'lw' not in os.environ.get('KA_SKIP', ''):
                    load_w(wp2, wpw2[l], 256, "wp2")
                wgl = sa([128, 2, 512], BF16, "wgl")
                if 'lw' not in os.environ.get('KA_SKIP', ''):
                    load_w(wgl, wglu[l], 256, "wgl")
                dwt = sa([128, 62, 128], BF16, "dwt")
                for cj in (range(62) if 'dwt' not in os.environ.get('KA_SKIP', '') else []):
                    f = s.dve
                    f(REC("tensor_scalar", out=dwt[:, cj, :], in0=ident_f, scalar1=V(V_CONVW + cj), scalar2=None, op0=ALU.mult),
                      r=["cm", "vec"], w=["dwt"])
                sp_ = sa([128, 128], F32, "s5p")

                def P(i, n=8):
                    return sp_[:, i * 8:i * 8 + n]
                DT, MAG, TH, TI_F, THR, T2, LBR, LBI, DEN, FR, FI, NFI, TMP = range(13)
                spi = sa([128, 8], I32, "s5pi")
                CB = sa([128, 16, 128], BF16, "CB")
                BB = sa([128, 16, 128], BF16, "BB")
                K5 = ["s5p"]
                if 'prep' not in os.environ.get('KA_SKIP', ''):
                    s.act(REC("activation", out=P(DT), in_=V(V_LOGDT, 8), func=AF.Exp), r=["vec"], w=K5)
                    s.dve(REC("tensor_tensor", out=P(MAG), in0=V(V_LR, 8), in1=P(DT), op=ALU.mult), r=K5 + ["vec"], w=K5)
                    s.act(REC("activation", out=P(MAG), in_=P(MAG), func=AF.Exp), r=K5, w=K5)
                    s.dve(REC("scalar_tensor_tensor", out=P(TH), in0=V(V_LI, 8), scalar=float(1.0 / TWO_PI), in1=P(DT), op0=ALU.mult, op1=ALU.mult), r=K5 + ["vec"], w=K5)
                    s.dve(REC("tensor_copy", out=spi[:], in_=P(TH)), r=K5, w=["s5pi"])
                    s.dve(REC("tensor_copy", out=P(TI_F), in_=spi[:]), r=["s5pi"], w=K5)
                    s.dve(REC("tensor_tensor", out=P(THR), in0=P(TH), in1=P(TI_F), op=ALU.subtract), r=K5, w=K5)
                    s.act(REC("activation", out=P(LBI), in_=P(THR), func=AF.Sin, scale=6.28318), r=K5, w=K5)
                    s.dve(REC("tensor_scalar", out=P(T2), in0=P(THR), scalar1=0.25, scalar2=None, op0=ALU.add), r=K5, w=K5)
                    s.dve(REC("tensor_copy", out=spi[:], in_=P(T2)), r=K5, w=["s5pi"])
                    s.dve(REC("tensor_copy", out=P(TI_F), in_=spi[:]), r=["s5pi"], w=K5)
                    s.dve(REC("tensor_tensor", out=P(T2), in0=P(T2), in1=P(TI_F), op=ALU.subtract), r=K5, w=K5)
                    s.act(REC("activation", out=P(LBR), in_=P(T2), func=AF.Sin, scale=6.28318), r=K5, w=K5)
                    s.dve(REC("tensor_tensor", out=P(LBR), in0=P(LBR), in1=P(MAG), op=ALU.mult), r=K5, w=K5)
                    s.dve(REC("tensor_tensor", out=P(LBI), in0=P(LBI), in1=P(MAG), op=ALU.mult), r=K5, w=K5)
                    s.dve(REC("tensor_tensor", out=P(DEN), in0=V(V_LR, 8), in1=V(V_LR, 8), op=ALU.mult), r=["vec"] + K5, w=K5)
                    s.dve(REC("tensor_tensor", out=P(TMP), in0=V(V_LI, 8), in1=V(V_LI, 8), op=ALU.mult), r=["vec"] + K5, w=K5)
                    s.dve(REC("tensor_tensor", out=P(DEN), in0=P(DEN), in1=P(TMP), op=ALU.add), r=K5, w=K5)
                    s.dve(REC("reciprocal", out=P(DEN), in_=P(DEN)), r=K5, w=K5)
                    s.dve(REC("tensor_scalar", out=P(T2), in0=P(LBR), scalar1=-1.0, scalar2=None, op0=ALU.add), r=K5, w=K5)
                    s.dve(REC("tensor_tensor", out=P(FR), in0=P(T2), in1=V(V_LR, 8), op=ALU.mult), r=K5 + ["vec"], w=K5)
                    s.dve(REC("tensor_tensor", out=P(TMP), in0=P(LBI), in1=V(V_LI, 8), op=ALU.mult), r=K5 + ["vec"], w=K5)
                    s.dve(REC("tensor_tensor", out=P(FR), in0=P(FR), in1=P(TMP), op=ALU.add), r=K5, w=K5)
                    s.dve(REC("tensor_tensor", out=P(FR), in0=P(FR), in1=P(DEN), op=ALU.mult), r=K5, w=K5)
                    s.dve(REC("tensor_tensor", out=P(FI), in0=P(LBI), in1=V(V_LR, 8), op=ALU.mult), r=K5 + ["vec"], w=K5)
                    s.dve(REC("tensor_tensor", out=P(TMP), in0=P(T2), in1=V(V_LI, 8), op=ALU.mult), r=K5 + ["vec"], w=K5)
                    s.dve(REC("tensor_tensor", out=P(FI), in0=P(FI), in1=P(TMP), op=ALU.subtract), r=K5, w=K5)
                    s.dve(REC("tensor_tensor", out=P(FI), in0=P(FI), in1=P(DEN), op=ALU.mult), r=K5, w=K5)
                    s.dve(REC("tensor_scalar", out=P(NFI), in0=P(FI), scalar1=-1.0, scalar2=None, op0=ALU.mult), r=K5, w=K5)
                bst = sp2([128, 256], F32, "bst")
                cst = sp2([128, 256], F32, "cst")
                s.dma(REC("dma_start", out=bst[:], in_=s5b[l]), w=["bst"])
                s.dma(REC("dma_start", out=cst[:], in_=s5c[l]), w=["cst"])
                bb = sp2([128, 256], F32, "bb")
                tmpb = sp2([128, 128], F32, "tmpb")
                b3 = lambda t, ri: t[:, ri * 128:(ri + 1) * 128].rearrange("p (a h) -> p a h", h=16)
                fr_b = P(FR).unsqueeze(2).broadcast_to([128, 8, 16])
                fi_b = P(FI).unsqueeze(2).broadcast_to([128, 8, 16])
                nfi_b = P(NFI).unsqueeze(2).broadcast_to([128, 8, 16])
                t3 = tmpb[:, :].rearrange("p (a h) -> p a h", h=16)
                if 'bbc' not in os.environ.get('KA_SKIP', ''):
                    s.dve(REC("tensor_tensor", out=b3(bb, 0), in0=b3(bst, 0), in1=fr_b, op=ALU.mult), r=["bst"] + K5, w=["bb"])
                    s.dve(REC("tensor_tensor", out=t3, in0=b3(bst, 1), in1=nfi_b, op=ALU.mult), r=["bst"] + K5, w=["tmpb"])
                    s.dve(REC("tensor_tensor", out=b3(bb, 0), in0=b3(bb, 0), in1=t3, op=ALU.add), r=["bb", "tmpb"], w=["bb"])
                    s.dve(REC("tensor_tensor", out=b3(bb, 1), in0=b3(bst, 1), in1=fr_b, op=ALU.mult), r=["bst"] + K5, w=["bb"])
                    s.dve(REC("tensor_tensor", out=t3, in0=b3(bst, 0), in1=fi_b, op=ALU.mult), r=["bst", "bb"] + K5, w=["tmpb"])
                    s.dve(REC("tensor_tensor", out=b3(bb, 1), in0=b3(bb, 1), in1=t3, op=ALU.add), r=["bb", "tmpb"], w=["bb"])
                BT = sp2([128, 16, 128], F32, "BT")
                if 'ms' not in os.environ.get('KA_SKIP', ''):
                    s.pool(REC("memset", BT[:], 0.0), w=["BT"])
                    s.pool(REC("memset", CB[:], 0.0), w=["CB"])
                if 'scat' not in os.environ.get('KA_SKIP', ''):
                    for ri in range(2):
                        for half in range(2):
                            for grp in range(2):
                                prt = slice(half * 64, half * 64 + 64)
                                dst = BT[prt, ri * 8 + grp * 4: ri * 8 + grp * 4 + 4, :]
                                src = bb[prt, ri * 128 + grp * 64: ri * 128 + grp * 64 + 64].rearrange("p (a h) -> p a h", h=16)
                                for a in range(4):
                                    s.dve(REC("tensor_copy",
                                        out=BT[prt, ri * 8 + grp * 4 + a, a * 32 + half * 16: a * 32 + half * 16 + 16],
                                        in_=bb[prt, ri * 128 + (grp * 4 + a) * 16: ri * 128 + (grp * 4 + a) * 16 + 16]), r=["bb", "BT"], w=["BT"])
                                    if ri == 0:
                                        s.pool(REC("tensor_copy",
                                            out=CB[prt, grp * 4 + a, a * 32 + half * 16: a * 32 + half * 16 + 16],
                                            in_=cst[prt, (grp * 4 + a) * 16:(grp * 4 + a) * 16 + 16]), r=["cst", "CB"], w=["CB"])
                                    else:
                                        s.dve(REC("tensor_scalar",
                                            out=CB[prt, 8 + grp * 4 + a, a * 32 + half * 16: a * 32 + half * 16 + 16],
                                            in0=cst[prt, 128 + (grp * 4 + a) * 16:128 + (grp * 4 + a) * 16 + 16], scalar1=-1.0, scalar2=None, op0=ALU.mult), r=["cst", "CB"], w=["CB"])
                if 'tr' not in os.environ.get('KA_SKIP', ''):
                    for t in range(16):
                        pt, pk = nps()
                        s.pe(REC("transpose", pt[:, 0:128], BT[:, t, :], ident_f), r=["BT", "cm"], w=list(pk))
                        s.act(REC("activation", out=BB[:, t, :], in_=pt[:, 0:128], func=AF.Copy), r=list(pk), w=["BB"])
                rt2 = [sp2([128, TB], F32, "rt%d" % i) for i in range(2)]
                rti2 = [sp2([128, TB], I32, "rti%d" % i) for i in range(2)]
                rtf2 = [sp2([128, TB], F32, "rtf%d" % i) for i in range(2)]
                rto = [sp2([128, TB], F32, "rto%d" % i) for i in range(2)]
                if 'rot' not in os.environ.get('KA_SKIP', ''):
                    for p in range(8):
                        for cs in range(2):
                            ko = "rto%d" % cs
                            rt, rti, rtf = rt2[cs], rti2[cs], rtf2[cs]
                            krt, krti, krtf = "rt%d" % cs, "rti%d" % cs, "rtf%d" % cs
                            s.dve(REC("tensor_scalar", out=rt[:], in0=iota_f, scalar1=sp_[:, THR * 8 + p:THR * 8 + p + 1], scalar2=(0.25 if cs == 0 else 0.0), op0=ALU.mult, op1=ALU.add),
                                  r=["cm"] + K5, w=[krt])
                            s.dve(REC("tensor_copy", out=rti[:], in_=rt[:]), r=[krt], w=[krti])
                            s.dve(REC("tensor_copy", out=rtf[:], in_=rti[:]), r=[krti], w=[krtf])
                            s.dve(REC("tensor_tensor", out=rt[:], in0=rt[:], in1=rtf[:], op=ALU.subtract), r=[krt, krtf], w=[krt])
                            s.act(REC("activation", out=rto[cs][:], in_=rt[:], func=AF.Sin, scale=6.28318), r=[krt], w=[ko])
                            s.dma(REC("dma_start", out=rotd[:, (p * 2 + cs) * TB:(p * 2 + cs + 1) * TB], in_=rto[cs][:]), r=[ko], w=["rotd%d" % p])
                s.barrier()
                scP.close()
                carry = sa([128, 16], F32, "carry")
                s.dve(REC("memset", carry[:], 0.0), w=["carryR%d" % p for p in range(8)] + ["carryI%d" % p for p in range(8)])
                gbuf = sa([128, 2, 32 + TB], BF16, "gbuf")
                s.pool(REC("memset", gbuf[:], 0.0), w=["gbuf0", "gbuf1"])
                szA = sa([128, 4, TB], BF16, "szA")
                sg = [sa([128, TB], F32, "sg%d" % i) for i in range(2)]
                hcf = sa([128, 2, TB], F32, "hcf")
                hcb = sa([128, 2, TB], BF16, "hcb")
                hsq = sa([128, 2, TB], BF16, "hsq")
                mean_sb = sa([128, TB], F32, "mean_sb")
                var_sb = sa([128, TB], F32, "var_sb")
                hnb = sa([128, 2, TB], BF16, "hnb")
                ysA = sa([128, 4, TB], BF16, "ysA")
                uf = sa([128, 2, TB], F32, "uf")
                ub = sa([128, 2, TB], BF16, "ub")
                rot = [sa([128, 2, TB], F32, "rot%d" % i) for i in range(4)]
                rotc = [0, 0]
                w5 = [[sa([128, TB], F32, "w5_%d_%d" % (i, j)) for j in range(6)] for i in range(2)]
                hb = sa([128, 16, TB], BF16, "hb")
                ygf = sa([128, 2, TB], F32, "ygf")
                ygb = sa([128, 2, TB], BF16, "ygb")
                sgl = sa([128, 2, TB], F32, "sgl")
                tmpc = sa([128, 2, TB], F32, "tmpc")

                xbA = [xb, sa([128, 8, TB], BF16, "xbA2")]
                szAs = [szA, sa([128, 4, TB], BF16, "szA2")]
                ufs = [uf, sa([128, 2, TB], F32, "uf2")]
                ubs = [ub, sa([128, 2, TB], BF16, "ub2")]
                ysAs = [ysA, sa([128, 4, TB], BF16, "ysA2")]

                if l == 0 and os.environ.get('KDBG'):
                    print('SBUF remaining after sweep A allocs', nc.sbuf_bytes_remaining)

                def blockA(b):
                    xb = xbA[b % 2]
                    szA = szAs[b % 2]
                    uf = ufs[b % 2]
                    ub = ubs[b % 2]
                    ysA = ysAs[b % 2]
                    load_x(l, b, xb)
                    yield
                    c0 = b * TB
                    def proj(mt, wt=wA):
                        pt, pk = nps()
                        mm(pt, pk, [(wt[:, k, mt * 128:(mt + 1) * 128], xb[:, k, :], [wkey("wA", mt * 128), "xb%d" % k]) for k in range(8)])
                        return pt, pk
                    for c in range(2):
                        pz, kz = proj(4 + c)
                        s.act(REC("activation", out=szA[:, c, :], in_=pz, func=AF.Silu), r=list(kz), w=["szA%d" % c])
                        yield
                        pz2, kz2 = proj(8 + c)
                        s.act(REC("activation", out=szA[:, 2 + c, :], in_=pz2, func=AF.Silu), r=list(kz2), w=["szA%d" % (2 + c)])
                        yield
                        pu, ku = proj(6 + c)
                        if 'ufa' not in os.environ.get('KA_SKIP', ''):
                            s.act(REC("activation", out=uf[:, c, :], in_=pu, func=AF.Copy), r=list(ku), w=["uf%d" % c])
                        if 'ubd' not in os.environ.get('KA_SKIP', ''):
                            s.pool(REC("tensor_copy", out=ub[:, c, :], in_=uf[:, c, :]), r=["uf%d" % c], w=["ub%d" % c])
                        yield
                    yield ("wait", "conv")
                    for c in range(2):
                        pg, kg = proj(2 + c)
                        s.act(REC("activation", out=sg[c][:], in_=pg, func=AF.Sigmoid), r=list(kg), w=["sg%d" % c])
                        pv, kv = proj(c)
                        if 'glu' not in os.environ.get('KA_SKIP', ''):
                            s.dve(REC("tensor_tensor", out=gbuf[:, c, 32:32 + TB], in0=pv, in1=sg[c][:], op=ALU.mult), r=list(kv) + ["sg%d" % c], w=["gbuf%d" % c])
                        yield
                    if 'conv' not in os.environ.get('KA_SKIP', ''):
                        pcs = []
                        for c in range(2):
                            pc, kc = nps()
                            mm(pc, kc, [(dwt[:, c * 31 + j, :], gbuf[:, c, 2 + j:2 + j + TB], ["dwt", "gbuf%d" % c]) for j in range(31)])
                            pcs.append((pc, kc))
                            s.act(REC("activation", out=hcf[:, c, :], in_=pc, func=AF.Identity, bias=V(V_CONVB + c)), r=list(kc) + ["vec"], w=["hcf%d" % c])
                            s.act(REC("activation", out=hsq[:, c, :], in_=pc, func=AF.Square, bias=V(V_CONVB + c)), r=list(kc) + ["vec"], w=["hsq%d" % c])
                            s.pool(REC("tensor_copy", out=hcb[:, c, :], in_=hcf[:, c, :]), r=["hcf%d" % c], w=["hcb%d" % c])
                            s.pool(REC("tensor_copy", out=gbuf[:, c, 0:32], in_=gbuf[:, c, TB:TB + 32]), r=["gbuf%d" % c], w=["gbuf%d" % c])
                            yield
                        pm, km = nps()
                        mm(pm, km, [(ones_b[:], hcb[:, c, :], ["ones", "hcb%d" % c]) for c in range(2)])
                        pq, kq = nps()
                        mm(pq, kq, [(ones_b[:], hsq[:, c, :], ["ones", "hsq%d" % c]) for c in range(2)])
                        s.act(REC("activation", out=mean_sb[:], in_=pm, func=AF.Copy, scale=1.0 / 256), r=list(km), w=["mean_sb"])
                        s.dve(REC("tensor_tensor", out=var_sb[:], in0=mean_sb[:], in1=mean_sb[:], op=ALU.mult), r=["mean_sb"], w=["var_sb"])
                        s.dve(REC("scalar_tensor_tensor", out=var_sb[:], in0=pq, scalar=1.0 / 256, in1=var_sb[:], op0=ALU.mult, op1=ALU.subtract), r=list(kq) + ["var_sb"], w=["var_sb"])
                        rsqrt(var_sb[:], var_sb[:], 1.0, LN_EPS, ["var_sb"], ["var_sb"])
                        for c in range(2):
                            s.dve(REC("tensor_tensor", out=hcf[:, c, :], in0=hcf[:, c, :], in1=mean_sb[:], op=ALU.subtract), r=["hcf%d" % c, "mean_sb"], w=["hcf%d" % c])
                            s.dve(REC("tensor_tensor", out=hcf[:, c, :], in0=hcf[:, c, :], in1=var_sb[:], op=ALU.mult), r=["hcf%d" % c, "var_sb"], w=["hcf%d" % c])
                            s.act(REC("activation", out=hnb[:, c, :], in_=hcf[:, c, :], func=AF.Silu, scale=V(V_CONVG + c), bias=V(V_CONVBETA + c)), r=["hcf%d" % c, "vec"], w=["hnb%d" % c])
                        for m in range(2):
                            py, ky = nps()
                            mm(py, ky, [(wp2[:, k, m * 128:(m + 1) * 128], hnb[:, k, :], [wkey("wp2"), "hnb%d" % k]) for k in range(2)])
                            s.dve(REC("tensor_tensor", out=ysA[:, m, :], in0=py, in1=szA[:, m, :], op=ALU.mult), r=list(ky) + ["szA%d" % m], w=["ysA%d" % m])
                            yield
                    yield ("done", "conv")
                    yield ("wait", "cmat")
                    if 's5' not in os.environ.get('KA_SKIP', ''):
                        for p in range(8):
                            eng = s.dve if p % 2 == 0 else s.pool
                            ei = p % 2
                            W_ = w5[ei]
                            WK = ["w5_%d_%d" % (ei, j) for j in range(6)]
                            ri_ = ei * 2 + rotc[ei] % 2
                            rotc[ei] += 1
                            rk = "rot%d" % ri_
                            s.dma(REC("dma_start", out=rot[ri_][:, :, :], in_=rotd[:, p * 2 * TB:(p * 2 + 2) * TB].rearrange("p (c t) -> p c t", c=2)), r=["rotd%d" % p], w=[rk])
                            Cc = rot[ri_][:, 0, :]
                            Ss = rot[ri_][:, 1, :]
                            kt = p // 4
                            pr, kr = nps()
                            mm(pr, kr, [(BB[:, p, :], ub[:, kt, :], ["BB", "ub%d" % kt])])
                            pi_, ki = nps()
                            mm(pi_, ki, [(BB[:, 8 + p, :], ub[:, kt, :], ["BB", "ub%d" % kt])])
                            if ei == 1:
                                s.act(REC("activation", out=W_[4][:], in_=pr, func=AF.Copy), r=list(kr), w=[WK[4]])
                                s.act(REC("activation", out=W_[5][:], in_=pi_, func=AF.Copy), r=list(ki), w=[WK[5]])
                                pr, kr = W_[4][:], [WK[4]]
                                pi_, ki = W_[5][:], [WK[5]]
                            eng(REC("tensor_tensor", out=W_[0][:], in0=pr, in1=Cc, op=ALU.mult), r=list(kr) + [rk], w=[WK[0]])
                            eng(REC("tensor_tensor", out=W_[1][:], in0=pi_, in1=Ss, op=ALU.mult), r=list(ki) + [rk], w=[WK[1]])
                            eng(REC("tensor_tensor", out=W_[0][:], in0=W_[0][:], in1=W_[1][:], op=ALU.add), r=[WK[0], WK[1]], w=[WK[0]])
                            eng(REC("tensor_tensor", out=W_[2][:], in0=pi_, in1=Cc, op=ALU.mult), r=list(ki) + [rk], w=[WK[2]])
                            eng(REC("tensor_tensor", out=W_[3][:], in0=pr, in1=Ss, op=ALU.mult), r=list(kr) + [rk], w=[WK[3]])
                            eng(REC("tensor_tensor", out=W_[2][:], in0=W_[2][:], in1=W_[3][:], op=ALU.subtract), r=[WK[2], WK[3]], w=[WK[2]])
                            magb = sp_[:, MAG * 8 + p:MAG * 8 + p + 1].broadcast_to([128, TB])
                            s.dve(REC("tensor_tensor_scan", out=W_[4][:], data0=magb, data1=W_[0][:], initial=carry[:, p:p + 1], op0=ALU.mult, op1=ALU.add), r=[WK[0], "carryR%d" % p] + K5, w=[WK[4]])
                            s.dve(REC("tensor_tensor_scan", out=W_[5][:], data0=magb, data1=W_[2][:], initial=carry[:, 8 + p:9 + p], op0=ALU.mult, op1=ALU.add), r=[WK[2], "carryI%d" % p] + K5, w=[WK[5]])
                            yield
                            eng(REC("tensor_tensor", out=W_[0][:], in0=W_[4][:], in1=Cc, op=ALU.mult), r=[WK[4], rk], w=[WK[0]])
                            eng(REC("tensor_tensor", out=W_[1][:], in0=W_[5][:], in1=Ss, op=ALU.mult), r=[WK[5], rk], w=[WK[1]])
                            eng(REC("tensor_tensor", out=W_[0][:], in0=W_[0][:], in1=W_[1][:], op=ALU.subtract), r=[WK[0], WK[1]], w=[WK[0]])
                            eng(REC("tensor_tensor", out=W_[2][:], in0=W_[5][:], in1=Cc, op=ALU.mult), r=[WK[5], rk], w=[WK[2]])
                            eng(REC("tensor_tensor", out=W_[3][:], in0=W_[4][:], in1=Ss, op=ALU.mult), r=[WK[4], rk], w=[WK[3]])
                            eng(REC("tensor_tensor", out=W_[2][:], in0=W_[2][:], in1=W_[3][:], op=ALU.add), r=[WK[2], WK[3]], w=[WK[2]])
                            eng(REC("tensor_copy", out=carry[:, p:p + 1], in_=W_[0][:, TB - 1:TB]), r=[WK[0]], w=["carryR%d" % p])
                            eng(REC("tensor_copy", out=carry[:, 8 + p:9 + p], in_=W_[2][:, TB - 1:TB]), r=[WK[2]], w=["carryI%d" % p])
                            s.act(REC("activation", out=hb[:, p, :], in_=W_[0][:], func=AF.Copy), r=[WK[0]], w=["hb%d" % p])
                            s.act(REC("activation", out=hb[:, 8 + p, :], in_=W_[2][:], func=AF.Copy), r=[WK[2]], w=["hb%d" % (8 + p)])
                            yield
                        for m in range(2):
                            pyc, kyc = nps()
                            prs = []
                            for p in range(4 * m, 4 * m + 4):
                                prs.append((CB[:, p, :], hb[:, p, :], ["CB", "hb%d" % p]))
                                prs.append((CB[:, 8 + p, :], hb[:, 8 + p, :], ["CB", "hb%d" % (8 + p)]))
                            mm(pyc, kyc, prs)
                            s.dve(REC("scalar_tensor_tensor", out=ygf[:, m, :], in0=uf[:, m, :], scalar=V(V_SSMD + m), in1=pyc, op0=ALU.mult, op1=ALU.add), r=list(kyc) + ["uf%d" % m, "vec"], w=["ygf%d" % m])
                            yield
                        yield ("done", "cmat")
                        for m in range(2):
                            s.act(REC("activation", out=tmpc[:, m, :], in_=ygf[:, m, :], func=AF.Square), r=["ygf%d" % m], w=["tmpc%d" % m])
                            s.dve(REC("tensor_scalar", out=tmpc[:, m, :], in0=tmpc[:, m, :], scalar1=0.044715, scalar2=1.0, op0=ALU.mult, op1=ALU.add), r=["tmpc%d" % m], w=["tmpc%d" % m])
                            s.dve(REC("tensor_tensor", out=tmpc[:, m, :], in0=tmpc[:, m, :], in1=ygf[:, m, :], op=ALU.mult), r=["tmpc%d" % m, "ygf%d" % m], w=["tmpc%d" % m])
                            yield
                            s.act(REC("activation", out=sgl[:, m, :], in_=tmpc[:, m, :], func=AF.Sigmoid, scale=1.5957691216057308), r=["tmpc%d" % m], w=["sgl%d" % m])
                            s.pool(REC("tensor_tensor", out=ygb[:, m, :], in0=ygf[:, m, :], in1=sgl[:, m, :], op=ALU.mult), r=["ygf%d" % m, "sgl%d" % m], w=["ygb%d" % m])
                            yield
                        for m in range(2):
                            p2, k2 = nps()
                            mm(p2, k2, [(wgl[:, k, (2 + m) * 128:(3 + m) * 128], ygb[:, k, :], [wkey("wgl"), "ygb%d" % k]) for k in range(2)])
                            s.act(REC("activation", out=sgl[:, m, :], in_=p2, func=AF.Sigmoid), r=list(k2), w=["sgl%d" % m])
                            yield
                            p1, k1 = nps()
                            mm(p1, k1, [(wgl[:, k, m * 128:(m + 1) * 128], ygb[:, k, :], [wkey("wgl"), "ygb%d" % k]) for k in range(2)])
                            s.dve(REC("tensor_tensor", out=tmpc[:, m, :], in0=p1, in1=sgl[:, m, :], op=ALU.mult), r=list(k1) + ["sgl%d" % m], w=["tmpc%d" % m])
                            yield
                            s.pool(REC("tensor_tensor", out=ysA[:, 2 + m, :], in0=tmpc[:, m, :], in1=szA[:, 2 + m, :], op=ALU.mult), r=["tmpc%d" % m, "szA%d" % (2 + m)], w=["ysA%d" % (2 + m)])
                            yield
                    for m in range(4):
                        row = (m if m < 2 else 2 + m) * 128
                        s.dma(REC("dma_start", out=ysd[row:row + 128, c0:c0 + TB], in_=ysA[:, m, :]), r=["ysA%d" % m], w=["ysd_%d_%d" % (b, row // 128)])
                    yield

                run_pipe(blockA, NB, int(os.environ.get('KLAG_A', '4')), ("xb", "szA", "uf", "ub", "ysA"))

            s.barrier()
            scB = ExitStack()
            with scB:
              if 'B' in SW:
                def sbb(shape, dt=F32, name=None):
                    uid[0] += 1
                    return scB.enter_context(nc.sbuf_tensor("%s_%d" % (name or "b", uid[0]), list(shape), dt))
                wB = sbb([128, 8, 13 * 128], BF16, "wB")
                load_w(wB, w1b[l], D, "wB", bounds=[0, 384, 640, 1664], order=[0, 1])
                wq = sbb([128, 2, 1024], BF16, "wq")
                wkv = sbb([128, 1, 512], BF16, "wkv")
                load_w(wkv, wukv[l], 128, "wkv")
                load_w(wq, wuq[l], 256, "wq")
                load_w(wB, w1b[l], D, "wB", bounds=[0, 384, 640, 1664], order=[2])
                Kc = sbb([128, 4, S], BF16, "Kc")
                Vc = sbb([128, NT, 4, 128], BF16, "Vc")
                s.pool(REC("memset", Vc[:], 1.0), w=["Vc"] + ["Vc_%d" % t for t in range(NT)])
                Ksw = sbb([128, 128 + TB], BF16, "Ksw")
                s.pool(REC("memset", Ksw[:], 0.0), w=["Ksw"])
                Vsw = sbb([128, 5, 2, 128], BF16, "Vsw")
                s.pool(REC("memset", Vsw[:], 1.0), w=["Vsw"])
                esk = sbb([128, 4], F32, "esk")
                s.act(REC("activation", out=esk[:], in_=V(V_SINK, 4), func=AF.Exp), r=["vec"], w=["esk"])
                eskf = sbb([128, 4, 128], F32, "eskf")
                s.dve(REC("tensor_copy", out=eskf[:], in_=esk[:, :].unsqueeze(2).broadcast_to([128, 4, 128])), r=["esk"], w=["eskf"])
                szB = sbb([128, 4, TB], BF16, "szB")
                cqf = sbb([128, 3, TB], F32, "cqf")
                cqs = sbb([128, 3, TB], BF16, "cqs")
                cqn = sbb([128, 3, TB], BF16, "cqn")
                rstd = [sbb([128, TB], F32, "rstd%d" % i) for i in range(2)]
                rc = sbb([128, TB], F32, "rc")
                rs_ = sbb([128, TB], F32, "rs")
                t1 = sbb([128, TB], F32, "t1")
                t2 = sbb([128, TB], F32, "t2")
                Qh = sbb([128, 4, TB], BF16, "Qh")
                qsw = sbb([128, 2, TB], BF16, "qsw")
                pbuf = [sbb([128, TB], BF16, "pbuf%d" % i) for i in range(3)]
                pbs = [sbb([128, 1024], BF16, "pbs%d" % i) for i in range(2)]
                rden = sbb([128, TB], F32, "rden")
                tmpo = sbb([128, TB], F32, "tmpo")
                ysB = sbb([128, 4, TB], BF16, "ysB")
                pcount = [0]

                for b in range(NB):
                    load_x(l, b)
                    c0 = b * TB
                    s.dma(REC("dma_start", out=rc[:], in_=ropec[:, c0:c0 + TB]), w=["rc"])
                    s.dma(REC("dma_start", out=rs_[:], in_=ropes[:, c0:c0 + TB]), w=["rs"])

                    def projB(mt):
                        pt, pk = nps()
                        mm(pt, pk, [(wB[:, k, mt * 128:(mt + 1) * 128], xb[:, k, :], [wkey("wB", mt * 128), "xb%d" % k]) for k in range(8)])
                        return pt, pk
                    for c in range(3):
                        pc, kc = projB(c)
                        s.act(REC("activation", out=cqf[:, c, :], in_=pc, func=AF.Copy), r=list(kc), w=["cqf%d" % c])
                        s.act(REC("activation", out=cqs[:, c, :], in_=pc, func=AF.Square), r=list(kc), w=["cqs%d" % c])
                    pm, km = nps()
                    mm(pm, km, [(ones_b[:], cqs[:, c, :], ["ones", "cqs%d" % c]) for c in range(2)])
                    rsqrt(rstd[0][:], pm, 1.0 / 256, RMS_EPS, list(km), ["rstd0"])
                    pm2, km2 = nps()
                    mm(pm2, km2, [(ones_b[:], cqs[:, 2, :], ["ones", "cqs2"])])
                    rsqrt(rstd[1][:], pm2, 1.0 / 128, RMS_EPS, list(km2), ["rstd1"])
                    for c in range(3):
                        gcol = V_QNG + c if c < 2 else V_KVNG
                        ri = 0 if c < 2 else 1
                        s.dve(REC("scalar_tensor_tensor", out=cqn[:, c, :], in0=cqf[:, c, :], scalar=V(gcol), in1=rstd[ri][:], op0=ALU.mult, op1=ALU.mult),
                              r=["cqf%d" % c, "rstd%d" % ri, "vec"], w=["cqn%d" % c])
                    pa, ka = projB(3)
                    pb_, kb = projB(4)
                    R = slice(64, 96)
                    s.dve(REC("tensor_tensor", out=t1[R, :], in0=pa[R, :], in1=rc[R, :], op=ALU.mult), r=list(ka) + ["rc"], w=["t1"])
                    s.dve(REC("tensor_tensor", out=t2[R, :], in0=pb_[R, :], in1=rs_[R, :], op=ALU.mult), r=list(kb) + ["rs"], w=["t2"])
                    for h in range(4):
                        f = s.dve if h % 2 == 0 else s.pool
                        f(REC("tensor_tensor", out=Kc[R, h, c0:c0 + TB], in0=t1[R, :], in1=t2[R, :], op=ALU.add), r=["t1", "t2"], w=["Kc_%d_%d" % (h, b)])
                    for h in range(4):
                        pk_, kk = nps()
                        mm(pk_, kk, [(wkv[:, 0, h * 64:(h + 1) * 64], cqn[:, 2, :], [wkey("wkv"), "cqn2"])])
                        s.act(REC("activation", out=Kc[0:64, h, c0:c0 + TB], in_=pk_[0:64, :], func=AF.Copy), r=list(kk), w=["Kc_%d_%d" % (h, b)])
                    for j in range(4):
                        pv, kv = nps()
                        mm(pv[:, 0:256], kv, [(cqn[:, 2, j * 128:(j + 1) * 128], wkv[:, 0, 256:512], [wkey("wkv"), "cqn2"])])
                        s.act(REC("activation", out=Vc[:, b * 4 + j, :, 0:64], in_=pv[:, 0:256].rearrange("p (h d) -> p h d", d=64), func=AF.Copy), r=list(kv) + ["Vc"], w=["Vc_%d" % (b * 4 + j)])
                    for h in range(4):
                        pA, kA = nps()
                        mm(pA, kA, [(wq[:, k, (2 * h) * 128:(2 * h) * 128 + 96], cqn[:, k, :], [wkey("wq"), "cqn%d" % k]) for k in range(2)])
                        pB, kB = nps()
                        mm(pB, kB, [(wq[:, k, (2 * h + 1) * 128:(2 * h + 1) * 128 + 96], cqn[:, k, :], [wkey("wq"), "cqn%d" % k]) for k in range(2)])
                        s.act(REC("activation", out=Qh[0:64, h, :], in_=pA[0:64, :], func=AF.Copy), r=list(kA), w=["Qh%d" % h])
                        s.dve(REC("tensor_tensor", out=t1[R, :], in0=pA[R, :], in1=rc[R, :], op=ALU.mult), r=list(kA) + ["rc"], w=["t1"])
                        s.dve(REC("tensor_tensor", out=t2[R, :], in0=pB[R, :], in1=rs_[R, :], op=ALU.mult), r=list(kB) + ["rs"], w=["t2"])
                        s.dve(REC("tensor_tensor", out=Qh[R, h, :], in0=t1[R, :], in1=t2[R, :], op=ALU.add), r=["t1", "t2", "Qh%d" % h], w=["Qh%d" % h])
                    for c in range(2):
                        pz, kz = projB(5 + c)
                        s.act(REC("activation", out=szB[:, c, :], in_=pz, func=AF.Silu), r=list(kz), w=["szB%d" % c])
                        pz2, kz2 = projB(11 + c)
                        s.act(REC("activation", out=szB[:, 2 + c, :], in_=pz2, func=AF.Silu), r=list(kz2), w=["szB%d" % (2 + c)])
                    scale = (64 + 32) ** -0.5
                    nkt = 4 * b + 4
                    SK = 2
                    accs = {}

                    def emit_S(h, kt):
                        if kt == 0:
                            accs[h] = nacc()
                        jl = kt - 4 * b
                        qs = max(0, jl) * 128
                        ps_, ks = nps()
                        kb_ = kt // 4
                        s.pe(REC("matmul", ps_[:, qs:TB], lhsT=Kc[0:96, h, kt * 128:(kt + 1) * 128], rhs=Qh[0:96, h, qs:TB], start=True, stop=True),
                             r=["Kc_%d_%d" % (h, kb_), "Qh%d" % h], w=list(ks))
                        pi = pcount[0] % 3
                        pcount[0] += 1
                        s.act(REC("activation", out=pbuf[pi][:, qs:TB], in_=ps_[:, qs:TB], func=AF.Exp, scale=float(scale)), r=list(ks), w=["pbuf%d" % pi])
                        if jl >= 0:
                            s.pool(REC("tensor_tensor", out=pbuf[pi][:, qs:qs + 128], in0=pbuf[pi][:, qs:qs + 128], in1=tri_b, op=ALU.mult), r=["pbuf%d" % pi, "cmb"], w=["pbuf%d" % pi])
                        return pi, qs

                    def emit_PV(h, kt, pi, qs):
                        po, ko = accs[h]
                        s.pe(REC("matmul", po[:, qs:TB], lhsT=Vc[:, kt, h, :], rhs=pbuf[pi][:, qs:TB], start=(kt == 0), stop=(kt == nkt - 1)),
                             r=["Vc_%d" % kt, "pbuf%d" % pi], w=list(ko))
                        if kt == nkt - 1:
                            hp = slice((h % 2) * 64, (h % 2) * 64 + 64)
                            s.dve(REC("reciprocal", out=rden[64:128, :], in_=po[64:128, :]), r=list(ko), w=["rden"])
                            s.dve(REC("tensor_tensor", out=tmpo[hp, :], in0=po[0:64, :], in1=rden[64:128, :], op=ALU.mult), r=list(ko) + ["rden"], w=["tmpo"])
                            s.pool(REC("tensor_tensor", out=ysB[hp, h // 2, :], in0=tmpo[hp, :], in1=szB[hp, h // 2, :], op=ALU.mult), r=["tmpo", "szB%d" % (h // 2)], w=["ysB%d" % (h // 2)])

                    fifo = []
                    for h in range(4):
                        for kt in range(nkt):
                            fifo.append((h, kt) + emit_S(h, kt))
                            if len(fifo) > SK:
                                emit_PV(*fifo.pop(0))
                    while fifo:
                        emit_PV(*fifo.pop(0))
                    for c in range(2):
                        pq_, kq_ = projB(7 + c)
                        s.act(REC("activation", out=qsw[:, c, :], in_=pq_, func=AF.Copy), r=list(kq_), w=["qsw%d" % c])
                    pk2, kk2 = projB(9)
                    s.act(REC("activation", out=Ksw[:, 128:128 + TB], in_=pk2, func=AF.Copy), r=list(kk2), w=["Ksw"])
                    for j in range(4):
                        pv, kv = nps()
                        mm(pv[:, 0:128], kv, [(xb[:, k, j * 128:(j + 1) * 128], wB[:, k, 10 * 128:11 * 128], [wkey("wB", 10 * 128), "xb%d" % k]) for k in range(8)])
                        s.act(REC("activation", out=Vsw[:, 1 + j, :, 0:64], in_=pv[:, 0:128].rearrange("p (g d) -> p g d", d=64), func=AF.Copy), r=list(kv), w=["Vsw"])
                    sscale = 64 ** -0.5
                    for h in range(4):
                        g = h // 2
                        qt_ = h % 2
                        gp = slice(g * 64, g * 64 + 64)
                        p2b, k2b = nps(2)
                        for i in range(4):
                            for wch in range(2):
                                col = (i * 2 + wch) * 128
                                kcol = (i + wch) * 128
                                s.pe(REC("matmul", p2b[:, col:col + 128], lhsT=Ksw[gp, kcol:kcol + 128], rhs=qsw[gp, qt_, i * 128:(i + 1) * 128], start=True, stop=True),
                                     r=["Ksw", "qsw%d" % qt_], w=list(k2b))
                        pi = h % 2
                        for hh in range(2):
                            s.act(REC("activation", out=pbs[pi][:, hh * 512:(hh + 1) * 512], in_=p2b[:, hh * 512:(hh + 1) * 512], func=AF.Exp, scale=float(sscale)), r=list(k2b), w=["pbs%d" % pi])
                        s.pool(REC("tensor_tensor", out=pbs[pi][:, :].rearrange("p (i m) -> p i m", m=256), in0=pbs[pi][:, :].rearrange("p (i m) -> p i m", m=256),
                                                            in1=swm_b.unsqueeze(1).broadcast_to([128, 4, 256]), op=ALU.mult), r=["pbs%d" % pi, "cmb"], w=["pbs%d" % pi])
                        po, ko = nacc()
                        for i in range(4):
                            first = True
                            for wch in range(2):
                                if b == 0 and i == 0 and wch == 0:
                                    continue
                                col = (i * 2 + wch) * 128
                                s.pe(REC("matmul", po[:, i * 128:(i + 1) * 128], lhsT=Vsw[:, i + wch, g, :], rhs=pbs[pi][:, col:col + 128], start=first, stop=(wch == 1)),
                                     r=["Vsw", "pbs%d" % pi], w=list(ko))
                                first = False
                        hp = slice((h % 2) * 64, (h % 2) * 64 + 64)
                        s.dve(REC("tensor_tensor", out=rden[64:128, :].rearrange("p (i m) -> p i m", m=128), in0=po[64:128, :].rearrange("p (i m) -> p i m", m=128),
                                                                in1=eskf[64:128, h:h + 1, :].broadcast_to([64, 4, 128]), op=ALU.add), r=list(ko) + ["eskf"], w=["rden"])
                        s.dve(REC("reciprocal", out=rden[64:128, :], in_=rden[64:128, :]), r=["rden"], w=["rden"])
                        s.dve(REC("tensor_tensor", out=tmpo[hp, :], in0=po[0:64, :], in1=rden[64:128, :], op=ALU.mult), r=list(ko) + ["rden"], w=["tmpo"])
                        s.pool(REC("tensor_tensor", out=ysB[hp, 2 + h // 2, :], in0=tmpo[hp, :], in1=szB[hp, 2 + h // 2, :], op=ALU.mult), r=["tmpo", "szB%d" % (2 + h // 2)], w=["ysB%d" % (2 + h // 2)])
                    s.pool(REC("tensor_copy", out=Ksw[:, 0:128], in_=Ksw[:, TB:TB + 128]), r=["Ksw"], w=["Ksw"])
                    s.pool(REC("tensor_copy", out=Vsw[:, 0, :, 0:64], in_=Vsw[:, 4, :, 0:64]), r=["Vsw"], w=["Vsw"])
                    for m in range(4):
                        row = (2 + m if m < 2 else 4 + m) * 128
                        s.dma(REC("dma_start", out=ysd[row:row + 128, c0:c0 + TB], in_=ysB[:, m, :]), r=["ysB%d" % m], w=["ysd_%d_%d" % (b, row // 128)])

            s.barrier()
            scC = ExitStack()
            with scC:
              if 'C1' in SW:
                def sc(shape, dt=F32, name=None):
                    uid[0] += 1
                    return scC.enter_context(nc.sbuf_tensor("%s_%d" % (name or "c", uid[0]), list(shape), dt))
                wm = sc([128, 8, 4 * D], BF16, "wm")
                WMB = [0, 512, 1024, 2048, 3072, 4096]
                load_w(wm, wmerge[l], D, "wm", bounds=WMB, order=[0])
                wbr = sc([128, 8, D], BF16, "wbr")
                load_w(wbr, wbranch[l], 1024, "wbr", by_k=True)
                load_w(wm, wmerge[l], D, "wm", bounds=WMB, order=[1, 2, 3, 4])
                ysl = sc([128, 8, TB], BF16, "ysl")
                gs = [sc([128, TB], F32, "gs%d" % i) for i in range(2)]
                tp = [sc([128, TB], F32, "tp%d" % i) for i in range(2)]
                mg = sc([128, 8, TB], F32, "mg")
                mgb = sc([128, 8, TB], BF16, "mgb")
                cnt = [0]
                for b in range(NB):
                    load_x(l, b)
                    c0 = b * TB
                    for t in range(8):
                        s.dma(REC("dma_start", out=ysl[:, t, :], in_=ysd[t * 128:(t + 1) * 128, c0:c0 + TB]), r=["ysd_%d_%d" % (b, t)], w=["ysl%d" % t])
                    for n in range(4):
                        for m in range(8):
                            i = cnt[0] % 2
                            cnt[0] += 1
                            pg, kg = nps()
                            mm(pg, kg, [(wm[:, k, n * D + m * 128:n * D + (m + 1) * 128], xb[:, k, :], [wkey("wm", n * D + m * 128), "xb%d" % k]) for k in range(8)])
                            s.act(REC("activation", out=gs[i][:], in_=pg, func=AF.Sigmoid, bias=V(V_BMERGE + n * 8 + m)), r=list(kg) + ["vec"], w=["gs%d" % i])
                            pb2, kb2 = nps()
                            mm(pb2, kb2, [(wbr[:, 2 * n + k, m * 128:(m + 1) * 128], ysl[:, 2 * n + k, :], [wkey("wbr", kt=2 * n + k), "ysl%d" % (2 * n + k)]) for k in range(2)])
                            if n == 0:
                                s.dve(REC("tensor_tensor", out=mg[:, m, :], in0=pb2, in1=gs[i][:], op=ALU.mult), r=list(kb2) + ["gs%d" % i], w=["mg%d" % m])
                            else:
                                s.dve(REC("tensor_tensor", out=tp[i][:], in0=pb2, in1=gs[i][:], op=ALU.mult), r=list(kb2) + ["gs%d" % i], w=["tp%d" % i])
                                if n < 3:
                                    s.pool(REC("tensor_tensor", out=mg[:, m, :], in0=mg[:, m, :], in1=tp[i][:], op=ALU.add), r=["mg%d" % m, "tp%d" % i], w=["mg%d" % m])
                                else:
                                    s.pool(REC("tensor_tensor", out=mgb[:, m, :], in0=mg[:, m, :], in1=tp[i][:], op=ALU.add), r=["mg%d" % m, "tp%d" % i], w=["mgb%d" % m])
                                    s.dma(REC("dma_start", out=mgd[m * 128:(m + 1) * 128, c0:c0 + TB], in_=mgb[:, m, :]), r=["mgb%d" % m], w=["mgd_%d_%d" % (b, m)])

            s.barrier()
            scD = ExitStack()
            with scD:
              if 'C2' in SW:
                def sd(shape, dt=F32, name=None):
                    uid[0] += 1
                    return scD.enter_context(nc.sbuf_tensor("%s_%d" % (name or "d", uid[0]), list(shape), dt))
                wo = sd([128, 8, D], BF16, "wo")
                load_w(wo, wout[l], D, "wo", bounds=[0, 256, 1024])
                wpg = sd([128, 8, D], BF16, "wpg")
                wpl = sd([128, 2, D], BF16, "wpl")
                load_w(wpl, wple[l], 256, "wpl")
                load_w(wpg, wpleg[l], D, "wpg")
                mgl = sd([128, 8, TB], BF16, "mgl")
                pb16 = sd([128, 2, TB], BF16, "pb16")
                zb = [sd([128, TB], BF16, "zb%d" % i) for i in range(2)]
                zq = [sd([128, TB], BF16, "zq%d" % i) for i in range(2)]
                mean2 = sd([128, TB], F32, "mean2")
                var2 = sd([128, TB], F32, "var2")
                xlb = sd([128, 8, TB], BF16, "xlb")
                ef = sd([128, 8, TB], F32, "ef")
                sge = [sd([128, TB], F32, "sge%d" % i) for i in range(2)]
                rse = sd([128, TB], F32, "rse")
                xo = [sd([128, TB], F32, "xo%d" % i) for i in range(2)]
                alpha = (2.0 * 4) ** 0.25
                dst = yT if l == L - 1 else xs
                xfs = [xf, sd([128, 8, TB], F32, "xfD2")]
                mgls = [mgl, sd([128, 8, TB], BF16, "mgl2")]
                pb16s = [pb16, sd([128, 2, TB], BF16, "pb16b")]
                xlbs = [xlb, sd([128, 8, TB], BF16, "xlb2")]
                efs = [ef, sd([128, 8, TB], F32, "ef2")]
                mean2s = [mean2, sd([128, TB], F32, "mean2b")]
                var2s = [var2, sd([128, TB], F32, "var2b")]
                rses = [rse, sd([128, TB], F32, "rseb")]

                def blockD(b):
                    xf = xfs[b % 2]
                    mgl = mgls[b % 2]
                    pb16 = pb16s[b % 2]
                    xlb = xlbs[b % 2]
                    ef = efs[b % 2]
                    mean2 = mean2s[b % 2]
                    var2 = var2s[b % 2]
                    rse = rses[b % 2]
                    load_x(l, b, xf=xf, cast=False)
                    c0 = b * TB
                    for t in range(8):
                        s.dma(REC("dma_start", out=mgl[:, t, :], in_=mgd[t * 128:(t + 1) * 128, c0:c0 + TB]), r=["mgd_%d_%d" % (b, t)], w=["mgl%d" % t])
                    for t in range(2):
                        s.dma(REC("dma_start", out=pb16[:, t, :], in_=pT[l, t * 128:(t + 1) * 128, c0:c0 + TB]), w=["pb16_%d" % t], q="pool")
                    yield
                    pmean, kmean = nacc()
                    pmsq, kmsq = nacc()
                    for m in range(8):
                        pz, kz = nps()
                        mm(pz, kz, [(wo[:, k, m * 128:(m + 1) * 128], mgl[:, k, :], [wkey("wo", m * 128), "mgl%d" % k]) for k in range(8)])
                        s.dve(REC("scalar_tensor_tensor", out=xf[:, m, :], in0=xf[:, m, :], scalar=float(alpha), in1=pz, op0=ALU.mult, op1=ALU.add), r=list(kz) + ["xf%d" % m], w=["xf%d" % m])
                        i = m % 2
                        s.act(REC("activation", out=zb[i][:], in_=xf[:, m, :], func=AF.Copy), r=["xf%d" % m], w=["zb%d" % i])
                        s.act(REC("activation", out=zq[i][:], in_=xf[:, m, :], func=AF.Square), r=["xf%d" % m], w=["zq%d" % i])
                        s.pe(REC("matmul", pmean, lhsT=ones_b[:], rhs=zb[i][:], start=(m == 0), stop=(m == 7)), r=["ones", "zb%d" % i], w=list(kmean))
                        s.pe(REC("matmul", pmsq, lhsT=ones_b[:], rhs=zq[i][:], start=(m == 0), stop=(m == 7)), r=["ones", "zq%d" % i], w=list(kmsq))
                        yield
                    s.act(REC("activation", out=mean2[:], in_=pmean, func=AF.Copy, scale=1.0 / D), r=list(kmean), w=["mean2"])
                    s.dve(REC("tensor_tensor", out=var2[:], in0=mean2[:], in1=mean2[:], op=ALU.mult), r=["mean2"], w=["var2"])
                    s.dve(REC("scalar_tensor_tensor", out=var2[:], in0=pmsq, scalar=1.0 / D, in1=var2[:], op0=ALU.mult, op1=ALU.subtract), r=list(kmsq) + ["var2"], w=["var2"])
                    rsqrt(var2[:], var2[:], 1.0, LN_EPS, ["var2"], ["var2"])
                    yield
                    for m in range(8):
                        f = s.dve if m % 2 == 0 else s.pool
                        f(REC("tensor_tensor", out=xf[:, m, :], in0=xf[:, m, :], in1=mean2[:], op=ALU.subtract), r=["xf%d" % m, "mean2"], w=["xf%d" % m])
                        f(REC("tensor_tensor", out=xf[:, m, :], in0=xf[:, m, :], in1=var2[:], op=ALU.mult), r=["xf%d" % m, "var2"], w=["xf%d" % m])
                        s.act(REC("activation", out=xf[:, m, :], in_=xf[:, m, :], func=AF.Identity, scale=V(V_LNG + m), bias=V(V_LNB + m)), r=["xf%d" % m, "vec"], w=["xf%d" % m])
                        s.act(REC("activation", out=xlb[:, m, :], in_=xf[:, m, :], func=AF.Copy), r=["xf%d" % m], w=["xlb%d" % m])
                        yield
                    pms, kms = nacc()
                    for m in range(8):
                        i = m % 2
                        pgt, kgt = nps()
                        mm(pgt, kgt, [(wpg[:, k, m * 128:(m + 1) * 128], xlb[:, k, :], [wkey("wpg"), "xlb%d" % k]) for k in range(8)])
                        s.act(REC("activation", out=sge[i][:], in_=pgt, func=AF.Sigmoid), r=list(kgt), w=["sge%d" % i])
                        ppe, kpe = nps()
                        mm(ppe, kpe, [(wpl[:, k, m * 128:(m + 1) * 128], pb16[:, k, :], [wkey("wpl"), "pb16_%d" % k]) for k in range(2)])
                        s.dve(REC("tensor_tensor", out=ef[:, m, :], in0=ppe, in1=sge[i][:], op=ALU.mult), r=list(kpe) + ["sge%d" % i], w=["ef%d" % m])
                        s.act(REC("activation", out=zq[i][:], in_=ef[:, m, :], func=AF.Square), r=["ef%d" % m], w=["zq%d" % i])
                        s.pe(REC("matmul", pms, lhsT=ones_b[:], rhs=zq[i][:], start=(m == 0), stop=(m == 7)), r=["ones", "zq%d" % i], w=list(kms))
                        yield
                    rsqrt(rse[:], pms, 1.0 / D, RMS_EPS, list(kms), ["rse"])
                    yield
                    for m in range(8):
                        i = m % 2
                        f = s.dve if m % 2 == 0 else s.pool
                        s.dve(REC("scalar_tensor_tensor", out=ef[:, m, :], in0=ef[:, m, :], scalar=V(V_PLEG + m), in1=rse[:], op0=ALU.mult, op1=ALU.mult), r=["ef%d" % m, "rse", "vec"], w=["ef%d" % m])
                        f(REC("tensor_tensor", out=xo[i][:], in0=ef[:, m, :], in1=xf[:, m, :], op=ALU.add), r=["ef%d" % m, "xf%d" % m], w=["xo%d" % i])
                        s.dma(REC("dma_start", out=dst[m * 128:(m + 1) * 128, c0:c0 + TB], in_=xo[i][:]), r=["xo%d" % i], w=["xs_%d" % b], final=(l == L - 1))
                        yield
                run_pipe(blockD, NB, int(os.environ.get('KLAG_D', '11')), ("xf", "mgl", "pb", "xlb", "ef", "mean", "var", "rse"))
            s.barrier()
        s.emit(st)
    return nc


def pack_weights(inp, L, S):
    f = np.float32
    W = {}
    w_in = np.asarray(inp["w_in"], f)
    offs = np.cumsum([0, 256, 256, 256, 256, 128, 32, 256, 256, 256, 256, 128, 128, 256])
    seg = {n: (offs[i], offs[i + 1]) for i, n in enumerate(
        ["a_val", "a_gate", "a_z", "c_q", "c_kv", "k_r", "b_z", "u", "c_z", "q", "k", "v", "d_z"])}

    def cols(n):
        a, b = seg[n]
        return w_in[:, :, a:b]
    w1a = np.concatenate([cols("a_val"), cols("a_gate"), cols("a_z"), cols("u"), cols("c_z")], axis=2)
    z = lambda *sh: np.zeros(sh, f)
    kr = cols("k_r")
    krA = z(L, D, 128); krA[:, :, 64:96] = kr
    krB = z(L, D, 128); krB[:, :, 64:80] = kr[:, :, 16:32]; krB[:, :, 80:96] = kr[:, :, 0:16]
    q = cols("q")
    q02 = np.concatenate([q[:, :, 0:64], q[:, :, 128:192]], axis=2)
    q13 = np.concatenate([q[:, :, 64:128], q[:, :, 192:256]], axis=2)
    w1b = np.concatenate([cols("c_q"), cols("c_kv"), krA, krB, cols("b_z"), q02, q13, cols("k"), cols("v"), cols("d_z")], axis=2)
    W["w1a"] = np.ascontiguousarray(w1a)
    W["w1b"] = np.ascontiguousarray(w1b)
    wuq_ = np.asarray(inp["w_uq"], f)
    wq = z(L, 256, 8 * 128)
    for h in range(4):
        wq[:, :, (2 * h) * 128:(2 * h) * 128 + 96] = wuq_[:, :, h * 96:(h + 1) * 96]
        wq[:, :, (2 * h + 1) * 128 + 64:(2 * h + 1) * 128 + 80] = wuq_[:, :, h * 96 + 80:h * 96 + 96]
        wq[:, :, (2 * h + 1) * 128 + 80:(2 * h + 1) * 128 + 96] = wuq_[:, :, h * 96 + 64:h * 96 + 80]
    W["wuq"] = wq
    wukv_ = np.asarray(inp["w_ukv"], f).reshape(L, 128, 4, 128)
    W["wukv"] = np.ascontiguousarray(np.concatenate([wukv_[:, :, :, 0:64].reshape(L, 128, 256), wukv_[:, :, :, 64:128].reshape(L, 128, 256)], axis=2))
    W["wpw2"] = np.asarray(inp["w_pw2"], f)
    W["wglu"] = np.asarray(inp["w_glu"], f)
    vec = z(L, 128, NV)
    cw = np.asarray(inp["conv_w"], f)
    for c in range(2):
        vec[:, :, V_CONVW + c * 31:V_CONVW + (c + 1) * 31] = cw[:, :, c * 128:(c + 1) * 128].transpose(0, 2, 1)

    def pv(name, col, nt):
        a = np.asarray(inp[name], f).reshape(L, nt, 128)
        vec[:, :, col:col + nt] = a.transpose(0, 2, 1)
    pv("conv_b", V_CONVB, 2); pv("conv_norm_g", V_CONVG, 2); pv("conv_norm_b", V_CONVBETA, 2)
    pv("mla_q_norm_g", V_QNG, 2); pv("mla_kv_norm_g", V_KVNG, 1); pv("ssm_d", V_SSMD, 2)
    pv("b_merge", V_BMERGE, 32); pv("ln_g", V_LNG, 8); pv("ln_b", V_LNB, 8); pv("ple_norm_g", V_PLEG, 8)

    def sm(a):
        return a.reshape(L, 8, 2, 64).transpose(0, 2, 3, 1).reshape(L, 128, 8)
    vec[:, :, V_LR:V_LR + 8] = sm(np.asarray(inp["ssm_a_re"], f))
    vec[:, :, V_LI:V_LI + 8] = sm(np.asarray(inp["ssm_a_im"], f))
    vec[:, :, V_LOGDT:V_LOGDT + 8] = sm(np.repeat(np.asarray(inp["ssm_log_dt"], f)[:, :, None], 64, axis=2))
    vec[:, :, V_SINK:V_SINK + 4] = np.asarray(inp["attn_sinks"], f)[:, None, :]
    W["vecs"] = vec

    def smB(a):
        return a.reshape(L, 8, 2, 64, 16).transpose(0, 2, 3, 1, 4).reshape(L, 128, 128)

    def smC(a):
        return a.reshape(L, 8, 2, 16, 64).transpose(0, 2, 4, 1, 3).reshape(L, 128, 128)
    W["s5b"] = np.ascontiguousarray(np.concatenate([smB(np.asarray(inp["ssm_b_re"], f)), smB(np.asarray(inp["ssm_b_im"], f))], axis=2))
    W["s5c"] = np.ascontiguousarray(np.concatenate([smC(np.asarray(inp["ssm_c_re"], f)), smC(np.asarray(inp["ssm_c_im"], f))], axis=2))
    W["wmerge"] = np.asarray(inp["w_merge"], f)
    W["wbranch"] = np.asarray(inp["w_branch"], f).reshape(L, 1024, D)
    W["wout"] = np.asarray(inp["w_out"], f)
    W["wple"] = np.asarray(inp["w_ple"], f)
    W["wpleg"] = np.asarray(inp["w_ple_gate"], f)
    pos = np.arange(S, dtype=np.float64)
    inv = 10000.0 ** (-np.arange(0, 32, 2, dtype=np.float64) / 32)
    rc = np.zeros((128, S), f); rs = np.zeros((128, S), f)
    for p in range(128):
        i = p % 32
        ang = pos * inv[i % 16]
        ang = (pos.astype(f) * inv.astype(f)[i % 16]).astype(np.float64)
        rc[p] = np.cos(ang)
        rs[p] = (-1.0 if i < 16 else 1.0) * np.sin(ang)
    W["ropec"] = rc; W["ropes"] = rs
    cmv = np.zeros((128, NCM), f)
    cmv[:, C_IDENT:C_IDENT + 128] = np.eye(128, dtype=f)
    k = np.arange(128)[:, None]; qq = np.arange(128)[None, :]
    cmv[:, C_TRI:C_TRI + 128] = (k <= qq)
    cmv[:, C_SWM:C_SWM + 128] = (k > qq)
    cmv[:, C_SWM + 128:C_SWM + 256] = (k <= qq)
    cmv[:, C_IOTA:C_IOTA + 512] = np.arange(1, 513, dtype=f)[None, :]
    W["cmat"] = cmv
    return W


_NC_CACHE = {}


def run(inputs, S, L, ncores, debug=False):
    x = np.asarray(inputs["x"], np.float32)
    p = np.asarray(inputs["p"], np.float32)
    W = pack_weights(inputs, L, S)
    key = (S, L, debug)
    if key not in _NC_CACHE:
        _NC_CACHE[key] = build(S, L, debug)
    nc = _NC_CACHE[key]
    in_maps = []
    for c in range(ncores):
        m = dict(W)
        m["xT"] = np.ascontiguousarray(x[c].T)
        m["pT"] = np.ascontiguousarray(p[:, c].transpose(0, 2, 1))
        in_maps.append(m)
    res = run_bass_kernel_spmd(nc, in_maps, core_ids=list(range(ncores)))
    out = np.stack([np.ascontiguousarray(r["yT"].T) for r in res.results], axis=0)
    if debug:
        return out.astype(np.float32), [dict(r) for r in res.results]
    return out.astype(np.float32)


def kernel(**inputs):
    return run(inputs, 4096, 4, 8)
```

```python
import math
import os
from contextlib import ExitStack
import numpy as np
import concourse.bass as bass
import concourse.mybir as mybir
from concourse.bass_utils import run_bass_kernel_spmd

F32 = mybir.dt.float32
BF16 = mybir.dt.bfloat16
I32 = mybir.dt.int32
ALU = mybir.AluOpType
AF = mybir.ActivationFunctionType

ENGS = ("pe", "dve", "act", "pool", "sp")
N_DMA_SEMS = 24

D = 1024
TB = 512
LN_EPS = 1e-5
RMS_EPS = 1e-6
TWO_PI = 2.0 * math.pi


def REC(name, *args, **kwargs):
    def fn(e):
        return getattr(e, name)(*args, **kwargs)
    return fn


class Sched:
    def __init__(self, nc):
        self.nc = nc
        self.ops = []
        self.last_w = {}
        self.readers = {}

    par = 0
    dbl = ()

    def _k(self, t):
        if self.dbl and isinstance(t, str) and t.rstrip("0123456789_") in self.dbl:
            return t + "#%d" % self.par
        return t

    def op(self, eng, fn, reads=(), writes=(), dma=False, final=False):
        reads = [self._k(t) for t in reads]
        writes = [self._k(t) for t in writes]
        idx = len(self.ops)
        deps = set()
        for t in reads:
            w = self.last_w.get(t)
            if w is not None:
                deps.add((w, "raw"))
        for t in writes:
            w = self.last_w.get(t)
            if w is not None:
                deps.add((w, "waw"))
            for r in self.readers.get(t, ()):
                deps.add((r, "war"))
        for t in writes:
            self.last_w[t] = idx
            self.readers[t] = []
        for t in reads:
            self.readers.setdefault(t, []).append(idx)
        self.ops.append(dict(eng=eng, fn=fn, deps=deps, dma=dma, final=final))
        return idx

    def pe(self, fn, r=(), w=()):
        return self.op("pe", fn, r, w)

    def dve(self, fn, r=(), w=()):
        return self.op("dve", fn, r, w)

    def act(self, fn, r=(), w=()):
        return self.op("act", fn, r, w)

    def pool(self, fn, r=(), w=()):
        return self.op("pool", fn, r, w)

    def dma(self, fn, r=(), w=(), q="sp", final=False):
        return self.op(q, fn, r, w, dma=True, final=final)

    def barrier(self):
        self.ops.append(dict(eng=None, fn=None, deps=set(), dma=False, final=False, barrier=True))
        self.last_w.clear()
        self.readers.clear()

    def emit(self, stack):
        nc = self.nc
        ops = self.ops
        n = len(ops)
        CE = ("pe", "dve", "act", "pool")
        seen = {e: {f: -1 for f in ENGS} for e in ENGS}
        seen_dma = {e: set() for e in ENGS}
        need = [[] for _ in range(n)]
        is_prod = [False] * n
        last_c = {e: -1 for e in CE}
        bar_need = {}
        for i, o in enumerate(ops):
            if o.get("barrier"):
                for e in ENGS:
                    lst = []
                    for e2 in CE:
                        j = last_c[e2]
                        if j >= 0 and e2 != e and j > seen[e][e2]:
                            lst.append(j)
                            is_prod[j] = True
                            seen[e][e2] = j
                    bar_need[(i, e)] = lst
                continue
            e = o["eng"]
            if not o["dma"]:
                last_c[e] = i
            best = {}
            for (d, kind) in o["deps"]:
                po = ops[d]
                pe_ = po["eng"]
                if po["dma"]:
                    if d in seen_dma[e]:
                        continue
                    best[("dma", d)] = d
                    continue
                if pe_ == e and not o["dma"]:
                    if kind != "raw" or e == "pe":
                        continue
                if d <= seen[e][pe_]:
                    continue
                k = ("c", pe_)
                if k not in best or best[k] < d:
                    best[k] = d
            for k, d in best.items():
                need[i].append(d)
                is_prod[d] = True
                if k[0] == "dma":
                    seen_dma[e].add(d)
                else:
                    seen[e][k[1]] = d
        csem = {e: stack.enter_context(nc.semaphore("c_" + e)) for e in ENGS}
        dsem = [stack.enter_context(nc.semaphore("d%d" % k)) for k in range(N_DMA_SEMS)]
        bsem = stack.enter_context(nc.semaphore("bar"))
        cnt = {e: 0 for e in ENGS}
        dcount = [0] * N_DMA_SEMS
        ndma = 0
        waitval = [None] * n
        dma_slot_prev = {}
        extra_wait = [None] * n
        slot_last = {}
        bar_idx = {}
        nbar = 0
        for i, o in enumerate(ops):
            if o.get("barrier"):
                slot_last[i] = dict(dma_slot_prev)
                nbar += 1
                bar_idx[i] = nbar
                continue
            if o["dma"]:
                slot = ndma % N_DMA_SEMS
                ndma += 1
                dcount[slot] += 16
                waitval[i] = (dsem[slot], dcount[slot])
                if slot in dma_slot_prev:
                    extra_wait[i] = dma_slot_prev[slot]
                dma_slot_prev[slot] = i
                o["dsem"] = dsem[slot]
            elif is_prod[i]:
                cnt[o["eng"]] += 1
                waitval[i] = (csem[o["eng"]], cnt[o["eng"]])
        per_eng = {e: [i for i, o in enumerate(ops) if o["eng"] == e or o.get("barrier")] for e in ENGS}
        block = stack.enter_context(nc.Block())
        self.n_waits = 0
        self.n_ins = {e: len(per_eng[e]) for e in ENGS}

        def run(engobj, ename):
            for i in per_eng[ename]:
                o = ops[i]
                if o.get("barrier"):
                    for d in bar_need[(i, ename)]:
                        s_, v = waitval[d]
                        engobj.wait_ge(s_, v)
                    if ename == "sp":
                        for slot, d in slot_last[i].items():
                            s_, v = waitval[d]
                            engobj.wait_ge(s_, v)
                        engobj.dma_start(out=self.bar_dst, in_=self.bar_src).then_inc(bsem, 16)
                    else:
                        engobj.wait_ge(bsem, 16 * bar_idx[i])
                    continue
                ws = list(need[i])
                if extra_wait[i] is not None:
                    ws.append(extra_wait[i])
                for d in ws:
                    s_, v = waitval[d]
                    engobj.wait_ge(s_, v)
                    self.n_waits += 1
                ins = o["fn"](engobj)
                if o["dma"]:
                    ins.then_inc(o["dsem"], 16)
                elif is_prod[i]:
                    ins.then_inc(csem[ename], 1)
            for i in per_eng[ename]:
                if ops[i]["dma"] and ops[i]["final"] and ops[i]["eng"] == ename:
                    s_, v = waitval[i]
                    engobj.wait_ge(s_, v)

        @block.tensor
        def _(e):
            run(e, "pe")

        @block.vector
        def _(e):
            run(e, "dve")

        @block.scalar
        def _(e):
            run(e, "act")

        @block.gpsimd
        def _(e):
            run(e, "pool")

        @block.sync
        def _(e):
            run(e, "sp")


V_CONVW = 0
V_CONVB = 62
V_CONVG = 64
V_CONVBETA = 66
V_QNG = 68
V_KVNG = 70
V_SSMD = 71
V_BMERGE = 73
V_LNG = 105
V_LNB = 113
V_PLEG = 121
V_LR = 129
V_LI = 137
V_LOGDT = 145
V_SINK = 153
NV = 160

C_IDENT = 0
C_TRI = 128
C_SWM = 256
C_IOTA = 512
NCM = 1024


def build(S, L, debug=False):
    NB = S // TB
    NT = S // 128
    nc = bass.Bass("TRN2", target_bir_lowering=False)

    def din(name, shape):
        return nc.dram_tensor(name, list(shape), F32, kind="ExternalInput").ap()

    xT = din("xT", [D, S])
    pT = din("pT", [L, 256, S])
    w1a = din("w1a", [L, D, 10 * 128])
    w1b = din("w1b", [L, D, 13 * 128])
    wuq = din("wuq", [L, 256, 8 * 128])
    wukv = din("wukv", [L, 128, 512])
    wpw2 = din("wpw2", [L, 256, 256])
    wglu = din("wglu", [L, 256, 512])
    vecs = din("vecs", [L, 128, NV])
    s5b = din("s5b", [L, 128, 2 * 8 * 16])
    s5c = din("s5c", [L, 128, 2 * 8 * 16])
    wmerge = din("wmerge", [L, D, 4 * D])
    wbranch = din("wbranch", [L, 4 * 256, D])
    wout = din("wout", [L, D, D])
    wple = din("wple", [L, 256, D])
    wpleg = din("wpleg", [L, D, D])
    ropec = din("ropec", [128, S])
    ropes = din("ropes", [128, S])
    cmat = din("cmat", [128, NCM])
    yT = nc.dram_tensor("yT", [D, S], F32, kind="ExternalOutput").ap()
    ysd = nc.dram_tensor("ysd", [D, S], BF16, kind=("ExternalOutput" if debug else "Internal")).ap()
    mgd = nc.dram_tensor("mgd", [D, S], BF16, kind=("ExternalOutput" if debug else "Internal")).ap()
    xs = nc.dram_tensor("xs", [D, S], F32, kind="Internal").ap()
    rotd = nc.dram_tensor("rotd", [128, 16 * TB], F32, kind="Internal").ap()

    st = ExitStack()
    with st:
        s = Sched(nc)
        uid = [0]

        def sb(shape, dt=F32, name=None):
            uid[0] += 1
            return st.enter_context(nc.sbuf_tensor("%s_%d" % (name or "t", uid[0]), list(shape), dt))

        psum = st.enter_context(nc.psum_tensor("psum", [128, 8 * 512], F32))
        psi = [0]

        def nps(nbanks=1):
            NR = NROT[0]
            b = psi[0] % NR
            if nbanks == 2:
                while b % 2 == 1 or b + 2 > NR:
                    psi[0] += 1
                    b = psi[0] % NR
            psi[0] += nbanks
            key = tuple("ps%d" % (b + i) for i in range(nbanks))
            return psum[:, b * 512:(b + nbanks) * 512], key

        acci = [0]
        NROT = [5]
        s5i = [0]

        def ns5():
            b = 4 + 2 * (s5i[0] % 2)
            s5i[0] += 1
            return (psum[:, b * 512:(b + 1) * 512], ("ps%d" % b,)), (psum[:, (b + 1) * 512:(b + 2) * 512], ("ps%d" % (b + 1),))

        def nacc():
            b = 5 + acci[0] % 3
            acci[0] += 1
            return psum[:, b * 512:(b + 1) * 512], ("ps%d" % b,)

        cm = sb([128, NCM], F32, "cm")
        s.dma(REC("dma_start", out=cm[:], in_=cmat), w=["cm"])
        cmb = sb([128, 512], BF16, "cmb")
        s.dve(REC("tensor_copy", out=cmb[:], in_=cm[:, 0:512]), r=["cm"], w=["cmb"])
        ident_f = cm[:, C_IDENT:C_IDENT + 128]
        tri_b = cmb[:, C_TRI:C_TRI + 128]
        swm_b = cmb[:, C_SWM:C_SWM + 256]
        iota_f = cm[:, C_IOTA:C_IOTA + 512]
        bard = nc.dram_tensor("bard", [1, 16], F32, kind="Internal").ap()
        s.bar_dst = bard
        s.bar_src = cm[0:1, 0:16]
        ones_b = sb([128, 128], BF16, "ones")
        s.dve(REC("memset", ones_b[:], 1.0), w=["ones"])
        vec = sb([128, NV], F32, "vec")

        def V(c, n=1):
            return vec[:, c:c + n]

        def mm(ps_ap, pskey, pairs, extra_r=()):
            n = len(pairs)
            for i, (l, r, keys) in enumerate(pairs):
                M_ = l.shape[1]
                o_ap = ps_ap if M_ == 128 else ps_ap[0:M_, :]
                s.pe(REC("matmul", o_ap, lhsT=l, rhs=r, start=(i == 0), stop=(i == n - 1)),
                     r=list(keys) + list(extra_r), w=list(pskey))

        def rsqrt(out, in_, scale, eps, rk, wk, eng="dve"):
            s.dve(REC("tensor_scalar", out=out, in0=in_, scalar1=float(scale), scalar2=float(eps), op0=ALU.mult, op1=ALU.add), r=rk, w=wk)
            s.act(REC("activation", out=out, in_=out, func=AF.Sqrt), r=wk, w=wk)
            s.dve(REC("reciprocal", out=out, in_=out), r=wk, w=wk)

        WSL = {}

        def load_w(tile, dram2d, K, key, bounds=None, order=None, by_k=False):
            nk = max(1, K // 128)
            N = tile.shape[2]
            if by_k:
                WSL[key] = ["k"]
                for kt in range(nk):
                    s.dma(REC("dma_start", out=tile[:, kt, :], in_=dram2d[kt * 128:(kt + 1) * 128, :]), w=["%s@k%d" % (key, kt)], q="pool")
                return
            bounds = list(bounds or [0, N])
            WSL[key] = bounds
            for si in (order if order is not None else range(len(bounds) - 1)):
                c0, c1 = bounds[si], bounds[si + 1]
                for kt in range(nk):
                    s.dma(REC("dma_start", out=tile[:, kt, c0:c1], in_=dram2d[kt * 128:(kt + 1) * 128, c0:c1]), w=["%s@%d" % (key, si)], q="pool")

        def wkey(key, col=0, kt=None):
            bd = WSL[key]
            if bd[0] == "k":
                return "%s@k%d" % (key, kt)
            si = 0
            while si + 1 < len(bd) - 1 and col >= bd[si + 1]:
                si += 1
            return "%s@%d" % (key, si)

        xf = sb([128, 8, TB], F32, "xf")
        xb = sb([128, 8, TB], BF16, "xb")

        def load_x(l, b, xb=xb, xf=xf, cast=True):
            src = xT if l == 0 else xs
            for kt in range(8):
                s.dma(REC("dma_start", out=xf[:, kt, :], in_=src[kt * 128:(kt + 1) * 128, b * TB:(b + 1) * TB]),
                      w=["xf%d" % kt], r=(["xs_%d" % b] if l > 0 else []))
            if not cast:
                return
            for kt in range(8):
                if kt % 2 == 0:
                    s.act(REC("activation", out=xb[:, kt, :], in_=xf[:, kt, :], func=AF.Copy), r=["xf%d" % kt], w=["xb%d" % kt])
                else:
                    s.pool(REC("tensor_copy", out=xb[:, kt, :], in_=xf[:, kt, :]), r=["xf%d" % kt], w=["xb%d" % kt])

        XB = ["xb%d" % k for k in range(8)]

        def run_pipe(gen_fn, nblocks, lag, dbl):
            s.dbl = tuple(dbl)
            active = []
            nxt = 0
            while nxt < nblocks or active:
                if nxt < nblocks and len(active) < 2 and (not active or active[-1]["n"] >= lag):
                    active.append(dict(g=gen_fn(nxt), b=nxt, n=0, done=set(), blocked=None))
                    nxt += 1
                for a in list(active):
                    older = active[0] if (a is not active[0]) else None
                    if a["blocked"] is not None:
                        if older is None or a["blocked"] in older["done"]:
                            a["blocked"] = None
                        else:
                            continue
                    s.par = a["b"] % 2
                    try:
                        r = next(a["g"])
                        a["n"] += 1
                        if isinstance(r, tuple):
                            if r[0] == "done":
                                a["done"].add(r[1])
                            elif r[0] == "wait" and older is not None and r[1] not in older["done"]:
                                a["blocked"] = r[1]
                    except StopIteration:
                        active.remove(a)
            s.par = 0
            s.dbl = ()

        for l in range(L):
            s.dma(REC("dma_start", out=vec[:], in_=vecs[l]), w=["vec"])
            scA = ExitStack()
            SW = os.environ.get('KSWEEPS', 'A,B,C1,C2').split(',')
            with scA:
              if 'A' in SW:
                def sa(shape, dt=F32, name=None):
                    uid[0] += 1
                    return scA.enter_context(nc.sbuf_tensor("%s_%d" % (name or "a", uid[0]), list(shape), dt))
                scP = ExitStack()

                def sp2(shape, dt=F32, name=None):
                    uid[0] += 1
                    return scP.enter_context(nc.sbuf_tensor("%s_%d" % (name or "p", uid[0]), list(shape), dt))
                wA = sa([128, 8, 1280], BF16, "wA")
                if 'lw' not in os.environ.get('KA_SKIP', ''):
                    load_w(wA, w1a[l], D, "wA", bounds=[0, 512, 768, 1280], order=[1, 2, 0])
                wp2 = sa([128, 2, 256], BF16, "wp2")
                if 'lw' not in os.environ.get('KA_SKIP', ''):
                    load_w(wp2, wpw2[l], 256, "wp2")
                wgl = sa([128, 2, 512], BF16, "wgl")
                if 'lw' not in os.environ.get('KA_SKIP', ''):
                    load_w(wgl, wglu[l], 256, "wgl")
                dwt = sa([128, 62, 128], BF16, "dwt")
                for cj in (range(62) if 'dwt' not in os.environ.get('KA_SKIP', '') else []):
                    f = s.dve
                    f(REC("tensor_scalar", out=dwt[:, cj, :], in0=ident_f, scalar1=V(V_CONVW + cj), scalar2=None, op0=ALU.mult),
                      r=["cm", "vec"], w=["dwt"])
                sp_ = sa([128, 128], F32, "s5p")

                def P(i, n=8):
                    return sp_[:, i * 8:i * 8 + n]
                DT, MAG, TH, TI_F, THR, T2, LBR, LBI, DEN, FR, FI, NFI, TMP = range(13)
                spi = sa([128, 8], I32, "s5pi")
                CB = sa([128, 16, 128], BF16, "CB")
                BB = sa([128, 16, 128], BF16, "BB")
                K5 = ["s5p"]
                if 'prep' not in os.environ.get('KA_SKIP', ''):
                    s.act(REC("activation", out=P(DT), in_=V(V_LOGDT, 8), func=AF.Exp), r=["vec"], w=K5)
                    s.dve(REC("tensor_tensor", out=P(MAG), in0=V(V_LR, 8), in1=P(DT), op=ALU.mult), r=K5 + ["vec"], w=K5)
                    s.act(REC("activation", out=P(MAG), in_=P(MAG), func=AF.Exp), r=K5, w=K5)
                    s.dve(REC("scalar_tensor_tensor", out=P(TH), in0=V(V_LI, 8), scalar=float(1.0 / TWO_PI), in1=P(DT), op0=ALU.mult, op1=ALU.mult), r=K5 + ["vec"], w=K5)
                    s.dve(REC("tensor_copy", out=spi[:], in_=P(TH)), r=K5, w=["s5pi"])
                    s.dve(REC("tensor_copy", out=P(TI_F), in_=spi[:]), r=["s5pi"], w=K5)
                    s.dve(REC("tensor_tensor", out=P(THR), in0=P(TH), in1=P(TI_F), op=ALU.subtract), r=K5, w=K5)
                    s.act(REC("activation", out=P(LBI), in_=P(THR), func=AF.Sin, scale=6.28318), r=K5, w=K5)
                    s.dve(REC("tensor_scalar", out=P(T2), in0=P(THR), scalar1=0.25, scalar2=None, op0=ALU.add), r=K5, w=K5)
                    s.dve(REC("tensor_copy", out=spi[:], in_=P(T2)), r=K5, w=["s5pi"])
                    s.dve(REC("tensor_copy", out=P(TI_F), in_=spi[:]), r=["s5pi"], w=K5)
                    s.dve(REC("tensor_tensor", out=P(T2), in0=P(T2), in1=P(TI_F), op=ALU.subtract), r=K5, w=K5)
                    s.act(REC("activation", out=P(LBR), in_=P(T2), func=AF.Sin, scale=6.28318), r=K5, w=K5)
                    s.dve(REC("tensor_tensor", out=P(LBR), in0=P(LBR), in1=P(MAG), op=ALU.mult), r=K5, w=K5)
                    s.dve(REC("tensor_tensor", out=P(LBI), in0=P(LBI), in1=P(MAG), op=ALU.mult), r=K5, w=K5)
                    s.dve(REC("tensor_tensor", out=P(DEN), in0=V(V_LR, 8), in1=V(V_LR, 8), op=ALU.mult), r=["vec"] + K5, w=K5)
                    s.dve(REC("tensor_tensor", out=P(TMP), in0=V(V_LI, 8), in1=V(V_LI, 8), op=ALU.mult), r=["vec"] + K5, w=K5)
                    s.dve(REC("tensor_tensor", out=P(DEN), in0=P(DEN), in1=P(TMP), op=ALU.add), r=K5, w=K5)
                    s.dve(REC("reciprocal", out=P(DEN), in_=P(DEN)), r=K5, w=K5)
                    s.dve(REC("tensor_scalar", out=P(T2), in0=P(LBR), scalar1=-1.0, scalar2=None, op0=ALU.add), r=K5, w=K5)
                    s.dve(REC("tensor_tensor", out=P(FR), in0=P(T2), in1=V(V_LR, 8), op=ALU.mult), r=K5 + ["vec"], w=K5)
                    s.dve(REC("tensor_tensor", out=P(TMP), in0=P(LBI), in1=V(V_LI, 8), op=ALU.mult), r=K5 + ["vec"], w=K5)
                    s.dve(REC("tensor_tensor", out=P(FR), in0=P(FR), in1=P(TMP), op=ALU.add), r=K5, w=K5)
                    s.dve(REC("tensor_tensor", out=P(FR), in0=P(FR), in1=P(DEN), op=ALU.mult), r=K5, w=K5)
                    s.dve(REC("tensor_tensor", out=P(FI), in0=P(LBI), in1=V(V_LR, 8), op=ALU.mult), r=K5 + ["vec"], w=K5)
                    s.dve(REC("tensor_tensor", out=P(TMP), in0=P(T2), in1=V(V_LI, 8), op=ALU.mult), r=K5 + ["vec"], w=K5)
                    s.dve(REC("tensor_tensor", out=P(FI), in0=P(FI), in1=P(TMP), op=ALU.subtract), r=K5, w=K5)
                    s.dve(REC("tensor_tensor", out=P(FI), in0=P(FI), in1=P(DEN), op=ALU.mult), r=K5, w=K5)
                    s.dve(REC("tensor_scalar", out=P(NFI), in0=P(FI), scalar1=-1.0, scalar2=None, op0=ALU.mult), r=K5, w=K5)
                bst = sp2([128, 256], F32, "bst")
                cst = sp2([128, 256], F32, "cst")
                s.dma(REC("dma_start", out=bst[:], in_=s5b[l]), w=["bst"])
                s.dma(REC("dma_start", out=cst[:], in_=s5c[l]), w=["cst"])
                bb = sp2([128, 256], F32, "bb")
                tmpb = sp2([128, 128], F32, "tmpb")
                b3 = lambda t, ri: t[:, ri * 128:(ri + 1) * 128].rearrange("p (a h) -> p a h", h=16)
                fr_b = P(FR).unsqueeze(2).broadcast_to([128, 8, 16])
                fi_b = P(FI).unsqueeze(2).broadcast_to([128, 8, 16])
                nfi_b = P(NFI).unsqueeze(2).broadcast_to([128, 8, 16])
                t3 = tmpb[:, :].rearrange("p (a h) -> p a h", h=16)
                if 'bbc' not in os.environ.get('KA_SKIP', ''):
                    s.dve(REC("tensor_tensor", out=b3(bb, 0), in0=b3(bst, 0), in1=fr_b, op=ALU.mult), r=["bst"] + K5, w=["bb"])
                    s.dve(REC("tensor_tensor", out=t3, in0=b3(bst, 1), in1=nfi_b, op=ALU.mult), r=["bst"] + K5, w=["tmpb"])
                    s.dve(REC("tensor_tensor", out=b3(bb, 0), in0=b3(bb, 0), in1=t3, op=ALU.add), r=["bb", "tmpb"], w=["bb"])
                    s.dve(REC("tensor_tensor", out=b3(bb, 1), in0=b3(bst, 1), in1=fr_b, op=ALU.mult), r=["bst"] + K5, w=["bb"])
                    s.dve(REC("tensor_tensor", out=t3, in0=b3(bst, 0), in1=fi_b, op=ALU.mult), r=["bst", "bb"] + K5, w=["tmpb"])
                    s.dve(REC("tensor_tensor", out=b3(bb, 1), in0=b3(bb, 1), in1=t3, op=ALU.add), r=["bb", "tmpb"], w=["bb"])
                BT = sp2([128, 16, 128], F32, "BT")
                if 'ms' not in os.environ.get('KA_SKIP', ''):
                    s.pool(REC("memset", BT[:], 0.0), w=["BT"])
                    s.pool(REC("memset", CB[:], 0.0), w=["CB"])
                if 'scat' not in os.environ.get('KA_SKIP', ''):
                    for ri in range(2):
                        for half in range(2):
                            for grp in range(2):
                                prt = slice(half * 64, half * 64 + 64)
                                dst = BT[prt, ri * 8 + grp * 4: ri * 8 + grp * 4 + 4, :]
                                src = bb[prt, ri * 128 + grp * 64: ri * 128 + grp * 64 + 64].rearrange("p (a h) -> p a h", h=16)
                                for a in range(4):
                                    s.dve(REC("tensor_copy",
                                        out=BT[prt, ri * 8 + grp * 4 + a, a * 32 + half * 16: a * 32 + half * 16 + 16],
                                        in_=bb[prt, ri * 128 + (grp * 4 + a) * 16: ri * 128 + (grp * 4 + a) * 16 + 16]), r=["bb", "BT"], w=["BT"])
                                    if ri == 0:
                                        s.pool(REC("tensor_copy",
                                            out=CB[prt, grp * 4 + a, a * 32 + half * 16: a * 32 + half * 16 + 16],
                                            in_=cst[prt, (grp * 4 + a) * 16:(grp * 4 + a) * 16 + 16]), r=["cst", "CB"], w=["CB"])
                                    else:
                                        s.dve(REC("tensor_scalar",
                                            out=CB[prt, 8 + grp * 4 + a, a * 32 + half * 16: a * 32 + half * 16 + 16],
                                            in0=cst[prt, 128 + (grp * 4 + a) * 16:128 + (grp * 4 + a) * 16 + 16], scalar1=-1.0, scalar2=None, op0=ALU.mult), r=["cst", "CB"], w=["CB"])
                if 'tr' not in os.environ.get('KA_SKIP', ''):
                    for t in range(16):
                        pt, pk = nps()
                        s.pe(REC("transpose", pt[:, 0:128], BT[:, t, :], ident_f), r=["BT", "cm"], w=list(pk))
                        s.act(REC("activation", out=BB[:, t, :], in_=pt[:, 0:128], func=AF.Copy), r=list(pk), w=["BB"])
                rt = sp2([128, TB], F32, "rt")
                rti = sp2([128, TB], I32, "rti")
                rtf = sp2([128, TB], F32, "rtf")
                rto = [sp2([128, TB], F32, "rto%d" % i) for i in range(2)]
                if 'rot' not in os.environ.get('KA_SKIP', ''):
                    for p in range(8):
                        for cs in range(2):
                            ko = "rto%d" % cs
                            s.dve(REC("tensor_scalar", out=rt[:], in0=iota_f, scalar1=sp_[:, THR * 8 + p:THR * 8 + p + 1], scalar2=(0.25 if cs == 0 else 0.0), op0=ALU.mult, op1=ALU.add),
                                  r=["cm"] + K5, w=["rt"])
                            s.dve(REC("tensor_copy", out=rti[:], in_=rt[:]), r=["rt"], w=["rti"])
                            s.dve(REC("tensor_copy", out=rtf[:], in_=rti[:]), r=["rti"], w=["rtf"])
                            s.dve(REC("tensor_tensor", out=rt[:], in0=rt[:], in1=rtf[:], op=ALU.subtract), r=["rt", "rtf"], w=["rt"])
                            s.act(REC("activation", out=rto[cs][:], in_=rt[:], func=AF.Sin, scale=6.28318), r=["rt"], w=[ko])
                            s.dma(REC("dma_start", out=rotd[:, (p * 2 + cs) * TB:(p * 2 + cs + 1) * TB], in_=rto[cs][:]), r=[ko], w=["rotd%d" % p])
                s.barrier()
                scP.close()
                carry = sa([128, 16], F32, "carry")
                s.dve(REC("memset", carry[:], 0.0), w=["carryR%d" % p for p in range(8)] + ["carryI%d" % p for p in range(8)])
                gbuf = sa([128, 2, 32 + TB], BF16, "gbuf")
                s.pool(REC("memset", gbuf[:], 0.0), w=["gbuf0", "gbuf1"])
                szA = sa([128, 4, TB], BF16, "szA")
                sg = [sa([128, TB], F32, "sg%d" % i) for i in range(2)]
                hcf = sa([128, 2, TB], F32, "hcf")
                hcb = sa([128, 2, TB], BF16, "hcb")
                hsq = sa([128, 2, TB], BF16, "hsq")
                mean_sb = sa([128, TB], F32, "mean_sb")
                var_sb = sa([128, TB], F32, "var_sb")
                hnb = sa([128, 2, TB], BF16, "hnb")
                ysA = sa([128, 4, TB], BF16, "ysA")
                uf = sa([128, 2, TB], F32, "uf")
                ub = sa([128, 2, TB], BF16, "ub")
                rot = [sa([128, 2, TB], F32, "rot%d" % i) for i in range(4)]
                rotc = [0, 0]
                w5 = [[sa([128, TB], F32, "w5_%d_%d" % (i, j)) for j in range(6)] for i in range(2)]
                hb = sa([128, 16, TB], BF16, "hb")
                ygf = sa([128, 2, TB], F32, "ygf")
                ygb = sa([128, 2, TB], BF16, "ygb")
                sgl = sa([128, 2, TB], F32, "sgl")
                tmpc = sa([128, 2, TB], F32, "tmpc")

                xbA = [xb, sa([128, 8, TB], BF16, "xbA2")]
                szAs = [szA, sa([128, 4, TB], BF16, "szA2")]
                ufs = [uf, sa([128, 2, TB], F32, "uf2")]
                ubs = [ub, sa([128, 2, TB], BF16, "ub2")]
                ysAs = [ysA, sa([128, 4, TB], BF16, "ysA2")]

                if l == 0 and os.environ.get('KDBG'):
                    print('SBUF remaining after sweep A allocs', nc.sbuf_bytes_remaining)

                NROT[0] = 4

                def blockA(b):
                    xb = xbA[b % 2]
                    szA = szAs[b % 2]
                    uf = ufs[b % 2]
                    ub = ubs[b % 2]
                    ysA = ysAs[b % 2]
                    load_x(l, b, xb)
                    yield
                    c0 = b * TB
                    def proj(mt, wt=wA):
                        pt, pk = nps()
                        mm(pt, pk, [(wt[:, k, mt * 128:(mt + 1) * 128], xb[:, k, :], [wkey("wA", mt * 128), "xb%d" % k]) for k in range(8)])
                        return pt, pk
                    for c in range(2):
                        pz, kz = proj(4 + c)
                        s.act(REC("activation", out=szA[:, c, :], in_=pz, func=AF.Silu), r=list(kz), w=["szA%d" % c])
                        yield
                        pz2, kz2 = proj(8 + c)
                        s.act(REC("activation", out=szA[:, 2 + c, :], in_=pz2, func=AF.Silu), r=list(kz2), w=["szA%d" % (2 + c)])
                        yield
                        pu, ku = proj(6 + c)
                        if 'ufa' not in os.environ.get('KA_SKIP', ''):
                            s.act(REC("activation", out=uf[:, c, :], in_=pu, func=AF.Copy), r=list(ku), w=["uf%d" % c])
                        if 'ubd' not in os.environ.get('KA_SKIP', ''):
                            s.pool(REC("tensor_copy", out=ub[:, c, :], in_=uf[:, c, :]), r=["uf%d" % c], w=["ub%d" % c])
                        yield
                    yield ("wait", "conv")
                    for c in range(2):
                        pg, kg = proj(2 + c)
                        s.act(REC("activation", out=sg[c][:], in_=pg, func=AF.Sigmoid), r=list(kg), w=["sg%d" % c])
                        pv, kv = proj(c)
                        if 'glu' not in os.environ.get('KA_SKIP', ''):
                            s.dve(REC("tensor_tensor", out=gbuf[:, c, 32:32 + TB], in0=pv, in1=sg[c][:], op=ALU.mult), r=list(kv) + ["sg%d" % c], w=["gbuf%d" % c])
                        yield
                    if 'conv' not in os.environ.get('KA_SKIP', ''):
                        pcs = []
                        for c in range(2):
                            pc, kc = nps()
                            mm(pc, kc, [(dwt[:, c * 31 + j, :], gbuf[:, c, 2 + j:2 + j + TB], ["dwt", "gbuf%d" % c]) for j in range(31)])
                            pcs.append((pc, kc))
                            s.act(REC("activation", out=hcf[:, c, :], in_=pc, func=AF.Identity, bias=V(V_CONVB + c)), r=list(kc) + ["vec"], w=["hcf%d" % c])
                            s.act(REC("activation", out=hsq[:, c, :], in_=pc, func=AF.Square, bias=V(V_CONVB + c)), r=list(kc) + ["vec"], w=["hsq%d" % c])
                            s.pool(REC("tensor_copy", out=hcb[:, c, :], in_=hcf[:, c, :]), r=["hcf%d" % c], w=["hcb%d" % c])
                            s.pool(REC("tensor_copy", out=gbuf[:, c, 0:32], in_=gbuf[:, c, TB:TB + 32]), r=["gbuf%d" % c], w=["gbuf%d" % c])
                            yield
                        pm, km = nps()
                        mm(pm, km, [(ones_b[:], hcb[:, c, :], ["ones", "hcb%d" % c]) for c in range(2)])
                        pq, kq = nps()
                        mm(pq, kq, [(ones_b[:], hsq[:, c, :], ["ones", "hsq%d" % c]) for c in range(2)])
                        s.act(REC("activation", out=mean_sb[:], in_=pm, func=AF.Copy, scale=1.0 / 256), r=list(km), w=["mean_sb"])
                        s.dve(REC("tensor_tensor", out=var_sb[:], in0=mean_sb[:], in1=mean_sb[:], op=ALU.mult), r=["mean_sb"], w=["var_sb"])
                        s.dve(REC("scalar_tensor_tensor", out=var_sb[:], in0=pq, scalar=1.0 / 256, in1=var_sb[:], op0=ALU.mult, op1=ALU.subtract), r=list(kq) + ["var_sb"], w=["var_sb"])
                        rsqrt(var_sb[:], var_sb[:], 1.0, LN_EPS, ["var_sb"], ["var_sb"])
                        for c in range(2):
                            s.dve(REC("tensor_tensor", out=hcf[:, c, :], in0=hcf[:, c, :], in1=mean_sb[:], op=ALU.subtract), r=["hcf%d" % c, "mean_sb"], w=["hcf%d" % c])
                            s.dve(REC("tensor_tensor", out=hcf[:, c, :], in0=hcf[:, c, :], in1=var_sb[:], op=ALU.mult), r=["hcf%d" % c, "var_sb"], w=["hcf%d" % c])
                            s.act(REC("activation", out=hnb[:, c, :], in_=hcf[:, c, :], func=AF.Silu, scale=V(V_CONVG + c), bias=V(V_CONVBETA + c)), r=["hcf%d" % c, "vec"], w=["hnb%d" % c])
                        for m in range(2):
                            py, ky = nps()
                            mm(py, ky, [(wp2[:, k, m * 128:(m + 1) * 128], hnb[:, k, :], [wkey("wp2"), "hnb%d" % k]) for k in range(2)])
                            s.dve(REC("tensor_tensor", out=ysA[:, m, :], in0=py, in1=szA[:, m, :], op=ALU.mult), r=list(ky) + ["szA%d" % m], w=["ysA%d" % m])
                            yield
                    yield ("done", "conv")
                    yield ("wait", "cmat")
                    if 's5' not in os.environ.get('KA_SKIP', ''):
                        for p in range(8):
                            eng = s.dve if p % 2 == 0 else s.pool
                            ei = p % 2
                            W_ = w5[ei]
                            WK = ["w5_%d_%d" % (ei, j) for j in range(6)]
                            ri_ = ei * 2 + rotc[ei] % 2
                            rotc[ei] += 1
                            rk = "rot%d" % ri_
                            s.dma(REC("dma_start", out=rot[ri_][:, :, :], in_=rotd[:, p * 2 * TB:(p * 2 + 2) * TB].rearrange("p (c t) -> p c t", c=2)), r=["rotd%d" % p], w=[rk])
                            Cc = rot[ri_][:, 0, :]
                            Ss = rot[ri_][:, 1, :]
                            kt = p // 4
                            (pr, kr), (pi_, ki) = ns5()
                            mm(pr, kr, [(BB[:, p, :], ub[:, kt, :], ["BB", "ub%d" % kt])])
                            mm(pi_, ki, [(BB[:, 8 + p, :], ub[:, kt, :], ["BB", "ub%d" % kt])])
                            if ei == 1:
                                s.act(REC("activation", out=W_[4][:], in_=pr, func=AF.Copy), r=list(kr), w=[WK[4]])
                                s.act(REC("activation", out=W_[5][:], in_=pi_, func=AF.Copy), r=list(ki), w=[WK[5]])
                                pr, kr = W_[4][:], [WK[4]]
                                pi_, ki = W_[5][:], [WK[5]]
                            eng(REC("tensor_tensor", out=W_[0][:], in0=pr, in1=Cc, op=ALU.mult), r=list(kr) + [rk], w=[WK[0]])
                            eng(REC("tensor_tensor", out=W_[1][:], in0=pi_, in1=Ss, op=ALU.mult), r=list(ki) + [rk], w=[WK[1]])
                            eng(REC("tensor_tensor", out=W_[0][:], in0=W_[0][:], in1=W_[1][:], op=ALU.add), r=[WK[0], WK[1]], w=[WK[0]])
                            eng(REC("tensor_tensor", out=W_[2][:], in0=pi_, in1=Cc, op=ALU.mult), r=list(ki) + [rk], w=[WK[2]])
                            eng(REC("tensor_tensor", out=W_[3][:], in0=pr, in1=Ss, op=ALU.mult), r=list(kr) + [rk], w=[WK[3]])
                            eng(REC("tensor_tensor", out=W_[2][:], in0=W_[2][:], in1=W_[3][:], op=ALU.subtract), r=[WK[2], WK[3]], w=[WK[2]])
                            magb = sp_[:, MAG * 8 + p:MAG * 8 + p + 1].broadcast_to([128, TB])
                            s.dve(REC("tensor_tensor_scan", out=W_[4][:], data0=magb, data1=W_[0][:], initial=carry[:, p:p + 1], op0=ALU.mult, op1=ALU.add), r=[WK[0], "carryR%d" % p] + K5, w=[WK[4]])
                            s.dve(REC("tensor_tensor_scan", out=W_[5][:], data0=magb, data1=W_[2][:], initial=carry[:, 8 + p:9 + p], op0=ALU.mult, op1=ALU.add), r=[WK[2], "carryI%d" % p] + K5, w=[WK[5]])
                            yield
                            eng(REC("tensor_tensor", out=W_[0][:], in0=W_[4][:], in1=Cc, op=ALU.mult), r=[WK[4], rk], w=[WK[0]])
                            eng(REC("tensor_tensor", out=W_[1][:], in0=W_[5][:], in1=Ss, op=ALU.mult), r=[WK[5], rk], w=[WK[1]])
                            eng(REC("tensor_tensor", out=W_[0][:], in0=W_[0][:], in1=W_[1][:], op=ALU.subtract), r=[WK[0], WK[1]], w=[WK[0]])
                            eng(REC("tensor_tensor", out=W_[2][:], in0=W_[5][:], in1=Cc, op=ALU.mult), r=[WK[5], rk], w=[WK[2]])
                            eng(REC("tensor_tensor", out=W_[3][:], in0=W_[4][:], in1=Ss, op=ALU.mult), r=[WK[4], rk], w=[WK[3]])
                            eng(REC("tensor_tensor", out=W_[2][:], in0=W_[2][:], in1=W_[3][:], op=ALU.add), r=[WK[2], WK[3]], w=[WK[2]])
                            eng(REC("tensor_copy", out=carry[:, p:p + 1], in_=W_[0][:, TB - 1:TB]), r=[WK[0]], w=["carryR%d" % p])
                            eng(REC("tensor_copy", out=carry[:, 8 + p:9 + p], in_=W_[2][:, TB - 1:TB]), r=[WK[2]], w=["carryI%d" % p])
                            s.act(REC("activation", out=hb[:, p, :], in_=W_[0][:], func=AF.Copy), r=[WK[0]], w=["hb%d" % p])
                            s.act(REC("activation", out=hb[:, 8 + p, :], in_=W_[2][:], func=AF.Copy), r=[WK[2]], w=["hb%d" % (8 + p)])
                            yield
                        for m in range(2):
                            pyc, kyc = nps()
                            prs = []
                            for p in range(4 * m, 4 * m + 4):
                                prs.append((CB[:, p, :], hb[:, p, :], ["CB", "hb%d" % p]))
                                prs.append((CB[:, 8 + p, :], hb[:, 8 + p, :], ["CB", "hb%d" % (8 + p)]))
                            mm(pyc, kyc, prs)
                            s.dve(REC("scalar_tensor_tensor", out=ygf[:, m, :], in0=uf[:, m, :], scalar=V(V_SSMD + m), in1=pyc, op0=ALU.mult, op1=ALU.add), r=list(kyc) + ["uf%d" % m, "vec"], w=["ygf%d" % m])
                            yield
                        yield ("done", "cmat")
                        for m in range(2):
                            s.act(REC("activation", out=tmpc[:, m, :], in_=ygf[:, m, :], func=AF.Square), r=["ygf%d" % m], w=["tmpc%d" % m])
                            s.dve(REC("tensor_scalar", out=tmpc[:, m, :], in0=tmpc[:, m, :], scalar1=0.044715, scalar2=1.0, op0=ALU.mult, op1=ALU.add), r=["tmpc%d" % m], w=["tmpc%d" % m])
                            s.dve(REC("tensor_tensor", out=tmpc[:, m, :], in0=tmpc[:, m, :], in1=ygf[:, m, :], op=ALU.mult), r=["tmpc%d" % m, "ygf%d" % m], w=["tmpc%d" % m])
                            yield
                            s.act(REC("activation", out=sgl[:, m, :], in_=tmpc[:, m, :], func=AF.Sigmoid, scale=1.5957691216057308), r=["tmpc%d" % m], w=["sgl%d" % m])
                            s.pool(REC("tensor_tensor", out=ygb[:, m, :], in0=ygf[:, m, :], in1=sgl[:, m, :], op=ALU.mult), r=["ygf%d" % m, "sgl%d" % m], w=["ygb%d" % m])
                            yield
                        for m in range(2):
                            p2, k2 = nps()
                            mm(p2, k2, [(wgl[:, k, (2 + m) * 128:(3 + m) * 128], ygb[:, k, :], [wkey("wgl"), "ygb%d" % k]) for k in range(2)])
                            s.act(REC("activation", out=sgl[:, m, :], in_=p2, func=AF.Sigmoid), r=list(k2), w=["sgl%d" % m])
                            yield
                            p1, k1 = nps()
                            mm(p1, k1, [(wgl[:, k, m * 128:(m + 1) * 128], ygb[:, k, :], [wkey("wgl"), "ygb%d" % k]) for k in range(2)])
                            s.dve(REC("tensor_tensor", out=tmpc[:, m, :], in0=p1, in1=sgl[:, m, :], op=ALU.mult), r=list(k1) + ["sgl%d" % m], w=["tmpc%d" % m])
                            yield
                            s.pool(REC("tensor_tensor", out=ysA[:, 2 + m, :], in0=tmpc[:, m, :], in1=szA[:, 2 + m, :], op=ALU.mult), r=["tmpc%d" % m, "szA%d" % (2 + m)], w=["ysA%d" % (2 + m)])
                            yield
                    for m in range(4):
                        row = (m if m < 2 else 2 + m) * 128
                        s.dma(REC("dma_start", out=ysd[row:row + 128, c0:c0 + TB], in_=ysA[:, m, :]), r=["ysA%d" % m], w=["ysd_%d_%d" % (b, row // 128)])
                    yield

                run_pipe(blockA, NB, int(os.environ.get('KLAG_A', '4')), ("xb", "szA", "uf", "ub", "ysA"))
                NROT[0] = 5

            s.barrier()
            scB = ExitStack()
            with scB:
              if 'B' in SW:
                def sbb(shape, dt=F32, name=None):
                    uid[0] += 1
                    return scB.enter_context(nc.sbuf_tensor("%s_%d" % (name or "b", uid[0]), list(shape), dt))
                wB = sbb([128, 8, 13 * 128], BF16, "wB")
                load_w(wB, w1b[l], D, "wB", bounds=[0, 640, 1664])
                wq = sbb([128, 2, 1024], BF16, "wq")
                load_w(wq, wuq[l], 256, "wq")
                wkv = sbb([128, 1, 512], BF16, "wkv")
                load_w(wkv, wukv[l], 128, "wkv")
                Kc = sbb([128, 4, S], BF16, "Kc")
                Vc = sbb([128, NT, 4, 128], BF16, "Vc")
                s.pool(REC("memset", Vc[:], 1.0), w=["Vc"] + ["Vc_%d" % t for t in range(NT)])
                Ksw = sbb([128, 128 + TB], BF16, "Ksw")
                s.pool(REC("memset", Ksw[:], 0.0), w=["Ksw"])
                Vsw = sbb([128, 5, 2, 128], BF16, "Vsw")
                s.pool(REC("memset", Vsw[:], 1.0), w=["Vsw"])
                esk = sbb([128, 4], F32, "esk")
                s.act(REC("activation", out=esk[:], in_=V(V_SINK, 4), func=AF.Exp), r=["vec"], w=["esk"])
                eskf = sbb([128, 4, 128], F32, "eskf")
                s.dve(REC("tensor_copy", out=eskf[:], in_=esk[:, :].unsqueeze(2).broadcast_to([128, 4, 128])), r=["esk"], w=["eskf"])
                szB = sbb([128, 4, TB], BF16, "szB")
                cqf = sbb([128, 3, TB], F32, "cqf")
                cqs = sbb([128, 3, TB], BF16, "cqs")
                cqn = sbb([128, 3, TB], BF16, "cqn")
                rstd = [sbb([128, TB], F32, "rstd%d" % i) for i in range(2)]
                rc = sbb([128, TB], F32, "rc")
                rs_ = sbb([128, TB], F32, "rs")
                t1 = sbb([128, TB], F32, "t1")
                t2 = sbb([128, TB], F32, "t2")
                Qh = sbb([128, 4, TB], BF16, "Qh")
                qsw = sbb([128, 2, TB], BF16, "qsw")
                pbuf = [sbb([128, TB], BF16, "pbuf%d" % i) for i in range(3)]
                pbs = [sbb([128, 1024], BF16, "pbs%d" % i) for i in range(2)]
                rden = sbb([128, TB], F32, "rden")
                tmpo = sbb([128, TB], F32, "tmpo")
                ysB = sbb([128, 4, TB], BF16, "ysB")
                pcount = [0]

                xbB = [xb, sbb([128, 8, TB], BF16, "xbB2")]
                szBs = [szB, sbb([128, 4, TB], BF16, "szB2")]
                Qhs = [Qh, sbb([128, 4, TB], BF16, "Qh2")]
                ysBs = [ysB, sbb([128, 4, TB], BF16, "ysB2")]

                def blockB(b):
                    xb = xbB[b % 2]
                    szB = szBs[b % 2]
                    Qh = Qhs[b % 2]
                    ysB = ysBs[b % 2]
                    load_x(l, b, xb)
                    yield
                    yield ("wait", "front")
                    c0 = b * TB
                    s.dma(REC("dma_start", out=rc[:], in_=ropec[:, c0:c0 + TB]), w=["rc"])
                    s.dma(REC("dma_start", out=rs_[:], in_=ropes[:, c0:c0 + TB]), w=["rs"])

                    def projB(mt):
                        pt, pk = nps()
                        mm(pt, pk, [(wB[:, k, mt * 128:(mt + 1) * 128], xb[:, k, :], [wkey("wB", mt * 128), "xb%d" % k]) for k in range(8)])
                        return pt, pk
                    for c in range(3):
                        pc, kc = projB(c)
                        s.act(REC("activation", out=cqf[:, c, :], in_=pc, func=AF.Copy), r=list(kc), w=["cqf%d" % c])
                        s.act(REC("activation", out=cqs[:, c, :], in_=pc, func=AF.Square), r=list(kc), w=["cqs%d" % c])
                        yield
                    pm, km = nps()
                    mm(pm, km, [(ones_b[:], cqs[:, c, :], ["ones", "cqs%d" % c]) for c in range(2)])
                    rsqrt(rstd[0][:], pm, 1.0 / 256, RMS_EPS, list(km), ["rstd0"])
                    yield
                    pm2, km2 = nps()
                    mm(pm2, km2, [(ones_b[:], cqs[:, 2, :], ["ones", "cqs2"])])
                    rsqrt(rstd[1][:], pm2, 1.0 / 128, RMS_EPS, list(km2), ["rstd1"])
                    yield
                    for c in range(3):
                        gcol = V_QNG + c if c < 2 else V_KVNG
                        ri = 0 if c < 2 else 1
                        s.dve(REC("scalar_tensor_tensor", out=cqn[:, c, :], in0=cqf[:, c, :], scalar=V(gcol), in1=rstd[ri][:], op0=ALU.mult, op1=ALU.mult),
                              r=["cqf%d" % c, "rstd%d" % ri, "vec"], w=["cqn%d" % c])
                        yield
                    pa, ka = projB(3)
                    pb_, kb = projB(4)
                    R = slice(64, 96)
                    s.dve(REC("tensor_tensor", out=t1[R, :], in0=pa[R, :], in1=rc[R, :], op=ALU.mult), r=list(ka) + ["rc"], w=["t1"])
                    s.dve(REC("tensor_tensor", out=t2[R, :], in0=pb_[R, :], in1=rs_[R, :], op=ALU.mult), r=list(kb) + ["rs"], w=["t2"])
                    yield
                    for h in range(4):
                        f = s.dve if h % 2 == 0 else s.pool
                        f(REC("tensor_tensor", out=Kc[R, h, c0:c0 + TB], in0=t1[R, :], in1=t2[R, :], op=ALU.add), r=["t1", "t2"], w=["Kc_%d_%d" % (h, b)])
                    for h in range(4):
                        pk_, kk = nps()
                        mm(pk_, kk, [(wkv[:, 0, h * 64:(h + 1) * 64], cqn[:, 2, :], [wkey("wkv"), "cqn2"])])
                        s.act(REC("activation", out=Kc[0:64, h, c0:c0 + TB], in_=pk_[0:64, :], func=AF.Copy), r=list(kk), w=["Kc_%d_%d" % (h, b)])
                        yield
                    for j in range(4):
                        pv, kv = nps()
                        mm(pv[:, 0:256], kv, [(cqn[:, 2, j * 128:(j + 1) * 128], wkv[:, 0, 256:512], [wkey("wkv"), "cqn2"])])
                        s.act(REC("activation", out=Vc[:, b * 4 + j, :, 0:64], in_=pv[:, 0:256].rearrange("p (h d) -> p h d", d=64), func=AF.Copy), r=list(kv) + ["Vc"], w=["Vc_%d" % (b * 4 + j)])
                        yield
                    for h in range(4):
                        pA, kA = nps()
                        mm(pA, kA, [(wq[:, k, (2 * h) * 128:(2 * h) * 128 + 96], cqn[:, k, :], [wkey("wq"), "cqn%d" % k]) for k in range(2)])
                        pB, kB = nps()
                        mm(pB, kB, [(wq[:, k, (2 * h + 1) * 128:(2 * h + 1) * 128 + 96], cqn[:, k, :], [wkey("wq"), "cqn%d" % k]) for k in range(2)])
                        s.act(REC("activation", out=Qh[0:64, h, :], in_=pA[0:64, :], func=AF.Copy), r=list(kA), w=["Qh%d" % h])
                        s.dve(REC("tensor_tensor", out=t1[R, :], in0=pA[R, :], in1=rc[R, :], op=ALU.mult), r=list(kA) + ["rc"], w=["t1"])
                        s.dve(REC("tensor_tensor", out=t2[R, :], in0=pB[R, :], in1=rs_[R, :], op=ALU.mult), r=list(kB) + ["rs"], w=["t2"])
                        s.dve(REC("tensor_tensor", out=Qh[R, h, :], in0=t1[R, :], in1=t2[R, :], op=ALU.add), r=["t1", "t2", "Qh%d" % h], w=["Qh%d" % h])
                        yield
                    yield ("done", "front")
                    yield ("wait", "end")
                    for c in range(2):
                        pz, kz = projB(5 + c)
                        s.act(REC("activation", out=szB[:, c, :], in_=pz, func=AF.Silu), r=list(kz), w=["szB%d" % c])
                        pz2, kz2 = projB(11 + c)
                        s.act(REC("activation", out=szB[:, 2 + c, :], in_=pz2, func=AF.Silu), r=list(kz2), w=["szB%d" % (2 + c)])
                        yield
                    scale = (64 + 32) ** -0.5
                    nkt = 4 * b + 4
                    SK = 2
                    accs = {}

                    def emit_S(h, kt):
                        if kt == 0:
                            accs[h] = nacc()
                        jl = kt - 4 * b
                        qs = max(0, jl) * 128
                        ps_, ks = nps()
                        kb_ = kt // 4
                        s.pe(REC("matmul", ps_[:, qs:TB], lhsT=Kc[0:96, h, kt * 128:(kt + 1) * 128], rhs=Qh[0:96, h, qs:TB], start=True, stop=True),
                             r=["Kc_%d_%d" % (h, kb_), "Qh%d" % h], w=list(ks))
                        pi = pcount[0] % 3
                        pcount[0] += 1
                        s.act(REC("activation", out=pbuf[pi][:, qs:TB], in_=ps_[:, qs:TB], func=AF.Exp, scale=float(scale)), r=list(ks), w=["pbuf%d" % pi])
                        if jl >= 0:
                            s.pool(REC("tensor_tensor", out=pbuf[pi][:, qs:qs + 128], in0=pbuf[pi][:, qs:qs + 128], in1=tri_b, op=ALU.mult), r=["pbuf%d" % pi, "cmb"], w=["pbuf%d" % pi])
                        return pi, qs

                    def emit_PV(h, kt, pi, qs):
                        po, ko = accs[h]
                        s.pe(REC("matmul", po[:, qs:TB], lhsT=Vc[:, kt, h, :], rhs=pbuf[pi][:, qs:TB], start=(kt == 0), stop=(kt == nkt - 1)),
                             r=["Vc_%d" % kt, "pbuf%d" % pi], w=list(ko))
                        if kt == nkt - 1:
                            hp = slice((h % 2) * 64, (h % 2) * 64 + 64)
                            s.dve(REC("reciprocal", out=rden[64:128, :], in_=po[64:128, :]), r=list(ko), w=["rden"])
                            s.dve(REC("tensor_tensor", out=tmpo[hp, :], in0=po[0:64, :], in1=rden[64:128, :], op=ALU.mult), r=list(ko) + ["rden"], w=["tmpo"])
                            s.pool(REC("tensor_tensor", out=ysB[hp, h // 2, :], in0=tmpo[hp, :], in1=szB[hp, h // 2, :], op=ALU.mult), r=["tmpo", "szB%d" % (h // 2)], w=["ysB%d" % (h // 2)])

                    fifo = []
                    for h in range(4):
                        for kt in range(nkt):
                            fifo.append((h, kt) + emit_S(h, kt))
                            if len(fifo) > SK:
                                emit_PV(*fifo.pop(0))
                            yield
                    while fifo:
                        emit_PV(*fifo.pop(0))
                        yield
                    for c in range(2):
                        pq_, kq_ = projB(7 + c)
                        s.act(REC("activation", out=qsw[:, c, :], in_=pq_, func=AF.Copy), r=list(kq_), w=["qsw%d" % c])
                        yield
                    pk2, kk2 = projB(9)
                    s.act(REC("activation", out=Ksw[:, 128:128 + TB], in_=pk2, func=AF.Copy), r=list(kk2), w=["Ksw"])
                    yield
                    for j in range(4):
                        pv, kv = nps()
                        mm(pv[:, 0:128], kv, [(xb[:, k, j * 128:(j + 1) * 128], wB[:, k, 10 * 128:11 * 128], [wkey("wB", 10 * 128), "xb%d" % k]) for k in range(8)])
                        s.act(REC("activation", out=Vsw[:, 1 + j, :, 0:64], in_=pv[:, 0:128].rearrange("p (g d) -> p g d", d=64), func=AF.Copy), r=list(kv), w=["Vsw"])
                        yield
                    sscale = 64 ** -0.5
                    for h in range(4):
                        g = h // 2
                        qt_ = h % 2
                        gp = slice(g * 64, g * 64 + 64)
                        p2b, k2b = nps(2)
                        for i in range(4):
                            for wch in range(2):
                                col = (i * 2 + wch) * 128
                                kcol = (i + wch) * 128
                                s.pe(REC("matmul", p2b[:, col:col + 128], lhsT=Ksw[gp, kcol:kcol + 128], rhs=qsw[gp, qt_, i * 128:(i + 1) * 128], start=True, stop=True),
                                     r=["Ksw", "qsw%d" % qt_], w=list(k2b))
                        pi = h % 2
                        for hh in range(2):
                            s.act(REC("activation", out=pbs[pi][:, hh * 512:(hh + 1) * 512], in_=p2b[:, hh * 512:(hh + 1) * 512], func=AF.Exp, scale=float(sscale)), r=list(k2b), w=["pbs%d" % pi])
                        s.pool(REC("tensor_tensor", out=pbs[pi][:, :].rearrange("p (i m) -> p i m", m=256), in0=pbs[pi][:, :].rearrange("p (i m) -> p i m", m=256),
                                                            in1=swm_b.unsqueeze(1).broadcast_to([128, 4, 256]), op=ALU.mult), r=["pbs%d" % pi, "cmb"], w=["pbs%d" % pi])
                        po, ko = nacc()
                        for i in range(4):
                            first = True
                            for wch in range(2):
                                if b == 0 and i == 0 and wch == 0:
                                    continue
                                col = (i * 2 + wch) * 128
                                s.pe(REC("matmul", po[:, i * 128:(i + 1) * 128], lhsT=Vsw[:, i + wch, g, :], rhs=pbs[pi][:, col:col + 128], start=first, stop=(wch == 1)),
                                     r=["Vsw", "pbs%d" % pi], w=list(ko))
                                first = False
                        hp = slice((h % 2) * 64, (h % 2) * 64 + 64)
                        s.dve(REC("tensor_tensor", out=rden[64:128, :].rearrange("p (i m) -> p i m", m=128), in0=po[64:128, :].rearrange("p (i m) -> p i m", m=128),
                                                                in1=eskf[64:128, h:h + 1, :].broadcast_to([64, 4, 128]), op=ALU.add), r=list(ko) + ["eskf"], w=["rden"])
                        s.dve(REC("reciprocal", out=rden[64:128, :], in_=rden[64:128, :]), r=["rden"], w=["rden"])
                        s.dve(REC("tensor_tensor", out=tmpo[hp, :], in0=po[0:64, :], in1=rden[64:128, :], op=ALU.mult), r=list(ko) + ["rden"], w=["tmpo"])
                        s.pool(REC("tensor_tensor", out=ysB[hp, 2 + h // 2, :], in0=tmpo[hp, :], in1=szB[hp, 2 + h // 2, :], op=ALU.mult), r=["tmpo", "szB%d" % (2 + h // 2)], w=["ysB%d" % (2 + h // 2)])
                        yield
                    s.pool(REC("tensor_copy", out=Ksw[:, 0:128], in_=Ksw[:, TB:TB + 128]), r=["Ksw"], w=["Ksw"])
                    s.pool(REC("tensor_copy", out=Vsw[:, 0, :, 0:64], in_=Vsw[:, 4, :, 0:64]), r=["Vsw"], w=["Vsw"])
                    for m in range(4):
                        row = (2 + m if m < 2 else 4 + m) * 128
                        s.dma(REC("dma_start", out=ysd[row:row + 128, c0:c0 + TB], in_=ysB[:, m, :]), r=["ysB%d" % m], w=["ysd_%d_%d" % (b, row // 128)])
                    yield ("done", "end")

                run_pipe(blockB, NB, 1, ("xb", "szB", "Qh", "ysB"))

            s.barrier()
            scC = ExitStack()
            with scC:
              if 'C1' in SW:
                def sc(shape, dt=F32, name=None):
                    uid[0] += 1
                    return scC.enter_context(nc.sbuf_tensor("%s_%d" % (name or "c", uid[0]), list(shape), dt))
                wm = sc([128, 8, 4 * D], BF16, "wm")
                load_w(wm, wmerge[l], D, "wm", bounds=[0, 1024, 2048, 3072, 4096], order=[0])
                wbr = sc([128, 8, D], BF16, "wbr")
                load_w(wbr, wbranch[l], 1024, "wbr", by_k=True)
                load_w(wm, wmerge[l], D, "wm", bounds=[0, 1024, 2048, 3072, 4096], order=[1, 2, 3])
                ysl = sc([128, 8, TB], BF16, "ysl")
                gs = [sc([128, TB], F32, "gs%d" % i) for i in range(2)]
                tp = [sc([128, TB], F32, "tp%d" % i) for i in range(2)]
                mg = sc([128, 8, TB], F32, "mg")
                mgb = sc([128, 8, TB], BF16, "mgb")
                cnt = [0]
                for b in range(NB):
                    load_x(l, b)
                    c0 = b * TB
                    for t in range(8):
                        s.dma(REC("dma_start", out=ysl[:, t, :], in_=ysd[t * 128:(t + 1) * 128, c0:c0 + TB]), r=["ysd_%d_%d" % (b, t)], w=["ysl%d" % t])
                    for n in range(4):
                        for m in range(8):
                            i = cnt[0] % 2
                            cnt[0] += 1
                            pg, kg = nps()
                            mm(pg, kg, [(wm[:, k, n * D + m * 128:n * D + (m + 1) * 128], xb[:, k, :], [wkey("wm", n * D + m * 128), "xb%d" % k]) for k in range(8)])
                            s.act(REC("activation", out=gs[i][:], in_=pg, func=AF.Sigmoid, bias=V(V_BMERGE + n * 8 + m)), r=list(kg) + ["vec"], w=["gs%d" % i])
                            pb2, kb2 = nps()
                            mm(pb2, kb2, [(wbr[:, 2 * n + k, m * 128:(m + 1) * 128], ysl[:, 2 * n + k, :], [wkey("wbr", kt=2 * n + k), "ysl%d" % (2 * n + k)]) for k in range(2)])
                            if n == 0:
                                s.dve(REC("tensor_tensor", out=mg[:, m, :], in0=pb2, in1=gs[i][:], op=ALU.mult), r=list(kb2) + ["gs%d" % i], w=["mg%d" % m])
                            else:
                                s.dve(REC("tensor_tensor", out=tp[i][:], in0=pb2, in1=gs[i][:], op=ALU.mult), r=list(kb2) + ["gs%d" % i], w=["tp%d" % i])
                                if n < 3:
                                    s.pool(REC("tensor_tensor", out=mg[:, m, :], in0=mg[:, m, :], in1=tp[i][:], op=ALU.add), r=["mg%d" % m, "tp%d" % i], w=["mg%d" % m])
                                else:
                                    s.pool(REC("tensor_tensor", out=mgb[:, m, :], in0=mg[:, m, :], in1=tp[i][:], op=ALU.add), r=["mg%d" % m, "tp%d" % i], w=["mgb%d" % m])
                                    s.dma(REC("dma_start", out=mgd[m * 128:(m + 1) * 128, c0:c0 + TB], in_=mgb[:, m, :]), r=["mgb%d" % m], w=["mgd_%d_%d" % (b, m)])

            s.barrier()
            scD = ExitStack()
            with scD:
              if 'C2' in SW:
                def sd(shape, dt=F32, name=None):
                    uid[0] += 1
                    return scD.enter_context(nc.sbuf_tensor("%s_%d" % (name or "d", uid[0]), list(shape), dt))
                wo = sd([128, 8, D], BF16, "wo")
                load_w(wo, wout[l], D, "wo", bounds=[0, 256, 1024])
                wpg = sd([128, 8, D], BF16, "wpg")
                wpl = sd([128, 2, D], BF16, "wpl")
                load_w(wpl, wple[l], 256, "wpl")
                load_w(wpg, wpleg[l], D, "wpg")
                mgl = sd([128, 8, TB], BF16, "mgl")
                pb16 = sd([128, 2, TB], BF16, "pb16")
                zb = [sd([128, TB], BF16, "zb%d" % i) for i in range(2)]
                zq = [sd([128, TB], BF16, "zq%d" % i) for i in range(2)]
                mean2 = sd([128, TB], F32, "mean2")
                var2 = sd([128, TB], F32, "var2")
                xlb = sd([128, 8, TB], BF16, "xlb")
                ef = sd([128, 8, TB], F32, "ef")
                sge = [sd([128, TB], F32, "sge%d" % i) for i in range(2)]
                rse = sd([128, TB], F32, "rse")
                xo = [sd([128, TB], F32, "xo%d" % i) for i in range(2)]
                alpha = (2.0 * 4) ** 0.25
                dst = yT if l == L - 1 else xs
                xfs = [xf, sd([128, 8, TB], F32, "xfD2")]
                mgls = [mgl, sd([128, 8, TB], BF16, "mgl2")]
                pb16s = [pb16, sd([128, 2, TB], BF16, "pb16b")]
                xlbs = [xlb, sd([128, 8, TB], BF16, "xlb2")]
                efs = [ef, sd([128, 8, TB], F32, "ef2")]
                mean2s = [mean2, sd([128, TB], F32, "mean2b")]
                var2s = [var2, sd([128, TB], F32, "var2b")]
                rses = [rse, sd([128, TB], F32, "rseb")]

                def blockD(b):
                    xf = xfs[b % 2]
                    mgl = mgls[b % 2]
                    pb16 = pb16s[b % 2]
                    xlb = xlbs[b % 2]
                    ef = efs[b % 2]
                    mean2 = mean2s[b % 2]
                    var2 = var2s[b % 2]
                    rse = rses[b % 2]
                    load_x(l, b, xf=xf, cast=False)
                    c0 = b * TB
                    for t in range(8):
                        s.dma(REC("dma_start", out=mgl[:, t, :], in_=mgd[t * 128:(t + 1) * 128, c0:c0 + TB]), r=["mgd_%d_%d" % (b, t)], w=["mgl%d" % t])
                    for t in range(2):
                        s.dma(REC("dma_start", out=pb16[:, t, :], in_=pT[l, t * 128:(t + 1) * 128, c0:c0 + TB]), w=["pb16_%d" % t], q="pool")
                    yield
                    pmean, kmean = nacc()
                    pmsq, kmsq = nacc()
                    for m in range(8):
                        pz, kz = nps()
                        mm(pz, kz, [(wo[:, k, m * 128:(m + 1) * 128], mgl[:, k, :], [wkey("wo", m * 128), "mgl%d" % k]) for k in range(8)])
                        s.dve(REC("scalar_tensor_tensor", out=xf[:, m, :], in0=xf[:, m, :], scalar=float(alpha), in1=pz, op0=ALU.mult, op1=ALU.add), r=list(kz) + ["xf%d" % m], w=["xf%d" % m])
                        i = m % 2
                        s.act(REC("activation", out=zb[i][:], in_=xf[:, m, :], func=AF.Copy), r=["xf%d" % m], w=["zb%d" % i])
                        s.act(REC("activation", out=zq[i][:], in_=xf[:, m, :], func=AF.Square), r=["xf%d" % m], w=["zq%d" % i])
                        s.pe(REC("matmul", pmean, lhsT=ones_b[:], rhs=zb[i][:], start=(m == 0), stop=(m == 7)), r=["ones", "zb%d" % i], w=list(kmean))
                        s.pe(REC("matmul", pmsq, lhsT=ones_b[:], rhs=zq[i][:], start=(m == 0), stop=(m == 7)), r=["ones", "zq%d" % i], w=list(kmsq))
                        yield
                    s.act(REC("activation", out=mean2[:], in_=pmean, func=AF.Copy, scale=1.0 / D), r=list(kmean), w=["mean2"])
                    s.dve(REC("tensor_tensor", out=var2[:], in0=mean2[:], in1=mean2[:], op=ALU.mult), r=["mean2"], w=["var2"])
                    s.dve(REC("scalar_tensor_tensor", out=var2[:], in0=pmsq, scalar=1.0 / D, in1=var2[:], op0=ALU.mult, op1=ALU.subtract), r=list(kmsq) + ["var2"], w=["var2"])
                    rsqrt(var2[:], var2[:], 1.0, LN_EPS, ["var2"], ["var2"])
                    yield
                    for m in range(8):
                        f = s.dve if m % 2 == 0 else s.pool
                        f(REC("tensor_tensor", out=xf[:, m, :], in0=xf[:, m, :], in1=mean2[:], op=ALU.subtract), r=["xf%d" % m, "mean2"], w=["xf%d" % m])
                        f(REC("tensor_tensor", out=xf[:, m, :], in0=xf[:, m, :], in1=var2[:], op=ALU.mult), r=["xf%d" % m, "var2"], w=["xf%d" % m])
                        s.act(REC("activation", out=xf[:, m, :], in_=xf[:, m, :], func=AF.Identity, scale=V(V_LNG + m), bias=V(V_LNB + m)), r=["xf%d" % m, "vec"], w=["xf%d" % m])
                        s.act(REC("activation", out=xlb[:, m, :], in_=xf[:, m, :], func=AF.Copy), r=["xf%d" % m], w=["xlb%d" % m])
                        yield
                    pms, kms = nacc()
                    for m in range(8):
                        i = m % 2
                        pgt, kgt = nps()
                        mm(pgt, kgt, [(wpg[:, k, m * 128:(m + 1) * 128], xlb[:, k, :], [wkey("wpg"), "xlb%d" % k]) for k in range(8)])
                        s.act(REC("activation", out=sge[i][:], in_=pgt, func=AF.Sigmoid), r=list(kgt), w=["sge%d" % i])
                        ppe, kpe = nps()
                        mm(ppe, kpe, [(wpl[:, k, m * 128:(m + 1) * 128], pb16[:, k, :], [wkey("wpl"), "pb16_%d" % k]) for k in range(2)])
                        s.dve(REC("tensor_tensor", out=ef[:, m, :], in0=ppe, in1=sge[i][:], op=ALU.mult), r=list(kpe) + ["sge%d" % i], w=["ef%d" % m])
                        s.act(REC("activation", out=zq[i][:], in_=ef[:, m, :], func=AF.Square), r=["ef%d" % m], w=["zq%d" % i])
                        s.pe(REC("matmul", pms, lhsT=ones_b[:], rhs=zq[i][:], start=(m == 0), stop=(m == 7)), r=["ones", "zq%d" % i], w=list(kms))
                        yield
                    rsqrt(rse[:], pms, 1.0 / D, RMS_EPS, list(kms), ["rse"])
                    yield
                    for m in range(8):
                        i = m % 2
                        f = s.dve if m % 2 == 0 else s.pool
                        s.dve(REC("scalar_tensor_tensor", out=ef[:, m, :], in0=ef[:, m, :], scalar=V(V_PLEG + m), in1=rse[:], op0=ALU.mult, op1=ALU.mult), r=["ef%d" % m, "rse", "vec"], w=["ef%d" % m])
                        f(REC("tensor_tensor", out=xo[i][:], in0=ef[:, m, :], in1=xf[:, m, :], op=ALU.add), r=["ef%d" % m, "xf%d" % m], w=["xo%d" % i])
                        s.dma(REC("dma_start", out=dst[m * 128:(m + 1) * 128, c0:c0 + TB], in_=xo[i][:]), r=["xo%d" % i], w=["xs_%d" % b], final=(l == L - 1))
                        yield
                run_pipe(blockD, NB, int(os.environ.get('KLAG_D', '11')), ("xf", "mgl", "pb", "xlb", "ef", "mean", "var", "rse"))
            s.barrier()
        s.emit(st)
    return nc


def pack_weights(inp, L, S):
    f = np.float32
    W = {}
    w_in = np.asarray(inp["w_in"], f)
    offs = np.cumsum([0, 256, 256, 256, 256, 128, 32, 256, 256, 256, 256, 128, 128, 256])
    seg = {n: (offs[i], offs[i + 1]) for i, n in enumerate(
        ["a_val", "a_gate", "a_z", "c_q", "c_kv", "k_r", "b_z", "u", "c_z", "q", "k", "v", "d_z"])}

    def cols(n):
        a, b = seg[n]
        return w_in[:, :, a:b]
    w1a = np.concatenate([cols("a_val"), cols("a_gate"), cols("a_z"), cols("u"), cols("c_z")], axis=2)
    z = lambda *sh: np.zeros(sh, f)
    kr = cols("k_r")
    krA = z(L, D, 128); krA[:, :, 64:96] = kr
    krB = z(L, D, 128); krB[:, :, 64:80] = kr[:, :, 16:32]; krB[:, :, 80:96] = kr[:, :, 0:16]
    q = cols("q")
    q02 = np.concatenate([q[:, :, 0:64], q[:, :, 128:192]], axis=2)
    q13 = np.concatenate([q[:, :, 64:128], q[:, :, 192:256]], axis=2)
    w1b = np.concatenate([cols("c_q"), cols("c_kv"), krA, krB, cols("b_z"), q02, q13, cols("k"), cols("v"), cols("d_z")], axis=2)
    W["w1a"] = np.ascontiguousarray(w1a)
    W["w1b"] = np.ascontiguousarray(w1b)
    wuq_ = np.asarray(inp["w_uq"], f)
    wq = z(L, 256, 8 * 128)
    for h in range(4):
        wq[:, :, (2 * h) * 128:(2 * h) * 128 + 96] = wuq_[:, :, h * 96:(h + 1) * 96]
        wq[:, :, (2 * h + 1) * 128 + 64:(2 * h + 1) * 128 + 80] = wuq_[:, :, h * 96 + 80:h * 96 + 96]
        wq[:, :, (2 * h + 1) * 128 + 80:(2 * h + 1) * 128 + 96] = wuq_[:, :, h * 96 + 64:h * 96 + 80]
    W["wuq"] = wq
    wukv_ = np.asarray(inp["w_ukv"], f).reshape(L, 128, 4, 128)
    W["wukv"] = np.ascontiguousarray(np.concatenate([wukv_[:, :, :, 0:64].reshape(L, 128, 256), wukv_[:, :, :, 64:128].reshape(L, 128, 256)], axis=2))
    W["wpw2"] = np.asarray(inp["w_pw2"], f)
    W["wglu"] = np.asarray(inp["w_glu"], f)
    vec = z(L, 128, NV)
    cw = np.asarray(inp["conv_w"], f)
    for c in range(2):
        vec[:, :, V_CONVW + c * 31:V_CONVW + (c + 1) * 31] = cw[:, :, c * 128:(c + 1) * 128].transpose(0, 2, 1)

    def pv(name, col, nt):
        a = np.asarray(inp[name], f).reshape(L, nt, 128)
        vec[:, :, col:col + nt] = a.transpose(0, 2, 1)
    pv("conv_b", V_CONVB, 2); pv("conv_norm_g", V_CONVG, 2); pv("conv_norm_b", V_CONVBETA, 2)
    pv("mla_q_norm_g", V_QNG, 2); pv("mla_kv_norm_g", V_KVNG, 1); pv("ssm_d", V_SSMD, 2)
    pv("b_merge", V_BMERGE, 32); pv("ln_g", V_LNG, 8); pv("ln_b", V_LNB, 8); pv("ple_norm_g", V_PLEG, 8)

    def sm(a):
        return a.reshape(L, 8, 2, 64).transpose(0, 2, 3, 1).reshape(L, 128, 8)
    vec[:, :, V_LR:V_LR + 8] = sm(np.asarray(inp["ssm_a_re"], f))
    vec[:, :, V_LI:V_LI + 8] = sm(np.asarray(inp["ssm_a_im"], f))
    vec[:, :, V_LOGDT:V_LOGDT + 8] = sm(np.repeat(np.asarray(inp["ssm_log_dt"], f)[:, :, None], 64, axis=2))
    vec[:, :, V_SINK:V_SINK + 4] = np.asarray(inp["attn_sinks"], f)[:, None, :]
    W["vecs"] = vec

    def smB(a):
        return a.reshape(L, 8, 2, 64, 16).transpose(0, 2, 3, 1, 4).reshape(L, 128, 128)

    def smC(a):
        return a.reshape(L, 8, 2, 16, 64).transpose(0, 2, 4, 1, 3).reshape(L, 128, 128)
    W["s5b"] = np.ascontiguousarray(np.concatenate([smB(np.asarray(inp["ssm_b_re"], f)), smB(np.asarray(inp["ssm_b_im"], f))], axis=2))
    W["s5c"] = np.ascontiguousarray(np.concatenate([smC(np.asarray(inp["ssm_c_re"], f)), smC(np.asarray(inp["ssm_c_im"], f))], axis=2))
    W["wmerge"] = np.asarray(inp["w_merge"], f)
    W["wbranch"] = np.asarray(inp["w_branch"], f).reshape(L, 1024, D)
    W["wout"] = np.asarray(inp["w_out"], f)
    W["wple"] = np.asarray(inp["w_ple"], f)
    W["wpleg"] = np.asarray(inp["w_ple_gate"], f)
    pos = np.arange(S, dtype=np.float64)
    inv = 10000.0 ** (-np.arange(0, 32, 2, dtype=np.float64) / 32)
    rc = np.zeros((128, S), f); rs = np.zeros((128, S), f)
    for p in range(128):
        i = p % 32
        ang = pos * inv[i % 16]
        ang = (pos.astype(f) * inv.astype(f)[i % 16]).astype(np.float64)
        rc[p] = np.cos(ang)
        rs[p] = (-1.0 if i < 16 else 1.0) * np.sin(ang)
    W["ropec"] = rc; W["ropes"] = rs
    cmv = np.zeros((128, NCM), f)
    cmv[:, C_IDENT:C_IDENT + 128] = np.eye(128, dtype=f)
    k = np.arange(128)[:, None]; qq = np.arange(128)[None, :]
    cmv[:, C_TRI:C_TRI + 128] = (k <= qq)
    cmv[:, C_SWM:C_SWM + 128] = (k > qq)
    cmv[:, C_SWM + 128:C_SWM + 256] = (k <= qq)
    cmv[:, C_IOTA:C_IOTA + 512] = np.arange(1, 513, dtype=f)[None, :]
    W["cmat"] = cmv
    return W


_NC_CACHE = {}


def run(inputs, S, L, ncores, debug=False):
    x = np.asarray(inputs["x"], np.float32)
    p = np.asarray(inputs["p"], np.float32)
    W = pack_weights(inputs, L, S)
    key = (S, L, debug)
    if key not in _NC_CACHE:
        _NC_CACHE[key] = build(S, L, debug)
    nc = _NC_CACHE[key]
    in_maps = []
    for c in range(ncores):
        m = dict(W)
        m["xT"] = np.ascontiguousarray(x[c].T)
        m["pT"] = np.ascontiguousarray(p[:, c].transpose(0, 2, 1))
        in_maps.append(m)
    res = run_bass_kernel_spmd(nc, in_maps, core_ids=list(range(ncores)))
    out = np.stack([np.ascontiguousarray(r["yT"].T) for r in res.results], axis=0)
    if debug:
        return out.astype(np.float32), [dict(r) for r in res.results]
    return out.astype(np.float32)


def kernel(**inputs):
    return run(inputs, 4096, 4, 8)
```

```python
import math
import os
from contextlib import ExitStack
import numpy as np
import concourse.bass as bass
import concourse.mybir as mybir
from concourse.bass_utils import run_bass_kernel_spmd

F32 = mybir.dt.float32
BF16 = mybir.dt.bfloat16
I32 = mybir.dt.int32
ALU = mybir.AluOpType
AF = mybir.ActivationFunctionType

ENGS = ("pe", "dve", "act", "pool", "sp")
N_DMA_SEMS = 24

D = 1024
TB = 512
LN_EPS = 1e-5
RMS_EPS = 1e-6
TWO_PI = 2.0 * math.pi


def REC(name, *args, **kwargs):
    def fn(e):
        return getattr(e, name)(*args, **kwargs)
    return fn


class Sched:
    def __init__(self, nc):
        self.nc = nc
        self.ops = []
        self.last_w = {}
        self.readers = {}

    par = 0
    dbl = ()

    def _k(self, t):
        if self.dbl and isinstance(t, str) and t.rstrip("0123456789_") in self.dbl:
            return t + "#%d" % self.par
        return t

    def op(self, eng, fn, reads=(), writes=(), dma=False, final=False):
        reads = [self._k(t) for t in reads]
        writes = [self._k(t) for t in writes]
        idx = len(self.ops)
        deps = set()
        for t in reads:
            w = self.last_w.get(t)
            if w is not None:
                deps.add((w, "raw"))
        for t in writes:
            w = self.last_w.get(t)
            if w is not None:
                deps.add((w, "waw"))
            for r in self.readers.get(t, ()):
                deps.add((r, "war"))
        for t in writes:
            self.last_w[t] = idx
            self.readers[t] = []
        for t in reads:
            self.readers.setdefault(t, []).append(idx)
        self.ops.append(dict(eng=eng, fn=fn, deps=deps, dma=dma, final=final))
        return idx

    def pe(self, fn, r=(), w=()):
        return self.op("pe", fn, r, w)

    def dve(self, fn, r=(), w=()):
        return self.op("dve", fn, r, w)

    def act(self, fn, r=(), w=()):
        return self.op("act", fn, r, w)

    def pool(self, fn, r=(), w=()):
        return self.op("pool", fn, r, w)

    def dma(self, fn, r=(), w=(), q="sp", final=False):
        return self.op(q, fn, r, w, dma=True, final=final)

    def barrier(self):
        self.ops.append(dict(eng=None, fn=None, deps=set(), dma=False, final=False, barrier=True))
        self.last_w.clear()
        self.readers.clear()

    def emit(self, stack):
        nc = self.nc
        ops = self.ops
        n = len(ops)
        CE = ("pe", "dve", "act", "pool")
        seen = {e: {f: -1 for f in ENGS} for e in ENGS}
        seen_dma = {e: set() for e in ENGS}
        need = [[] for _ in range(n)]
        is_prod = [False] * n
        last_c = {e: -1 for e in CE}
        bar_need = {}
        for i, o in enumerate(ops):
            if o.get("barrier"):
                for e in ENGS:
                    lst = []
                    for e2 in CE:
                        j = last_c[e2]
                        if j >= 0 and e2 != e and j > seen[e][e2]:
                            lst.append(j)
                            is_prod[j] = True
                            seen[e][e2] = j
                    bar_need[(i, e)] = lst
                continue
            e = o["eng"]
            if not o["dma"]:
                last_c[e] = i
            best = {}
            for (d, kind) in o["deps"]:
                po = ops[d]
                pe_ = po["eng"]
                if po["dma"]:
                    if d in seen_dma[e]:
                        continue
                    best[("dma", d)] = d
                    continue
                if pe_ == e and not o["dma"]:
                    if kind != "raw" or e == "pe":
                        continue
                if d <= seen[e][pe_]:
                    continue
                k = ("c", pe_)
                if k not in best or best[k] < d:
                    best[k] = d
            for k, d in best.items():
                need[i].append(d)
                is_prod[d] = True
                if k[0] == "dma":
                    seen_dma[e].add(d)
                else:
                    seen[e][k[1]] = d
        csem = {e: stack.enter_context(nc.semaphore("c_" + e)) for e in ENGS}
        dsem = [stack.enter_context(nc.semaphore("d%d" % k)) for k in range(N_DMA_SEMS)]
        bsem = stack.enter_context(nc.semaphore("bar"))
        cnt = {e: 0 for e in ENGS}
        dcount = [0] * N_DMA_SEMS
        ndma = 0
        waitval = [None] * n
        dma_slot_prev = {}
        extra_wait = [None] * n
        slot_last = {}
        bar_idx = {}
        nbar = 0
        for i, o in enumerate(ops):
            if o.get("barrier"):
                slot_last[i] = dict(dma_slot_prev)
                nbar += 1
                bar_idx[i] = nbar
                continue
            if o["dma"]:
                slot = ndma % N_DMA_SEMS
                ndma += 1
                dcount[slot] += 16
                waitval[i] = (dsem[slot], dcount[slot])
                if slot in dma_slot_prev:
                    extra_wait[i] = dma_slot_prev[slot]
                dma_slot_prev[slot] = i
                o["dsem"] = dsem[slot]
            elif is_prod[i]:
                cnt[o["eng"]] += 1
                waitval[i] = (csem[o["eng"]], cnt[o["eng"]])
        per_eng = {e: [i for i, o in enumerate(ops) if o["eng"] == e or o.get("barrier")] for e in ENGS}
        block = stack.enter_context(nc.Block())
        self.n_waits = 0
        self.n_ins = {e: len(per_eng[e]) for e in ENGS}

        def run(engobj, ename):
            for i in per_eng[ename]:
                o = ops[i]
                if o.get("barrier"):
                    for d in bar_need[(i, ename)]:
                        s_, v = waitval[d]
                        engobj.wait_ge(s_, v)
                    if ename == "sp":
                        for slot, d in slot_last[i].items():
                            s_, v = waitval[d]
                            engobj.wait_ge(s_, v)
                        engobj.dma_start(out=self.bar_dst, in_=self.bar_src).then_inc(bsem, 16)
                    else:
                        engobj.wait_ge(bsem, 16 * bar_idx[i])
                    continue
                ws = list(need[i])
                if extra_wait[i] is not None:
                    ws.append(extra_wait[i])
                for d in ws:
                    s_, v = waitval[d]
                    engobj.wait_ge(s_, v)
                    self.n_waits += 1
                ins = o["fn"](engobj)
                if o["dma"]:
                    ins.then_inc(o["dsem"], 16)
                elif is_prod[i]:
                    ins.then_inc(csem[ename], 1)
            for i in per_eng[ename]:
                if ops[i]["dma"] and ops[i]["final"] and ops[i]["eng"] == ename:
                    s_, v = waitval[i]
                    engobj.wait_ge(s_, v)

        @block.tensor
        def _(e):
            run(e, "pe")

        @block.vector
        def _(e):
            run(e, "dve")

        @block.scalar
        def _(e):
            run(e, "act")

        @block.gpsimd
        def _(e):
            run(e, "pool")

        @block.sync
        def _(e):
            run(e, "sp")


V_CONVW = 0
V_CONVB = 62
V_CONVG = 64
V_CONVBETA = 66
V_QNG = 68
V_KVNG = 70
V_SSMD = 71
V_BMERGE = 73
V_LNG = 105
V_LNB = 113
V_PLEG = 121
V_LR = 129
V_LI = 137
V_LOGDT = 145
V_SINK = 153
NV = 160

C_IDENT = 0
C_TRI = 128
C_SWM = 256
C_IOTA = 512
NCM = 1024


def build(S, L, debug=False):
    NB = S // TB
    NT = S // 128
    nc = bass.Bass("TRN2", target_bir_lowering=False)

    def din(name, shape):
        return nc.dram_tensor(name, list(shape), F32, kind="ExternalInput").ap()

    xT = din("xT", [D, S])
    pT = din("pT", [L, 256, S])
    w1a = din("w1a", [L, D, 10 * 128])
    w1b = din("w1b", [L, D, 13 * 128])
    wuq = din("wuq", [L, 256, 8 * 128])
    wukv = din("wukv", [L, 128, 512])
    wpw2 = din("wpw2", [L, 256, 256])
    wglu = din("wglu", [L, 256, 512])
    vecs = din("vecs", [L, 128, NV])
    s5b = din("s5b", [L, 128, 2 * 8 * 16])
    s5c = din("s5c", [L, 128, 2 * 8 * 16])
    wmerge = din("wmerge", [L, D, 4 * D])
    wbranch = din("wbranch", [L, 4 * 256, D])
    wout = din("wout", [L, D, D])
    wple = din("wple", [L, 256, D])
    wpleg = din("wpleg", [L, D, D])
    ropec = din("ropec", [128, S])
    ropes = din("ropes", [128, S])
    cmat = din("cmat", [128, NCM])
    yT = nc.dram_tensor("yT", [D, S], F32, kind="ExternalOutput").ap()
    ysd = nc.dram_tensor("ysd", [D, S], BF16, kind=("ExternalOutput" if debug else "Internal")).ap()
    mgd = nc.dram_tensor("mgd", [D, S], BF16, kind=("ExternalOutput" if debug else "Internal")).ap()
    xs = nc.dram_tensor("xs", [D, S], F32, kind="Internal").ap()
    rotd = nc.dram_tensor("rotd", [128, 16 * TB], F32, kind="Internal").ap()

    st = ExitStack()
    with st:
        s = Sched(nc)
        uid = [0]

        def sb(shape, dt=F32, name=None):
            uid[0] += 1
            return st.enter_context(nc.sbuf_tensor("%s_%d" % (name or "t", uid[0]), list(shape), dt))

        psum = st.enter_context(nc.psum_tensor("psum", [128, 8 * 512], F32))
        psi = [0]

        def nps(nbanks=1):
            NR = NROT[0]
            b = psi[0] % NR
            if nbanks == 2:
                while b % 2 == 1 or b + 2 > NR:
                    psi[0] += 1
                    b = psi[0] % NR
            psi[0] += nbanks
            key = tuple("ps%d" % (b + i) for i in range(nbanks))
            return psum[:, b * 512:(b + nbanks) * 512], key

        acci = [0]
        NROT = [5]
        s5i = [0]

        def ns5():
            b = 4 + 2 * (s5i[0] % 2)
            s5i[0] += 1
            return (psum[:, b * 512:(b + 1) * 512], ("ps%d" % b,)), (psum[:, (b + 1) * 512:(b + 2) * 512], ("ps%d" % (b + 1),))

        def nacc():
            b = 5 + acci[0] % 3
            acci[0] += 1
            return psum[:, b * 512:(b + 1) * 512], ("ps%d" % b,)

        cm = sb([128, NCM], F32, "cm")
        s.dma(REC("dma_start", out=cm[:], in_=cmat), w=["cm"])
        cmb = sb([128, 512], BF16, "cmb")
        s.dve(REC("tensor_copy", out=cmb[:], in_=cm[:, 0:512]), r=["cm"], w=["cmb"])
        ident_f = cm[:, C_IDENT:C_IDENT + 128]
        tri_b = cmb[:, C_TRI:C_TRI + 128]
        swm_b = cmb[:, C_SWM:C_SWM + 256]
        iota_f = cm[:, C_IOTA:C_IOTA + 512]
        bard = nc.dram_tensor("bard", [1, 16], F32, kind="Internal").ap()
        s.bar_dst = bard
        s.bar_src = cm[0:1, 0:16]
        ones_b = sb([128, 128], BF16, "ones")
        s.dve(REC("memset", ones_b[:], 1.0), w=["ones"])
        vec = sb([128, NV], F32, "vec")

        def V(c, n=1):
            return vec[:, c:c + n]

        def mm(ps_ap, pskey, pairs, extra_r=()):
            n = len(pairs)
            for i, (l, r, keys) in enumerate(pairs):
                M_ = l.shape[1]
                o_ap = ps_ap if M_ == 128 else ps_ap[0:M_, :]
                s.pe(REC("matmul", o_ap, lhsT=l, rhs=r, start=(i == 0), stop=(i == n - 1)),
                     r=list(keys) + list(extra_r), w=list(pskey))

        def rsqrt(out, in_, scale, eps, rk, wk, eng="dve"):
            s.dve(REC("tensor_scalar", out=out, in0=in_, scalar1=float(scale), scalar2=float(eps), op0=ALU.mult, op1=ALU.add), r=rk, w=wk)
            s.act(REC("activation", out=out, in_=out, func=AF.Sqrt), r=wk, w=wk)
            s.dve(REC("reciprocal", out=out, in_=out), r=wk, w=wk)

        WSL = {}

        def load_w(tile, dram2d, K, key, bounds=None, order=None, by_k=False):
            nk = max(1, K // 128)
            N = tile.shape[2]
            if by_k:
                WSL[key] = ["k"]
                for kt in range(nk):
                    s.dma(REC("dma_start", out=tile[:, kt, :], in_=dram2d[kt * 128:(kt + 1) * 128, :]), w=["%s@k%d" % (key, kt)], q="pool")
                return
            bounds = list(bounds or [0, N])
            WSL[key] = bounds
            for si in (order if order is not None else range(len(bounds) - 1)):
                c0, c1 = bounds[si], bounds[si + 1]
                for kt in range(nk):
                    s.dma(REC("dma_start", out=tile[:, kt, c0:c1], in_=dram2d[kt * 128:(kt + 1) * 128, c0:c1]), w=["%s@%d" % (key, si)], q="pool")

        def wkey(key, col=0, kt=None):
            bd = WSL[key]
            if bd[0] == "k":
                return "%s@k%d" % (key, kt)
            si = 0
            while si + 1 < len(bd) - 1 and col >= bd[si + 1]:
                si += 1
            return "%s@%d" % (key, si)

        xf = sb([128, 8, TB], F32, "xf")
        xb = sb([128, 8, TB], BF16, "xb")

        def load_x(l, b, xb=xb, xf=xf, cast=True):
            src = xT if l == 0 else xs
            for kt in range(8):
                s.dma(REC("dma_start", out=xf[:, kt, :], in_=src[kt * 128:(kt + 1) * 128, b * TB:(b + 1) * TB]),
                      w=["xf%d" % kt], r=(["xs_%d" % b] if l > 0 else []))
            if not cast:
                return
            for kt in range(8):
                if kt % 2 == 0:
                    s.act(REC("activation", out=xb[:, kt, :], in_=xf[:, kt, :], func=AF.Copy), r=["xf%d" % kt], w=["xb%d" % kt])
                else:
                    s.pool(REC("tensor_copy", out=xb[:, kt, :], in_=xf[:, kt, :]), r=["xf%d" % kt], w=["xb%d" % kt])

        XB = ["xb%d" % k for k in range(8)]

        def run_pipe(gen_fn, nblocks, lag, dbl):
            s.dbl = tuple(dbl)
            active = []
            nxt = 0
            while nxt < nblocks or active:
                if nxt < nblocks and len(active) < 2 and (not active or active[-1]["n"] >= lag):
                    active.append(dict(g=gen_fn(nxt), b=nxt, n=0, done=set(), blocked=None))
                    nxt += 1
                for a in list(active):
                    older = active[0] if (a is not active[0]) else None
                    if a["blocked"] is not None:
                        if older is None or a["blocked"] in older["done"]:
                            a["blocked"] = None
                        else:
                            continue
                    s.par = a["b"] % 2
                    try:
                        r = next(a["g"])
                        a["n"] += 1
                        if isinstance(r, tuple):
                            if r[0] == "done":
                                a["done"].add(r[1])
                            elif r[0] == "wait" and older is not None and r[1] not in older["done"]:
                                a["blocked"] = r[1]
                    except StopIteration:
                        active.remove(a)
            s.par = 0
            s.dbl = ()

        for l in range(L):
            s.dma(REC("dma_start", out=vec[:], in_=vecs[l]), w=["vec"])
            scA = ExitStack()
            SW = os.environ.get('KSWEEPS', 'A,B,C1,C2').split(',')
            with scA:
              if 'A' in SW:
                def sa(shape, dt=F32, name=None):
                    uid[0] += 1
                    return scA.enter_context(nc.sbuf_tensor("%s_%d" % (name or "a", uid[0]), list(shape), dt))
                scP = ExitStack()

                def sp2(shape, dt=F32, name=None):
                    uid[0] += 1
                    return scP.enter_context(nc.sbuf_tensor("%s_%d" % (name or "p", uid[0]), list(shape), dt))
                wA = sa([128, 8, 1280], BF16, "wA")
                if 'lw' not in os.environ.get('KA_SKIP', ''):
                    load_w(wA, w1a[l], D, "wA", bounds=[0, 512, 768, 1280], order=[1, 2, 0])
                wp2 = sa([128, 2, 256], BF16, "wp2")
                if 'lw' not in os.environ.get('KA_SKIP', ''):
                    load_w(wp2, wpw2[l], 256, "wp2")
                wgl = sa([128, 2, 512], BF16, "wgl")
                if 'lw' not in os.environ.get('KA_SKIP', ''):
                    load_w(wgl, wglu[l], 256, "wgl")
                dwt = sa([128, 62, 128], BF16, "dwt")
                for cj in (range(62) if 'dwt' not in os.environ.get('KA_SKIP', '') else []):
                    f = s.dve
                    f(REC("tensor_scalar", out=dwt[:, cj, :], in0=ident_f, scalar1=V(V_CONVW + cj), scalar2=None, op0=ALU.mult),
                      r=["cm", "vec"], w=["dwt"])
                sp_ = sa([128, 128], F32, "s5p")

                def P(i, n=8):
                    return sp_[:, i * 8:i * 8 + n]
                DT, MAG, TH, TI_F, THR, T2, LBR, LBI, DEN, FR, FI, NFI, TMP = range(13)
                spi = sa([128, 8], I32, "s5pi")
                CB = sa([128, 16, 128], BF16, "CB")
                BB = sa([128, 16, 128], BF16, "BB")
                K5 = ["s5p"]
                if 'prep' not in os.environ.get('KA_SKIP', ''):
                    s.act(REC("activation", out=P(DT), in_=V(V_LOGDT, 8), func=AF.Exp), r=["vec"], w=K5)
                    s.dve(REC("tensor_tensor", out=P(MAG), in0=V(V_LR, 8), in1=P(DT), op=ALU.mult), r=K5 + ["vec"], w=K5)
                    s.act(REC("activation", out=P(MAG), in_=P(MAG), func=AF.Exp), r=K5, w=K5)
                    s.dve(REC("scalar_tensor_tensor", out=P(TH), in0=V(V_LI, 8), scalar=float(1.0 / TWO_PI), in1=P(DT), op0=ALU.mult, op1=ALU.mult), r=K5 + ["vec"], w=K5)
                    s.dve(REC("tensor_copy", out=spi[:], in_=P(TH)), r=K5, w=["s5pi"])
                    s.dve(REC("tensor_copy", out=P(TI_F), in_=spi[:]), r=["s5pi"], w=K5)
                    s.dve(REC("tensor_tensor", out=P(THR), in0=P(TH), in1=P(TI_F), op=ALU.subtract), r=K5, w=K5)
                    s.act(REC("activation", out=P(LBI), in_=P(THR), func=AF.Sin, scale=6.28318), r=K5, w=K5)
                    s.dve(REC("tensor_scalar", out=P(T2), in0=P(THR), scalar1=0.25, scalar2=None, op0=ALU.add), r=K5, w=K5)
                    s.dve(REC("tensor_copy", out=spi[:], in_=P(T2)), r=K5, w=["s5pi"])
                    s.dve(REC("tensor_copy", out=P(TI_F), in_=spi[:]), r=["s5pi"], w=K5)
                    s.dve(REC("tensor_tensor", out=P(T2), in0=P(T2), in1=P(TI_F), op=ALU.subtract), r=K5, w=K5)
                    s.act(REC("activation", out=P(LBR), in_=P(T2), func=AF.Sin, scale=6.28318), r=K5, w=K5)
                    s.dve(REC("tensor_tensor", out=P(LBR), in0=P(LBR), in1=P(MAG), op=ALU.mult), r=K5, w=K5)
                    s.dve(REC("tensor_tensor", out=P(LBI), in0=P(LBI), in1=P(MAG), op=ALU.mult), r=K5, w=K5)
                    s.dve(REC("tensor_tensor", out=P(DEN), in0=V(V_LR, 8), in1=V(V_LR, 8), op=ALU.mult), r=["vec"] + K5, w=K5)
                    s.dve(REC("tensor_tensor", out=P(TMP), in0=V(V_LI, 8), in1=V(V_LI, 8), op=ALU.mult), r=["vec"] + K5, w=K5)
                    s.dve(REC("tensor_tensor", out=P(DEN), in0=P(DEN), in1=P(TMP), op=ALU.add), r=K5, w=K5)
                    s.dve(REC("reciprocal", out=P(DEN), in_=P(DEN)), r=K5, w=K5)
                    s.dve(REC("tensor_scalar", out=P(T2), in0=P(LBR), scalar1=-1.0, scalar2=None, op0=ALU.add), r=K5, w=K5)
                    s.dve(REC("tensor_tensor", out=P(FR), in0=P(T2), in1=V(V_LR, 8), op=ALU.mult), r=K5 + ["vec"], w=K5)
                    s.dve(REC("tensor_tensor", out=P(TMP), in0=P(LBI), in1=V(V_LI, 8), op=ALU.mult), r=K5 + ["vec"], w=K5)
                    s.dve(REC("tensor_tensor", out=P(FR), in0=P(FR), in1=P(TMP), op=ALU.add), r=K5, w=K5)
                    s.dve(REC("tensor_tensor", out=P(FR), in0=P(FR), in1=P(DEN), op=ALU.mult), r=K5, w=K5)
                    s.dve(REC("tensor_tensor", out=P(FI), in0=P(LBI), in1=V(V_LR, 8), op=ALU.mult), r=K5 + ["vec"], w=K5)
                    s.dve(REC("tensor_tensor", out=P(TMP), in0=P(T2), in1=V(V_LI, 8), op=ALU.mult), r=K5 + ["vec"], w=K5)
                    s.dve(REC("tensor_tensor", out=P(FI), in0=P(FI), in1=P(TMP), op=ALU.subtract), r=K5, w=K5)
                    s.dve(REC("tensor_tensor", out=P(FI), in0=P(FI), in1=P(DEN), op=ALU.mult), r=K5, w=K5)
                    s.dve(REC("tensor_scalar", out=P(NFI), in0=P(FI), scalar1=-1.0, scalar2=None, op0=ALU.mult), r=K5, w=K5)
                bst = sp2([128, 256], F32, "bst")
                cst = sp2([128, 256], F32, "cst")
                s.dma(REC("dma_start", out=bst[:], in_=s5b[l]), w=["bst"])
                s.dma(REC("dma_start", out=cst[:], in_=s5c[l]), w=["cst"])
                bb = sp2([128, 256], F32, "bb")
                tmpb = sp2([128, 128], F32, "tmpb")
                b3 = lambda t, ri: t[:, ri * 128:(ri + 1) * 128].rearrange("p (a h) -> p a h", h=16)
                fr_b = P(FR).unsqueeze(2).broadcast_to([128, 8, 16])
                fi_b = P(FI).unsqueeze(2).broadcast_to([128, 8, 16])
                nfi_b = P(NFI).unsqueeze(2).broadcast_to([128, 8, 16])
                t3 = tmpb[:, :].rearrange("p (a h) -> p a h", h=16)
                if 'bbc' not in os.environ.get('KA_SKIP', ''):
                    s.dve(REC("tensor_tensor", out=b3(bb, 0), in0=b3(bst, 0), in1=fr_b, op=ALU.mult), r=["bst"] + K5, w=["bb"])
                    s.dve(REC("tensor_tensor", out=t3, in0=b3(bst, 1), in1=nfi_b, op=ALU.mult), r=["bst"] + K5, w=["tmpb"])
                    s.dve(REC("tensor_tensor", out=b3(bb, 0), in0=b3(bb, 0), in1=t3, op=ALU.add), r=["bb", "tmpb"], w=["bb"])
                    s.dve(REC("tensor_tensor", out=b3(bb, 1), in0=b3(bst, 1), in1=fr_b, op=ALU.mult), r=["bst"] + K5, w=["bb"])
                    s.dve(REC("tensor_tensor", out=t3, in0=b3(bst, 0), in1=fi_b, op=ALU.mult), r=["bst", "bb"] + K5, w=["tmpb"])
                    s.dve(REC("tensor_tensor", out=b3(bb, 1), in0=b3(bb, 1), in1=t3, op=ALU.add), r=["bb", "tmpb"], w=["bb"])
                BT = sp2([128, 16, 128], F32, "BT")
                if 'ms' not in os.environ.get('KA_SKIP', ''):
                    s.pool(REC("memset", BT[:], 0.0), w=["BT"])
                    s.pool(REC("memset", CB[:], 0.0), w=["CB"])
                if 'scat' not in os.environ.get('KA_SKIP', ''):
                    for ri in range(2):
                        for half in range(2):
                            for grp in range(2):
                                prt = slice(half * 64, half * 64 + 64)
                                dst = BT[prt, ri * 8 + grp * 4: ri * 8 + grp * 4 + 4, :]
                                src = bb[prt, ri * 128 + grp * 64: ri * 128 + grp * 64 + 64].rearrange("p (a h) -> p a h", h=16)
                                for a in range(4):
                                    s.dve(REC("tensor_copy",
                                        out=BT[prt, ri * 8 + grp * 4 + a, a * 32 + half * 16: a * 32 + half * 16 + 16],
                                        in_=bb[prt, ri * 128 + (grp * 4 + a) * 16: ri * 128 + (grp * 4 + a) * 16 + 16]), r=["bb", "BT"], w=["BT"])
                                    if ri == 0:
                                        s.pool(REC("tensor_copy",
                                            out=CB[prt, grp * 4 + a, a * 32 + half * 16: a * 32 + half * 16 + 16],
                                            in_=cst[prt, (grp * 4 + a) * 16:(grp * 4 + a) * 16 + 16]), r=["cst", "CB"], w=["CB"])
                                    else:
                                        s.dve(REC("tensor_scalar",
                                            out=CB[prt, 8 + grp * 4 + a, a * 32 + half * 16: a * 32 + half * 16 + 16],
                                            in0=cst[prt, 128 + (grp * 4 + a) * 16:128 + (grp * 4 + a) * 16 + 16], scalar1=-1.0, scalar2=None, op0=ALU.mult), r=["cst", "CB"], w=["CB"])
                if 'tr' not in os.environ.get('KA_SKIP', ''):
                    for t in range(16):
                        pt, pk = nps()
                        s.pe(REC("transpose", pt[:, 0:128], BT[:, t, :], ident_f), r=["BT", "cm"], w=list(pk))
                        s.act(REC("activation", out=BB[:, t, :], in_=pt[:, 0:128], func=AF.Copy), r=list(pk), w=["BB"])
                rt = sp2([128, TB], F32, "rt")
                rti = sp2([128, TB], I32, "rti")
                rtf = sp2([128, TB], F32, "rtf")
                rto = [sp2([128, TB], F32, "rto%d" % i) for i in range(2)]
                if 'rot' not in os.environ.get('KA_SKIP', ''):
                    for p in range(8):
                        for cs in range(2):
                            ko = "rto%d" % cs
                            s.dve(REC("tensor_scalar", out=rt[:], in0=iota_f, scalar1=sp_[:, THR * 8 + p:THR * 8 + p + 1], scalar2=(0.25 if cs == 0 else 0.0), op0=ALU.mult, op1=ALU.add),
                                  r=["cm"] + K5, w=["rt"])
                            s.dve(REC("tensor_copy", out=rti[:], in_=rt[:]), r=["rt"], w=["rti"])
                            s.dve(REC("tensor_copy", out=rtf[:], in_=rti[:]), r=["rti"], w=["rtf"])
                            s.dve(REC("tensor_tensor", out=rt[:], in0=rt[:], in1=rtf[:], op=ALU.subtract), r=["rt", "rtf"], w=["rt"])
                            s.act(REC("activation", out=rto[cs][:], in_=rt[:], func=AF.Sin, scale=6.28318), r=["rt"], w=[ko])
                            s.dma(REC("dma_start", out=rotd[:, (p * 2 + cs) * TB:(p * 2 + cs + 1) * TB], in_=rto[cs][:]), r=[ko], w=["rotd%d" % p])
                s.barrier()
                scP.close()
                carry = sa([128, 16], F32, "carry")
                s.dve(REC("memset", carry[:], 0.0), w=["carryR%d" % p for p in range(8)] + ["carryI%d" % p for p in range(8)])
                gbuf = sa([128, 2, 32 + TB], BF16, "gbuf")
                s.pool(REC("memset", gbuf[:], 0.0), w=["gbuf0", "gbuf1"])
                szA = sa([128, 4, TB], BF16, "szA")
                sg = [sa([128, TB], F32, "sg%d" % i) for i in range(2)]
                hcf = sa([128, 2, TB], F32, "hcf")
                hcb = sa([128, 2, TB], BF16, "hcb")
                hsq = sa([128, 2, TB], BF16, "hsq")
                mean_sb = sa([128, TB], F32, "mean_sb")
                var_sb = sa([128, TB], F32, "var_sb")
                hnb = sa([128, 2, TB], BF16, "hnb")
                ysA = sa([128, 4, TB], BF16, "ysA")
                uf = sa([128, 2, TB], F32, "uf")
                ub = sa([128, 2, TB], BF16, "ub")
                rot = [sa([128, 2, TB], F32, "rot%d" % i) for i in range(4)]
                rotc = [0, 0]
                w5 = [[sa([128, TB], F32, "w5_%d_%d" % (i, j)) for j in range(6)] for i in range(2)]
                hb = sa([128, 16, TB], BF16, "hb")
                ygf = sa([128, 2, TB], F32, "ygf")
                ygb = sa([128, 2, TB], BF16, "ygb")
                sgl = sa([128, 2, TB], F32, "sgl")
                tmpc = sa([128, 2, TB], F32, "tmpc")

                xbA = [xb, sa([128, 8, TB], BF16, "xbA2")]
                szAs = [szA, sa([128, 4, TB], BF16, "szA2")]
                ufs = [uf, sa([128, 2, TB], F32, "uf2")]
                ubs = [ub, sa([128, 2, TB], BF16, "ub2")]
                ysAs = [ysA, sa([128, 4, TB], BF16, "ysA2")]

                if l == 0 and os.environ.get('KDBG'):
                    print('SBUF remaining after sweep A allocs', nc.sbuf_bytes_remaining)

                NROT[0] = 4

                def blockA(b):
                    xb = xbA[b % 2]
                    szA = szAs[b % 2]
                    uf = ufs[b % 2]
                    ub = ubs[b % 2]
                    ysA = ysAs[b % 2]
                    load_x(l, b, xb)
                    yield
                    c0 = b * TB
                    def proj(mt, wt=wA):
                        pt, pk = nps()
                        mm(pt, pk, [(wt[:, k, mt * 128:(mt + 1) * 128], xb[:, k, :], [wkey("wA", mt * 128), "xb%d" % k]) for k in range(8)])
                        return pt, pk
                    for c in range(2):
                        pz, kz = proj(4 + c)
                        s.act(REC("activation", out=szA[:, c, :], in_=pz, func=AF.Silu), r=list(kz), w=["szA%d" % c])
                        yield
                        pz2, kz2 = proj(8 + c)
                        s.act(REC("activation", out=szA[:, 2 + c, :], in_=pz2, func=AF.Silu), r=list(kz2), w=["szA%d" % (2 + c)])
                        yield
                        pu, ku = proj(6 + c)
                        if 'ufa' not in os.environ.get('KA_SKIP', ''):
                            s.act(REC("activation", out=uf[:, c, :], in_=pu, func=AF.Copy), r=list(ku), w=["uf%d" % c])
                        if 'ubd' not in os.environ.get('KA_SKIP', ''):
                            s.pool(REC("tensor_copy", out=ub[:, c, :], in_=uf[:, c, :]), r=["uf%d" % c], w=["ub%d" % c])
                        yield
                    yield ("wait", "conv")
                    for c in range(2):
                        pg, kg = proj(2 + c)
                        s.act(REC("activation", out=sg[c][:], in_=pg, func=AF.Sigmoid), r=list(kg), w=["sg%d" % c])
                        pv, kv = proj(c)
                        if 'glu' not in os.environ.get('KA_SKIP', ''):
                            s.dve(REC("tensor_tensor", out=gbuf[:, c, 32:32 + TB], in0=pv, in1=sg[c][:], op=ALU.mult), r=list(kv) + ["sg%d" % c], w=["gbuf%d" % c])
                        yield
                    if 'conv' not in os.environ.get('KA_SKIP', ''):
                        pcs = []
                        for c in range(2):
                            pc, kc = nps()
                            mm(pc, kc, [(dwt[:, c * 31 + j, :], gbuf[:, c, 2 + j:2 + j + TB], ["dwt", "gbuf%d" % c]) for j in range(31)])
                            pcs.append((pc, kc))
                            s.act(REC("activation", out=hcf[:, c, :], in_=pc, func=AF.Identity, bias=V(V_CONVB + c)), r=list(kc) + ["vec"], w=["hcf%d" % c])
                            s.act(REC("activation", out=hsq[:, c, :], in_=pc, func=AF.Square, bias=V(V_CONVB + c)), r=list(kc) + ["vec"], w=["hsq%d" % c])
                            s.pool(REC("tensor_copy", out=hcb[:, c, :], in_=hcf[:, c, :]), r=["hcf%d" % c], w=["hcb%d" % c])
                            s.pool(REC("tensor_copy", out=gbuf[:, c, 0:32], in_=gbuf[:, c, TB:TB + 32]), r=["gbuf%d" % c], w=["gbuf%d" % c])
                            yield
                        pm, km = nps()
                        mm(pm, km, [(ones_b[:], hcb[:, c, :], ["ones", "hcb%d" % c]) for c in range(2)])
                        pq, kq = nps()
                        mm(pq, kq, [(ones_b[:], hsq[:, c, :], ["ones", "hsq%d" % c]) for c in range(2)])
                        s.act(REC("activation", out=mean_sb[:], in_=pm, func=AF.Copy, scale=1.0 / 256), r=list(km), w=["mean_sb"])
                        s.dve(REC("tensor_tensor", out=var_sb[:], in0=mean_sb[:], in1=mean_sb[:], op=ALU.mult), r=["mean_sb"], w=["var_sb"])
                        s.dve(REC("scalar_tensor_tensor", out=var_sb[:], in0=pq, scalar=1.0 / 256, in1=var_sb[:], op0=ALU.mult, op1=ALU.subtract), r=list(kq) + ["var_sb"], w=["var_sb"])
                        rsqrt(var_sb[:], var_sb[:], 1.0, LN_EPS, ["var_sb"], ["var_sb"])
                        for c in range(2):
                            s.dve(REC("tensor_tensor", out=hcf[:, c, :], in0=hcf[:, c, :], in1=mean_sb[:], op=ALU.subtract), r=["hcf%d" % c, "mean_sb"], w=["hcf%d" % c])
                            s.dve(REC("tensor_tensor", out=hcf[:, c, :], in0=hcf[:, c, :], in1=var_sb[:], op=ALU.mult), r=["hcf%d" % c, "var_sb"], w=["hcf%d" % c])
                            s.act(REC("activation", out=hnb[:, c, :], in_=hcf[:, c, :], func=AF.Silu, scale=V(V_CONVG + c), bias=V(V_CONVBETA + c)), r=["hcf%d" % c, "vec"], w=["hnb%d" % c])
                        for m in range(2):
                            py, ky = nps()
                            mm(py, ky, [(wp2[:, k, m * 128:(m + 1) * 128], hnb[:, k, :], [wkey("wp2"), "hnb%d" % k]) for k in range(2)])
                            s.dve(REC("tensor_tensor", out=ysA[:, m, :], in0=py, in1=szA[:, m, :], op=ALU.mult), r=list(ky) + ["szA%d" % m], w=["ysA%d" % m])
                            yield
                    yield ("done", "conv")
                    yield ("wait", "cmat")
                    if 's5' not in os.environ.get('KA_SKIP', ''):
                        for p in range(8):
                            eng = s.dve if p % 2 == 0 else s.pool
                            ei = p % 2
                            W_ = w5[ei]
                            WK = ["w5_%d_%d" % (ei, j) for j in range(6)]
                            ri_ = ei * 2 + rotc[ei] % 2
                            rotc[ei] += 1
                            rk = "rot%d" % ri_
                            s.dma(REC("dma_start", out=rot[ri_][:, :, :], in_=rotd[:, p * 2 * TB:(p * 2 + 2) * TB].rearrange("p (c t) -> p c t", c=2)), r=["rotd%d" % p], w=[rk])
                            Cc = rot[ri_][:, 0, :]
                            Ss = rot[ri_][:, 1, :]
                            kt = p // 4
                            (pr, kr), (pi_, ki) = ns5()
                            mm(pr, kr, [(BB[:, p, :], ub[:, kt, :], ["BB", "ub%d" % kt])])
                            mm(pi_, ki, [(BB[:, 8 + p, :], ub[:, kt, :], ["BB", "ub%d" % kt])])
                            if ei == 1:
                                s.act(REC("activation", out=W_[4][:], in_=pr, func=AF.Copy), r=list(kr), w=[WK[4]])
                                s.act(REC("activation", out=W_[5][:], in_=pi_, func=AF.Copy), r=list(ki), w=[WK[5]])
                                pr, kr = W_[4][:], [WK[4]]
                                pi_, ki = W_[5][:], [WK[5]]
                            eng(REC("tensor_tensor", out=W_[0][:], in0=pr, in1=Cc, op=ALU.mult), r=list(kr) + [rk], w=[WK[0]])
                            eng(REC("tensor_tensor", out=W_[1][:], in0=pi_, in1=Ss, op=ALU.mult), r=list(ki) + [rk], w=[WK[1]])
                            eng(REC("tensor_tensor", out=W_[0][:], in0=W_[0][:], in1=W_[1][:], op=ALU.add), r=[WK[0], WK[1]], w=[WK[0]])
                            eng(REC("tensor_tensor", out=W_[2][:], in0=pi_, in1=Cc, op=ALU.mult), r=list(ki) + [rk], w=[WK[2]])
                            eng(REC("tensor_tensor", out=W_[3][:], in0=pr, in1=Ss, op=ALU.mult), r=list(kr) + [rk], w=[WK[3]])
                            eng(REC("tensor_tensor", out=W_[2][:], in0=W_[2][:], in1=W_[3][:], op=ALU.subtract), r=[WK[2], WK[3]], w=[WK[2]])
                            magb = sp_[:, MAG * 8 + p:MAG * 8 + p + 1].broadcast_to([128, TB])
                            s.dve(REC("tensor_tensor_scan", out=W_[4][:], data0=magb, data1=W_[0][:], initial=carry[:, p:p + 1], op0=ALU.mult, op1=ALU.add), r=[WK[0], "carryR%d" % p] + K5, w=[WK[4]])
                            s.dve(REC("tensor_tensor_scan", out=W_[5][:], data0=magb, data1=W_[2][:], initial=carry[:, 8 + p:9 + p], op0=ALU.mult, op1=ALU.add), r=[WK[2], "carryI%d" % p] + K5, w=[WK[5]])
                            yield
                            eng(REC("tensor_tensor", out=W_[0][:], in0=W_[4][:], in1=Cc, op=ALU.mult), r=[WK[4], rk], w=[WK[0]])
                            eng(REC("tensor_tensor", out=W_[1][:], in0=W_[5][:], in1=Ss, op=ALU.mult), r=[WK[5], rk], w=[WK[1]])
                            eng(REC("tensor_tensor", out=W_[0][:], in0=W_[0][:], in1=W_[1][:], op=ALU.subtract), r=[WK[0], WK[1]], w=[WK[0]])
                            eng(REC("tensor_tensor", out=W_[2][:], in0=W_[5][:], in1=Cc, op=ALU.mult), r=[WK[5], rk], w=[WK[2]])
                            eng(REC("tensor_tensor", out=W_[3][:], in0=W_[4][:], in1=Ss, op=ALU.mult), r=[WK[4], rk], w=[WK[3]])
                            eng(REC("tensor_tensor", out=W_[2][:], in0=W_[2][:], in1=W_[3][:], op=ALU.add), r=[WK[2], WK[3]], w=[WK[2]])
                            eng(REC("tensor_copy", out=carry[:, p:p + 1], in_=W_[0][:, TB - 1:TB]), r=[WK[0]], w=["carryR%d" % p])
                            eng(REC("tensor_copy", out=carry[:, 8 + p:9 + p], in_=W_[2][:, TB - 1:TB]), r=[WK[2]], w=["carryI%d" % p])
                            s.act(REC("activation", out=hb[:, p, :], in_=W_[0][:], func=AF.Copy), r=[WK[0]], w=["hb%d" % p])
                            s.act(REC("activation", out=hb[:, 8 + p, :], in_=W_[2][:], func=AF.Copy), r=[WK[2]], w=["hb%d" % (8 + p)])
                            yield
                        for m in range(2):
                            pyc, kyc = nps()
                            prs = []
                            for p in range(4 * m, 4 * m + 4):
                                prs.append((CB[:, p, :], hb[:, p, :], ["CB", "hb%d" % p]))
                                prs.append((CB[:, 8 + p, :], hb[:, 8 + p, :], ["CB", "hb%d" % (8 + p)]))
                            mm(pyc, kyc, prs)
                            s.dve(REC("scalar_tensor_tensor", out=ygf[:, m, :], in0=uf[:, m, :], scalar=V(V_SSMD + m), in1=pyc, op0=ALU.mult, op1=ALU.add), r=list(kyc) + ["uf%d" % m, "vec"], w=["ygf%d" % m])
                            yield
                        yield ("done", "cmat")
                        for m in range(2):
                            s.act(REC("activation", out=tmpc[:, m, :], in_=ygf[:, m, :], func=AF.Square), r=["ygf%d" % m], w=["tmpc%d" % m])
                            s.dve(REC("tensor_scalar", out=tmpc[:, m, :], in0=tmpc[:, m, :], scalar1=0.044715, scalar2=1.0, op0=ALU.mult, op1=ALU.add), r=["tmpc%d" % m], w=["tmpc%d" % m])
                            s.dve(REC("tensor_tensor", out=tmpc[:, m, :], in0=tmpc[:, m, :], in1=ygf[:, m, :], op=ALU.mult), r=["tmpc%d" % m, "ygf%d" % m], w=["tmpc%d" % m])
                            yield
                            s.act(REC("activation", out=sgl[:, m, :], in_=tmpc[:, m, :], func=AF.Sigmoid, scale=1.5957691216057308), r=["tmpc%d" % m], w=["sgl%d" % m])
                            s.pool(REC("tensor_tensor", out=ygb[:, m, :], in0=ygf[:, m, :], in1=sgl[:, m, :], op=ALU.mult), r=["ygf%d" % m, "sgl%d" % m], w=["ygb%d" % m])
                            yield
                        for m in range(2):
                            p2, k2 = nps()
                            mm(p2, k2, [(wgl[:, k, (2 + m) * 128:(3 + m) * 128], ygb[:, k, :], [wkey("wgl"), "ygb%d" % k]) for k in range(2)])
                            s.act(REC("activation", out=sgl[:, m, :], in_=p2, func=AF.Sigmoid), r=list(k2), w=["sgl%d" % m])
                            yield
                            p1, k1 = nps()
                            mm(p1, k1, [(wgl[:, k, m * 128:(m + 1) * 128], ygb[:, k, :], [wkey("wgl"), "ygb%d" % k]) for k in range(2)])
                            s.dve(REC("tensor_tensor", out=tmpc[:, m, :], in0=p1, in1=sgl[:, m, :], op=ALU.mult), r=list(k1) + ["sgl%d" % m], w=["tmpc%d" % m])
                            yield
                            s.pool(REC("tensor_tensor", out=ysA[:, 2 + m, :], in0=tmpc[:, m, :], in1=szA[:, 2 + m, :], op=ALU.mult), r=["tmpc%d" % m, "szA%d" % (2 + m)], w=["ysA%d" % (2 + m)])
                            yield
                    for m in range(4):
                        row = (m if m < 2 else 2 + m) * 128
                        s.dma(REC("dma_start", out=ysd[row:row + 128, c0:c0 + TB], in_=ysA[:, m, :]), r=["ysA%d" % m], w=["ysd_%d_%d" % (b, row // 128)])
                    yield

                run_pipe(blockA, NB, int(os.environ.get('KLAG_A', '4')), ("xb", "szA", "uf", "ub", "ysA"))
                NROT[0] = 5

            s.barrier()
            scB = ExitStack()
            with scB:
              if 'B' in SW:
                def sbb(shape, dt=F32, name=None):
                    uid[0] += 1
                    return scB.enter_context(nc.sbuf_tensor("%s_%d" % (name or "b", uid[0]), list(shape), dt))
                wB = sbb([128, 8, 13 * 128], BF16, "wB")
                load_w(wB, w1b[l], D, "wB", bounds=[0, 640, 1664])
                wq = sbb([128, 2, 1024], BF16, "wq")
                load_w(wq, wuq[l], 256, "wq")
                wkv = sbb([128, 1, 512], BF16, "wkv")
                load_w(wkv, wukv[l], 128, "wkv")
                Kc = sbb([128, 4, S], BF16, "Kc")
                Vc = sbb([128, NT, 4, 128], BF16, "Vc")
                s.pool(REC("memset", Vc[:], 1.0), w=["Vc"] + ["Vc_%d" % t for t in range(NT)])
                Ksw = sbb([128, 128 + TB], BF16, "Ksw")
                s.pool(REC("memset", Ksw[:], 0.0), w=["Ksw"])
                Vsw = sbb([128, 5, 2, 128], BF16, "Vsw")
                s.pool(REC("memset", Vsw[:], 1.0), w=["Vsw"])
                esk = sbb([128, 4], F32, "esk")
                s.act(REC("activation", out=esk[:], in_=V(V_SINK, 4), func=AF.Exp), r=["vec"], w=["esk"])
                eskf = sbb([128, 4, 128], F32, "eskf")
                s.dve(REC("tensor_copy", out=eskf[:], in_=esk[:, :].unsqueeze(2).broadcast_to([128, 4, 128])), r=["esk"], w=["eskf"])
                szB = sbb([128, 4, TB], BF16, "szB")
                cqf = sbb([128, 3, TB], F32, "cqf")
                cqs = sbb([128, 3, TB], BF16, "cqs")
                cqn = sbb([128, 3, TB], BF16, "cqn")
                rstd = [sbb([128, TB], F32, "rstd%d" % i) for i in range(2)]
                rc = sbb([128, TB], F32, "rc")
                rs_ = sbb([128, TB], F32, "rs")
                t1 = sbb([128, TB], F32, "t1")
                t2 = sbb([128, TB], F32, "t2")
                Qh = sbb([128, 4, TB], BF16, "Qh")
                qsw = sbb([128, 2, TB], BF16, "qsw")
                pbuf = [sbb([128, TB], BF16, "pbuf%d" % i) for i in range(3)]
                pbs = [sbb([128, 1024], BF16, "pbs%d" % i) for i in range(2)]
                rden = sbb([128, TB], F32, "rden")
                tmpo = sbb([128, TB], F32, "tmpo")
                ysB = sbb([128, 4, TB], BF16, "ysB")
                pcount = [0]

                xbB = [xb, sbb([128, 8, TB], BF16, "xbB2")]
                szBs = [szB, sbb([128, 4, TB], BF16, "szB2")]
                Qhs = [Qh, sbb([128, 4, TB], BF16, "Qh2")]
                ysBs = [ysB, sbb([128, 4, TB], BF16, "ysB2")]

                def blockB(b):
                    xb = xbB[b % 2]
                    szB = szBs[b % 2]
                    Qh = Qhs[b % 2]
                    ysB = ysBs[b % 2]
                    load_x(l, b, xb)
                    yield
                    yield ("wait", "front")
                    c0 = b * TB
                    s.dma(REC("dma_start", out=rc[:], in_=ropec[:, c0:c0 + TB]), w=["rc"])
                    s.dma(REC("dma_start", out=rs_[:], in_=ropes[:, c0:c0 + TB]), w=["rs"])

                    def projB(mt):
                        pt, pk = nps()
                        mm(pt, pk, [(wB[:, k, mt * 128:(mt + 1) * 128], xb[:, k, :], [wkey("wB", mt * 128), "xb%d" % k]) for k in range(8)])
                        return pt, pk
                    for c in range(3):
                        pc, kc = projB(c)
                        s.act(REC("activation", out=cqf[:, c, :], in_=pc, func=AF.Copy), r=list(kc), w=["cqf%d" % c])
                        s.act(REC("activation", out=cqs[:, c, :], in_=pc, func=AF.Square), r=list(kc), w=["cqs%d" % c])
                        yield
                    pm, km = nps()
                    mm(pm, km, [(ones_b[:], cqs[:, c, :], ["ones", "cqs%d" % c]) for c in range(2)])
                    rsqrt(rstd[0][:], pm, 1.0 / 256, RMS_EPS, list(km), ["rstd0"])
                    yield
                    pm2, km2 = nps()
                    mm(pm2, km2, [(ones_b[:], cqs[:, 2, :], ["ones", "cqs2"])])
                    rsqrt(rstd[1][:], pm2, 1.0 / 128, RMS_EPS, list(km2), ["rstd1"])
                    yield
                    for c in range(3):
                        gcol = V_QNG + c if c < 2 else V_KVNG
                        ri = 0 if c < 2 else 1
                        s.dve(REC("scalar_tensor_tensor", out=cqn[:, c, :], in0=cqf[:, c, :], scalar=V(gcol), in1=rstd[ri][:], op0=ALU.mult, op1=ALU.mult),
                              r=["cqf%d" % c, "rstd%d" % ri, "vec"], w=["cqn%d" % c])
                        yield
                    pa, ka = projB(3)
                    pb_, kb = projB(4)
                    R = slice(64, 96)
                    s.dve(REC("tensor_tensor", out=t1[R, :], in0=pa[R, :], in1=rc[R, :], op=ALU.mult), r=list(ka) + ["rc"], w=["t1"])
                    s.dve(REC("tensor_tensor", out=t2[R, :], in0=pb_[R, :], in1=rs_[R, :], op=ALU.mult), r=list(kb) + ["rs"], w=["t2"])
                    yield
                    for h in range(4):
                        f = s.dve if h % 2 == 0 else s.pool
                        f(REC("tensor_tensor", out=Kc[R, h, c0:c0 + TB], in0=t1[R, :], in1=t2[R, :], op=ALU.add), r=["t1", "t2"], w=["Kc_%d_%d" % (h, b)])
                    for h in range(4):
                        pk_, kk = nps()
                        mm(pk_, kk, [(wkv[:, 0, h * 64:(h + 1) * 64], cqn[:, 2, :], [wkey("wkv"), "cqn2"])])
                        s.act(REC("activation", out=Kc[0:64, h, c0:c0 + TB], in_=pk_[0:64, :], func=AF.Copy), r=list(kk), w=["Kc_%d_%d" % (h, b)])
                        yield
                    for j in range(4):
                        pv, kv = nps()
                        mm(pv[:, 0:256], kv, [(cqn[:, 2, j * 128:(j + 1) * 128], wkv[:, 0, 256:512], [wkey("wkv"), "cqn2"])])
                        s.act(REC("activation", out=Vc[:, b * 4 + j, :, 0:64], in_=pv[:, 0:256].rearrange("p (h d) -> p h d", d=64), func=AF.Copy), r=list(kv) + ["Vc"], w=["Vc_%d" % (b * 4 + j)])
                        yield
                    for h in range(4):
                        pA, kA = nps()
                        mm(pA, kA, [(wq[:, k, (2 * h) * 128:(2 * h) * 128 + 96], cqn[:, k, :], [wkey("wq"), "cqn%d" % k]) for k in range(2)])
                        pB, kB = nps()
                        mm(pB, kB, [(wq[:, k, (2 * h + 1) * 128:(2 * h + 1) * 128 + 96], cqn[:, k, :], [wkey("wq"), "cqn%d" % k]) for k in range(2)])
                        s.act(REC("activation", out=Qh[0:64, h, :], in_=pA[0:64, :], func=AF.Copy), r=list(kA), w=["Qh%d" % h])
                        s.dve(REC("tensor_tensor", out=t1[R, :], in0=pA[R, :], in1=rc[R, :], op=ALU.mult), r=list(kA) + ["rc"], w=["t1"])
                        s.dve(REC("tensor_tensor", out=t2[R, :], in0=pB[R, :], in1=rs_[R, :], op=ALU.mult), r=list(kB) + ["rs"], w=["t2"])
                        s.dve(REC("tensor_tensor", out=Qh[R, h, :], in0=t1[R, :], in1=t2[R, :], op=ALU.add), r=["t1", "t2", "Qh%d" % h], w=["Qh%d" % h])
                        yield
                    yield ("done", "front")
                    yield ("wait", "end")
                    for c in range(2):
                        pz, kz = projB(5 + c)
                        s.act(REC("activation", out=szB[:, c, :], in_=pz, func=AF.Silu), r=list(kz), w=["szB%d" % c])
                        pz2, kz2 = projB(11 + c)
                        s.act(REC("activation", out=szB[:, 2 + c, :], in_=pz2, func=AF.Silu), r=list(kz2), w=["szB%d" % (2 + c)])
                        yield
                    scale = (64 + 32) ** -0.5
                    nkt = 4 * b + 4
                    SK = 2
                    accs = {}

                    def emit_S(h, kt):
                        if kt == 0:
                            bk = 5 + (h % 2)
                            accs[h] = (psum[:, bk * 512:(bk + 1) * 512], ("ps%d" % bk,))
                        jl = kt - 4 * b
                        qs = max(0, jl) * 128
                        ps_, ks = nps()
                        kb_ = kt // 4
                        s.pe(REC("matmul", ps_[:, qs:TB], lhsT=Kc[0:96, h, kt * 128:(kt + 1) * 128], rhs=Qh[0:96, h, qs:TB], start=True, stop=True),
                             r=["Kc_%d_%d" % (h, kb_), "Qh%d" % h], w=list(ks))
                        pi = pcount[0] % 3
                        pcount[0] += 1
                        s.act(REC("activation", out=pbuf[pi][:, qs:TB], in_=ps_[:, qs:TB], func=AF.Exp, scale=float(scale)), r=list(ks), w=["pbuf%d" % pi])
                        if jl >= 0:
                            s.pool(REC("tensor_tensor", out=pbuf[pi][:, qs:qs + 128], in0=pbuf[pi][:, qs:qs + 128], in1=tri_b, op=ALU.mult), r=["pbuf%d" % pi, "cmb"], w=["pbuf%d" % pi])
                        return pi, qs

                    def emit_PV(h, kt, pi, qs):
                        po, ko = accs[h]
                        s.pe(REC("matmul", po[:, qs:TB], lhsT=Vc[:, kt, h, :], rhs=pbuf[pi][:, qs:TB], start=(kt == 0), stop=(kt == nkt - 1)),
                             r=["Vc_%d" % kt, "pbuf%d" % pi], w=list(ko))
                        if kt == nkt - 1:
                            hp = slice((h % 2) * 64, (h % 2) * 64 + 64)
                            s.dve(REC("reciprocal", out=rden[64:128, :], in_=po[64:128, :]), r=list(ko), w=["rden"])
                            s.dve(REC("tensor_tensor", out=tmpo[hp, :], in0=po[0:64, :], in1=rden[64:128, :], op=ALU.mult), r=list(ko) + ["rden"], w=["tmpo"])
                            s.pool(REC("tensor_tensor", out=ysB[hp, h // 2, :], in0=tmpo[hp, :], in1=szB[hp, h // 2, :], op=ALU.mult), r=["tmpo", "szB%d" % (h // 2)], w=["ysB%d" % (h // 2)])

                    def swa_gen():
                        for c in range(2):
                            pq_, kq_ = projB(7 + c)
                            s.act(REC("activation", out=qsw[:, c, :], in_=pq_, func=AF.Copy), r=list(kq_), w=["qsw%d" % c])
                            yield
                        pk2, kk2 = projB(9)
                        s.act(REC("activation", out=Ksw[:, 128:128 + TB], in_=pk2, func=AF.Copy), r=list(kk2), w=["Ksw"])
                        yield
                        for j in range(4):
                            pv, kv = nps()
                            mm(pv[:, 0:128], kv, [(xb[:, k, j * 128:(j + 1) * 128], wB[:, k, 10 * 128:11 * 128], [wkey("wB", 10 * 128), "xb%d" % k]) for k in range(8)])
                            s.act(REC("activation", out=Vsw[:, 1 + j, :, 0:64], in_=pv[:, 0:128].rearrange("p (g d) -> p g d", d=64), func=AF.Copy), r=list(kv), w=["Vsw"])
                            yield
                        sscale = 64 ** -0.5
                        for h in range(4):
                            g = h // 2
                            qt_ = h % 2
                            gp = slice(g * 64, g * 64 + 64)
                            p2b, k2b = nps(2)
                            for i in range(4):
                                for wch in range(2):
                                    col = (i * 2 + wch) * 128
                                    kcol = (i + wch) * 128
                                    s.pe(REC("matmul", p2b[:, col:col + 128], lhsT=Ksw[gp, kcol:kcol + 128], rhs=qsw[gp, qt_, i * 128:(i + 1) * 128], start=True, stop=True),
                                         r=["Ksw", "qsw%d" % qt_], w=list(k2b))
                            pi = h % 2
                            for hh in range(2):
                                s.act(REC("activation", out=pbs[pi][:, hh * 512:(hh + 1) * 512], in_=p2b[:, hh * 512:(hh + 1) * 512], func=AF.Exp, scale=float(sscale)), r=list(k2b), w=["pbs%d" % pi])
                            s.pool(REC("tensor_tensor", out=pbs[pi][:, :].rearrange("p (i m) -> p i m", m=256), in0=pbs[pi][:, :].rearrange("p (i m) -> p i m", m=256),
                                                                in1=swm_b.unsqueeze(1).broadcast_to([128, 4, 256]), op=ALU.mult), r=["pbs%d" % pi, "cmb"], w=["pbs%d" % pi])
                            po, ko = psum[:, 7 * 512:8 * 512], ("ps7",)
                            for i in range(4):
                                first = True
                                for wch in range(2):
                                    if b == 0 and i == 0 and wch == 0:
                                        continue
                                    col = (i * 2 + wch) * 128
                                    s.pe(REC("matmul", po[:, i * 128:(i + 1) * 128], lhsT=Vsw[:, i + wch, g, :], rhs=pbs[pi][:, col:col + 128], start=first, stop=(wch == 1)),
                                         r=["Vsw", "pbs%d" % pi], w=list(ko))
                                    first = False
                            hp = slice((h % 2) * 64, (h % 2) * 64 + 64)
                            s.dve(REC("tensor_tensor", out=rden[64:128, :].rearrange("p (i m) -> p i m", m=128), in0=po[64:128, :].rearrange("p (i m) -> p i m", m=128),
                                                                    in1=eskf[64:128, h:h + 1, :].broadcast_to([64, 4, 128]), op=ALU.add), r=list(ko) + ["eskf"], w=["rden"])
                            s.dve(REC("reciprocal", out=rden[64:128, :], in_=rden[64:128, :]), r=["rden"], w=["rden"])
                            s.dve(REC("tensor_tensor", out=tmpo[hp, :], in0=po[0:64, :], in1=rden[64:128, :], op=ALU.mult), r=list(ko) + ["rden"], w=["tmpo"])
                            s.pool(REC("tensor_tensor", out=ysB[hp, 2 + h // 2, :], in0=tmpo[hp, :], in1=szB[hp, 2 + h // 2, :], op=ALU.mult), r=["tmpo", "szB%d" % (2 + h // 2)], w=["ysB%d" % (2 + h // 2)])
                            yield

                    sg_ = swa_gen()
                    n_items = 4 * nkt
                    every = max(1, n_items // 20)
                    fifo = []
                    it = 0
                    for h in range(4):
                        for kt in range(nkt):
                            fifo.append((h, kt) + emit_S(h, kt))
                            if len(fifo) > SK:
                                emit_PV(*fifo.pop(0))
                            it += 1
                            if it % every == 0:
                                next(sg_, None)
                            yield
                    while fifo:
                        emit_PV(*fifo.pop(0))
                        yield
                    for _ in sg_:
                        yield
                    s.pool(REC("tensor_copy", out=Ksw[:, 0:128], in_=Ksw[:, TB:TB + 128]), r=["Ksw"], w=["Ksw"])
                    s.pool(REC("tensor_copy", out=Vsw[:, 0, :, 0:64], in_=Vsw[:, 4, :, 0:64]), r=["Vsw"], w=["Vsw"])
                    for m in range(4):
                        row = (2 + m if m < 2 else 4 + m) * 128
                        s.dma(REC("dma_start", out=ysd[row:row + 128, c0:c0 + TB], in_=ysB[:, m, :]), r=["ysB%d" % m], w=["ysd_%d_%d" % (b, row // 128)])
                    yield ("done", "end")

                run_pipe(blockB, NB, 1, ("xb", "szB", "Qh", "ysB"))

            s.barrier()
            scC = ExitStack()
            with scC:
              if 'C1' in SW:
                def sc(shape, dt=F32, name=None):
                    uid[0] += 1
                    return scC.enter_context(nc.sbuf_tensor("%s_%d" % (name or "c", uid[0]), list(shape), dt))
                wm = sc([128, 8, 4 * D], BF16, "wm")
                load_w(wm, wmerge[l], D, "wm", bounds=[0, 1024, 2048, 3072, 4096], order=[0])
                wbr = sc([128, 8, D], BF16, "wbr")
                load_w(wbr, wbranch[l], 1024, "wbr", by_k=True)
                load_w(wm, wmerge[l], D, "wm", bounds=[0, 1024, 2048, 3072, 4096], order=[1, 2, 3])
                ysl = sc([128, 8, TB], BF16, "ysl")
                gs = [sc([128, TB], F32, "gs%d" % i) for i in range(2)]
                tp = [sc([128, TB], F32, "tp%d" % i) for i in range(2)]
                mg = sc([128, 8, TB], F32, "mg")
                mgb = sc([128, 8, TB], BF16, "mgb")
                cnt = [0]
                for b in range(NB):
                    load_x(l, b)
                    c0 = b * TB
                    for t in range(8):
                        s.dma(REC("dma_start", out=ysl[:, t, :], in_=ysd[t * 128:(t + 1) * 128, c0:c0 + TB]), r=["ysd_%d_%d" % (b, t)], w=["ysl%d" % t])
                    for n in range(4):
                        for m in range(8):
                            i = cnt[0] % 2
                            cnt[0] += 1
                            pg, kg = nps()
                            mm(pg, kg, [(wm[:, k, n * D + m * 128:n * D + (m + 1) * 128], xb[:, k, :], [wkey("wm", n * D + m * 128), "xb%d" % k]) for k in range(8)])
                            s.act(REC("activation", out=gs[i][:], in_=pg, func=AF.Sigmoid, bias=V(V_BMERGE + n * 8 + m)), r=list(kg) + ["vec"], w=["gs%d" % i])
                            pb2, kb2 = nps()
                            mm(pb2, kb2, [(wbr[:, 2 * n + k, m * 128:(m + 1) * 128], ysl[:, 2 * n + k, :], [wkey("wbr", kt=2 * n + k), "ysl%d" % (2 * n + k)]) for k in range(2)])
                            if n == 0:
                                s.dve(REC("tensor_tensor", out=mg[:, m, :], in0=pb2, in1=gs[i][:], op=ALU.mult), r=list(kb2) + ["gs%d" % i], w=["mg%d" % m])
                            else:
                                s.dve(REC("tensor_tensor", out=tp[i][:], in0=pb2, in1=gs[i][:], op=ALU.mult), r=list(kb2) + ["gs%d" % i], w=["tp%d" % i])
                                if n < 3:
                                    s.pool(REC("tensor_tensor", out=mg[:, m, :], in0=mg[:, m, :], in1=tp[i][:], op=ALU.add), r=["mg%d" % m, "tp%d" % i], w=["mg%d" % m])
                                else:
                                    s.pool(REC("tensor_tensor", out=mgb[:, m, :], in0=mg[:, m, :], in1=tp[i][:], op=ALU.add), r=["mg%d" % m, "tp%d" % i], w=["mgb%d" % m])
                                    s.dma(REC("dma_start", out=mgd[m * 128:(m + 1) * 128, c0:c0 + TB], in_=mgb[:, m, :]), r=["mgb%d" % m], w=["mgd_%d_%d" % (b, m)])

            s.barrier()
            scD = ExitStack()
            with scD:
              if 'C2' in SW:
                def sd(shape, dt=F32, name=None):
                    uid[0] += 1
                    return scD.enter_context(nc.sbuf_tensor("%s_%d" % (name or "d", uid[0]), list(shape), dt))
                wo = sd([128, 8, D], BF16, "wo")
                load_w(wo, wout[l], D, "wo", bounds=[0, 256, 1024])
                wpg = sd([128, 8, D], BF16, "wpg")
                wpl = sd([128, 2, D], BF16, "wpl")
                load_w(wpl, wple[l], 256, "wpl")
                load_w(wpg, wpleg[l], D, "wpg")
                mgl = sd([128, 8, TB], BF16, "mgl")
                pb16 = sd([128, 2, TB], BF16, "pb16")
                zb = [sd([128, TB], BF16, "zb%d" % i) for i in range(2)]
                zq = [sd([128, TB], BF16, "zq%d" % i) for i in range(2)]
                mean2 = sd([128, TB], F32, "mean2")
                var2 = sd([128, TB], F32, "var2")
                xlb = sd([128, 8, TB], BF16, "xlb")
                ef = sd([128, 8, TB], F32, "ef")
                sge = [sd([128, TB], F32, "sge%d" % i) for i in range(2)]
                rse = sd([128, TB], F32, "rse")
                xo = [sd([128, TB], F32, "xo%d" % i) for i in range(2)]
                alpha = (2.0 * 4) ** 0.25
                dst = yT if l == L - 1 else xs
                xfs = [xf, sd([128, 8, TB], F32, "xfD2")]
                mgls = [mgl, sd([128, 8, TB], BF16, "mgl2")]
                pb16s = [pb16, sd([128, 2, TB], BF16, "pb16b")]
                xlbs = [xlb, sd([128, 8, TB], BF16, "xlb2")]
                efs = [ef, sd([128, 8, TB], F32, "ef2")]
                mean2s = [mean2, sd([128, TB], F32, "mean2b")]
                var2s = [var2, sd([128, TB], F32, "var2b")]
                rses = [rse, sd([128, TB], F32, "rseb")]

                def blockD(b):
                    xf = xfs[b % 2]
                    mgl = mgls[b % 2]
                    pb16 = pb16s[b % 2]
                    xlb = xlbs[b % 2]
                    ef = efs[b % 2]
                    mean2 = mean2s[b % 2]
                    var2 = var2s[b % 2]
                    rse = rses[b % 2]
                    load_x(l, b, xf=xf, cast=False)
                    c0 = b * TB
                    for t in range(8):
                        s.dma(REC("dma_start", out=mgl[:, t, :], in_=mgd[t * 128:(t + 1) * 128, c0:c0 + TB]), r=["mgd_%d_%d" % (b, t)], w=["mgl%d" % t])
                    for t in range(2):
                        s.dma(REC("dma_start", out=pb16[:, t, :], in_=pT[l, t * 128:(t + 1) * 128, c0:c0 + TB]), w=["pb16_%d" % t], q="pool")
                    yield
                    pmean, kmean = nacc()
                    pmsq, kmsq = nacc()
                    for m in range(8):
                        pz, kz = nps()
                        mm(pz, kz, [(wo[:, k, m * 128:(m + 1) * 128], mgl[:, k, :], [wkey("wo", m * 128), "mgl%d" % k]) for k in range(8)])
                        s.dve(REC("scalar_tensor_tensor", out=xf[:, m, :], in0=xf[:, m, :], scalar=float(alpha), in1=pz, op0=ALU.mult, op1=ALU.add), r=list(kz) + ["xf%d" % m], w=["xf%d" % m])
                        i = m % 2
                        s.act(REC("activation", out=zb[i][:], in_=xf[:, m, :], func=AF.Copy), r=["xf%d" % m], w=["zb%d" % i])
                        s.act(REC("activation", out=zq[i][:], in_=xf[:, m, :], func=AF.Square), r=["xf%d" % m], w=["zq%d" % i])
                        s.pe(REC("matmul", pmean, lhsT=ones_b[:], rhs=zb[i][:], start=(m == 0), stop=(m == 7)), r=["ones", "zb%d" % i], w=list(kmean))
                        s.pe(REC("matmul", pmsq, lhsT=ones_b[:], rhs=zq[i][:], start=(m == 0), stop=(m == 7)), r=["ones", "zq%d" % i], w=list(kmsq))
                        yield
                    s.act(REC("activation", out=mean2[:], in_=pmean, func=AF.Copy, scale=1.0 / D), r=list(kmean), w=["mean2"])
                    s.dve(REC("tensor_tensor", out=var2[:], in0=mean2[:], in1=mean2[:], op=ALU.mult), r=["mean2"], w=["var2"])
                    s.dve(REC("scalar_tensor_tensor", out=var2[:], in0=pmsq, scalar=1.0 / D, in1=var2[:], op0=ALU.mult, op1=ALU.subtract), r=list(kmsq) + ["var2"], w=["var2"])
                    rsqrt(var2[:], var2[:], 1.0, LN_EPS, ["var2"], ["var2"])
                    yield
                    for m in range(8):
                        f = s.dve if m % 2 == 0 else s.pool
                        f(REC("tensor_tensor", out=xf[:, m, :], in0=xf[:, m, :], in1=mean2[:], op=ALU.subtract), r=["xf%d" % m, "mean2"], w=["xf%d" % m])
                        f(REC("tensor_tensor", out=xf[:, m, :], in0=xf[:, m, :], in1=var2[:], op=ALU.mult), r=["xf%d" % m, "var2"], w=["xf%d" % m])
                        s.act(REC("activation", out=xf[:, m, :], in_=xf[:, m, :], func=AF.Identity, scale=V(V_LNG + m), bias=V(V_LNB + m)), r=["xf%d" % m, "vec"], w=["xf%d" % m])
                        s.act(REC("activation", out=xlb[:, m, :], in_=xf[:, m, :], func=AF.Copy), r=["xf%d" % m], w=["xlb%d" % m])
                        yield
                    pms, kms = nacc()
                    for m in range(8):
                        i = m % 2
                        pgt, kgt = nps()
                        mm(pgt, kgt, [(wpg[:, k, m * 128:(m + 1) * 128], xlb[:, k, :], [wkey("wpg"), "xlb%d" % k]) for k in range(8)])
                        s.act(REC("activation", out=sge[i][:], in_=pgt, func=AF.Sigmoid), r=list(kgt), w=["sge%d" % i])
                        ppe, kpe = nps()
                        mm(ppe, kpe, [(wpl[:, k, m * 128:(m + 1) * 128], pb16[:, k, :], [wkey("wpl"), "pb16_%d" % k]) for k in range(2)])
                        s.dve(REC("tensor_tensor", out=ef[:, m, :], in0=ppe, in1=sge[i][:], op=ALU.mult), r=list(kpe) + ["sge%d" % i], w=["ef%d" % m])
                        s.act(REC("activation", out=zq[i][:], in_=ef[:, m, :], func=AF.Square), r=["ef%d" % m], w=["zq%d" % i])
                        s.pe(REC("matmul", pms, lhsT=ones_b[:], rhs=zq[i][:], start=(m == 0), stop=(m == 7)), r=["ones", "zq%d" % i], w=list(kms))
                        yield
                    rsqrt(rse[:], pms, 1.0 / D, RMS_EPS, list(kms), ["rse"])
                    yield
                    for m in range(8):
                        i = m % 2
                        f = s.dve if m % 2 == 0 else s.pool
                        s.dve(REC("scalar_tensor_tensor", out=ef[:, m, :], in0=ef[:, m, :], scalar=V(V_PLEG + m), in1=rse[:], op0=ALU.mult, op1=ALU.mult), r=["ef%d" % m, "rse", "vec"], w=["ef%d" % m])
                        f(REC("tensor_tensor", out=xo[i][:], in0=ef[:, m, :], in1=xf[:, m, :], op=ALU.add), r=["ef%d" % m, "xf%d" % m], w=["xo%d" % i])
                        s.dma(REC("dma_start", out=dst[m * 128:(m + 1) * 128, c0:c0 + TB], in_=xo[i][:]), r=["xo%d" % i], w=["xs_%d" % b], final=(l == L - 1))
                        yield
                run_pipe(blockD, NB, int(os.environ.get('KLAG_D', '11')), ("xf", "mgl", "pb", "xlb", "ef", "mean", "var", "rse"))
            s.barrier()
        s.emit(st)
    return nc


def pack_weights(inp, L, S):
    f = np.float32
    W = {}
    w_in = np.asarray(inp["w_in"], f)
    offs = np.cumsum([0, 256, 256, 256, 256, 128, 32, 256, 256, 256, 256, 128, 128, 256])
    seg = {n: (offs[i], offs[i + 1]) for i, n in enumerate(
        ["a_val", "a_gate", "a_z", "c_q", "c_kv", "k_r", "b_z", "u", "c_z", "q", "k", "v", "d_z"])}

    def cols(n):
        a, b = seg[n]
        return w_in[:, :, a:b]
    w1a = np.concatenate([cols("a_val"), cols("a_gate"), cols("a_z"), cols("u"), cols("c_z")], axis=2)
    z = lambda *sh: np.zeros(sh, f)
    kr = cols("k_r")
    krA = z(L, D, 128); krA[:, :, 64:96] = kr
    krB = z(L, D, 128); krB[:, :, 64:80] = kr[:, :, 16:32]; krB[:, :, 80:96] = kr[:, :, 0:16]
    q = cols("q")
    q02 = np.concatenate([q[:, :, 0:64], q[:, :, 128:192]], axis=2)
    q13 = np.concatenate([q[:, :, 64:128], q[:, :, 192:256]], axis=2)
    w1b = np.concatenate([cols("c_q"), cols("c_kv"), krA, krB, cols("b_z"), q02, q13, cols("k"), cols("v"), cols("d_z")], axis=2)
    W["w1a"] = np.ascontiguousarray(w1a)
    W["w1b"] = np.ascontiguousarray(w1b)
    wuq_ = np.asarray(inp["w_uq"], f)
    wq = z(L, 256, 8 * 128)
    for h in range(4):
        wq[:, :, (2 * h) * 128:(2 * h) * 128 + 96] = wuq_[:, :, h * 96:(h + 1) * 96]
        wq[:, :, (2 * h + 1) * 128 + 64:(2 * h + 1) * 128 + 80] = wuq_[:, :, h * 96 + 80:h * 96 + 96]
        wq[:, :, (2 * h + 1) * 128 + 80:(2 * h + 1) * 128 + 96] = wuq_[:, :, h * 96 + 64:h * 96 + 80]
    W["wuq"] = wq
    wukv_ = np.asarray(inp["w_ukv"], f).reshape(L, 128, 4, 128)
    W["wukv"] = np.ascontiguousarray(np.concatenate([wukv_[:, :, :, 0:64].reshape(L, 128, 256), wukv_[:, :, :, 64:128].reshape(L, 128, 256)], axis=2))
    W["wpw2"] = np.asarray(inp["w_pw2"], f)
    W["wglu"] = np.asarray(inp["w_glu"], f)
    vec = z(L, 128, NV)
    cw = np.asarray(inp["conv_w"], f)
    for c in range(2):
        vec[:, :, V_CONVW + c * 31:V_CONVW + (c + 1) * 31] = cw[:, :, c * 128:(c + 1) * 128].transpose(0, 2, 1)

    def pv(name, col, nt):
        a = np.asarray(inp[name], f).reshape(L, nt, 128)
        vec[:, :, col:col + nt] = a.transpose(0, 2, 1)
    pv("conv_b", V_CONVB, 2); pv("conv_norm_g", V_CONVG, 2); pv("conv_norm_b", V_CONVBETA, 2)
    pv("mla_q_norm_g", V_QNG, 2); pv("mla_kv_norm_g", V_KVNG, 1); pv("ssm_d", V_SSMD, 2)
    pv("b_merge", V_BMERGE, 32); pv("ln_g", V_LNG, 8); pv("ln_b", V_LNB, 8); pv("ple_norm_g", V_PLEG, 8)

    def sm(a):
        return a.reshape(L, 8, 2, 64).transpose(0, 2, 3, 1).reshape(L, 128, 8)
    vec[:, :, V_LR:V_LR + 8] = sm(np.asarray(inp["ssm_a_re"], f))
    vec[:, :, V_LI:V_LI + 8] = sm(np.asarray(inp["ssm_a_im"], f))
    vec[:, :, V_LOGDT:V_LOGDT + 8] = sm(np.repeat(np.asarray(inp["ssm_log_dt"], f)[:, :, None], 64, axis=2))
    vec[:, :, V_SINK:V_SINK + 4] = np.asarray(inp["attn_sinks"], f)[:, None, :]
    W["vecs"] = vec

    def smB(a):
        return a.reshape(L, 8, 2, 64, 16).transpose(0, 2, 3, 1, 4).reshape(L, 128, 128)

    def smC(a):
        return a.reshape(L, 8, 2, 16, 64).transpose(0, 2, 4, 1, 3).reshape(L, 128, 128)
    W["s5b"] = np.ascontiguousarray(np.concatenate([smB(np.asarray(inp["ssm_b_re"], f)), smB(np.asarray(inp["ssm_b_im"], f))], axis=2))
    W["s5c"] = np.ascontiguousarray(np.concatenate([smC(np.asarray(inp["ssm_c_re"], f)), smC(np.asarray(inp["ssm_c_im"], f))], axis=2))
    W["wmerge"] = np.asarray(inp["w_merge"], f)
    W["wbranch"] = np.asarray(inp["w_branch"], f).reshape(L, 1024, D)
    W["wout"] = np.asarray(inp["w_out"], f)
    W["wple"] = np.asarray(inp["w_ple"], f)
    W["wpleg"] = np.asarray(inp["w_ple_gate"], f)
    pos = np.arange(S, dtype=np.float64)
    inv = 10000.0 ** (-np.arange(0, 32, 2, dtype=np.float64) / 32)
    rc = np.zeros((128, S), f); rs = np.zeros((128, S), f)
    for p in range(128):
        i = p % 32
        ang = pos * inv[i % 16]
        ang = (pos.astype(f) * inv.astype(f)[i % 16]).astype(np.float64)
        rc[p] = np.cos(ang)
        rs[p] = (-1.0 if i < 16 else 1.0) * np.sin(ang)
    W["ropec"] = rc; W["ropes"] = rs
    cmv = np.zeros((128, NCM), f)
    cmv[:, C_IDENT:C_IDENT + 128] = np.eye(128, dtype=f)
    k = np.arange(128)[:, None]; qq = np.arange(128)[None, :]
    cmv[:, C_TRI:C_TRI + 128] = (k <= qq)
    cmv[:, C_SWM:C_SWM + 128] = (k > qq)
    cmv[:, C_SWM + 128:C_SWM + 256] = (k <= qq)
    cmv[:, C_IOTA:C_IOTA + 512] = np.arange(1, 513, dtype=f)[None, :]
    W["cmat"] = cmv
    return W


_NC_CACHE = {}


def run(inputs, S, L, ncores, debug=False):
    x = np.asarray(inputs["x"], np.float32)
    p = np.asarray(inputs["p"], np.float32)
    W = pack_weights(inputs, L, S)
    key = (S, L, debug)
    if key not in _NC_CACHE:
        _NC_CACHE[key] = build(S, L, debug)
    nc = _NC_CACHE[key]
    in_maps = []
    for c in range(ncores):
        m = dict(W)
        m["xT"] = np.ascontiguousarray(x[c].T)
        m["pT"] = np.ascontiguousarray(p[:, c].transpose(0, 2, 1))
        in_maps.append(m)
    res = run_bass_kernel_spmd(nc, in_maps, core_ids=list(range(ncores)))
    out = np.stack([np.ascontiguousarray(r["yT"].T) for r in res.results], axis=0)
    if debug:
        return out.astype(np.float32), [dict(r) for r in res.results]
    return out.astype(np.float32)


def kernel(**inputs):
    return run(inputs, 4096, 4, 8)
```

```python
import math
import os
from contextlib import ExitStack
import numpy as np
import concourse.bass as bass
import concourse.mybir as mybir
from concourse.bass_utils import run_bass_kernel_spmd

F32 = mybir.dt.float32
BF16 = mybir.dt.bfloat16
I32 = mybir.dt.int32
ALU = mybir.AluOpType
AF = mybir.ActivationFunctionType

ENGS = ("pe", "dve", "act", "pool", "sp")
N_DMA_SEMS = 24

D = 1024
TB = 512
LN_EPS = 1e-5
RMS_EPS = 1e-6
TWO_PI = 2.0 * math.pi


def REC(name, *args, **kwargs):
    def fn(e):
        return getattr(e, name)(*args, **kwargs)
    return fn


class Sched:
    def __init__(self, nc):
        self.nc = nc
        self.ops = []
        self.last_w = {}
        self.readers = {}

    par = 0
    dbl = ()

    def _k(self, t):
        if self.dbl and isinstance(t, str) and t.rstrip("0123456789_") in self.dbl:
            return t + "#%d" % self.par
        return t

    def op(self, eng, fn, reads=(), writes=(), dma=False, final=False):
        reads = [self._k(t) for t in reads]
        writes = [self._k(t) for t in writes]
        idx = len(self.ops)
        deps = set()
        for t in reads:
            w = self.last_w.get(t)
            if w is not None:
                deps.add((w, "raw"))
        for t in writes:
            w = self.last_w.get(t)
            if w is not None:
                deps.add((w, "waw"))
            for r in self.readers.get(t, ()):
                deps.add((r, "war"))
        for t in writes:
            self.last_w[t] = idx
            self.readers[t] = []
        for t in reads:
            self.readers.setdefault(t, []).append(idx)
        self.ops.append(dict(eng=eng, fn=fn, deps=deps, dma=dma, final=final))
        return idx

    def pe(self, fn, r=(), w=()):
        return self.op("pe", fn, r, w)

    def dve(self, fn, r=(), w=()):
        return self.op("dve", fn, r, w)

    def act(self, fn, r=(), w=()):
        return self.op("act", fn, r, w)

    def pool(self, fn, r=(), w=()):
        return self.op("pool", fn, r, w)

    def dma(self, fn, r=(), w=(), q="sp", final=False):
        return self.op(q, fn, r, w, dma=True, final=final)

    def barrier(self):
        self.ops.append(dict(eng=None, fn=None, deps=set(), dma=False, final=False, barrier=True))
        self.last_w.clear()
        self.readers.clear()

    def emit(self, stack):
        nc = self.nc
        ops = self.ops
        n = len(ops)
        CE = ("pe", "dve", "act", "pool")
        seen = {e: {f: -1 for f in ENGS} for e in ENGS}
        seen_dma = {e: set() for e in ENGS}
        need = [[] for _ in range(n)]
        is_prod = [False] * n
        last_c = {e: -1 for e in CE}
        bar_need = {}
        for i, o in enumerate(ops):
            if o.get("barrier"):
                for e in ENGS:
                    lst = []
                    for e2 in CE:
                        j = last_c[e2]
                        if j >= 0 and e2 != e and j > seen[e][e2]:
                            lst.append(j)
                            is_prod[j] = True
                            seen[e][e2] = j
                    bar_need[(i, e)] = lst
                continue
            e = o["eng"]
            if not o["dma"]:
                last_c[e] = i
            best = {}
            for (d, kind) in o["deps"]:
                po = ops[d]
                pe_ = po["eng"]
                if po["dma"]:
                    if d in seen_dma[e]:
                        continue
                    best[("dma", d)] = d
                    continue
                if pe_ == e and not o["dma"]:
                    if kind != "raw" or e == "pe":
                        continue
                if d <= seen[e][pe_]:
                    continue
                k = ("c", pe_)
                if k not in best or best[k] < d:
                    best[k] = d
            for k, d in best.items():
                need[i].append(d)
                is_prod[d] = True
                if k[0] == "dma":
                    seen_dma[e].add(d)
                else:
                    seen[e][k[1]] = d
        csem = {e: stack.enter_context(nc.semaphore("c_" + e)) for e in ENGS}
        dsem = [stack.enter_context(nc.semaphore("d%d" % k)) for k in range(N_DMA_SEMS)]
        bsem = stack.enter_context(nc.semaphore("bar"))
        cnt = {e: 0 for e in ENGS}
        dcount = [0] * N_DMA_SEMS
        ndma = 0
        waitval = [None] * n
        dma_slot_prev = {}
        extra_wait = [None] * n
        slot_last = {}
        bar_idx = {}
        nbar = 0
        for i, o in enumerate(ops):
            if o.get("barrier"):
                slot_last[i] = dict(dma_slot_prev)
                nbar += 1
                bar_idx[i] = nbar
                continue
            if o["dma"]:
                slot = ndma % N_DMA_SEMS
                ndma += 1
                dcount[slot] += 16
                waitval[i] = (dsem[slot], dcount[slot])
                if slot in dma_slot_prev:
                    extra_wait[i] = dma_slot_prev[slot]
                dma_slot_prev[slot] = i
                o["dsem"] = dsem[slot]
            elif is_prod[i]:
                cnt[o["eng"]] += 1
                waitval[i] = (csem[o["eng"]], cnt[o["eng"]])
        per_eng = {e: [i for i, o in enumerate(ops) if o["eng"] == e or o.get("barrier")] for e in ENGS}
        block = stack.enter_context(nc.Block())
        self.n_waits = 0
        self.n_ins = {e: len(per_eng[e]) for e in ENGS}

        def run(engobj, ename):
            for i in per_eng[ename]:
                o = ops[i]
                if o.get("barrier"):
                    for d in bar_need[(i, ename)]:
                        s_, v = waitval[d]
                        engobj.wait_ge(s_, v)
                    if ename == "sp":
                        for slot, d in slot_last[i].items():
                            s_, v = waitval[d]
                            engobj.wait_ge(s_, v)
                        engobj.dma_start(out=self.bar_dst, in_=self.bar_src).then_inc(bsem, 16)
                    else:
                        engobj.wait_ge(bsem, 16 * bar_idx[i])
                    continue
                ws = list(need[i])
                if extra_wait[i] is not None:
                    ws.append(extra_wait[i])
                for d in ws:
                    s_, v = waitval[d]
                    engobj.wait_ge(s_, v)
                    self.n_waits += 1
                ins = o["fn"](engobj)
                if o["dma"]:
                    ins.then_inc(o["dsem"], 16)
                elif is_prod[i]:
                    ins.then_inc(csem[ename], 1)
            for i in per_eng[ename]:
                if ops[i]["dma"] and ops[i]["final"] and ops[i]["eng"] == ename:
                    s_, v = waitval[i]
                    engobj.wait_ge(s_, v)

        @block.tensor
        def _(e):
            run(e, "pe")

        @block.vector
        def _(e):
            run(e, "dve")

        @block.scalar
        def _(e):
            run(e, "act")

        @block.gpsimd
        def _(e):
            run(e, "pool")

        @block.sync
        def _(e):
            run(e, "sp")


V_CONVW = 0
V_CONVB = 62
V_CONVG = 64
V_CONVBETA = 66
V_QNG = 68
V_KVNG = 70
V_SSMD = 71
V_BMERGE = 73
V_LNG = 105
V_LNB = 113
V_PLEG = 121
V_LR = 129
V_LI = 137
V_LOGDT = 145
V_SINK = 153
NV = 160

C_IDENT = 0
C_TRI = 128
C_SWM = 256
C_IOTA = 512
NCM = 1024


def build(S, L, debug=False):
    NB = S // TB
    NT = S // 128
    nc = bass.Bass("TRN2", target_bir_lowering=False)

    def din(name, shape):
        return nc.dram_tensor(name, list(shape), F32, kind="ExternalInput").ap()

    xT = din("xT", [D, S])
    pT = din("pT", [L, 256, S])
    w1a = din("w1a", [L, D, 10 * 128])
    w1b = din("w1b", [L, D, 13 * 128])
    wuq = din("wuq", [L, 256, 8 * 128])
    wukv = din("wukv", [L, 128, 512])
    wpw2 = din("wpw2", [L, 256, 256])
    wglu = din("wglu", [L, 256, 512])
    vecs = din("vecs", [L, 128, NV])
    s5b = din("s5b", [L, 128, 2 * 8 * 16])
    s5c = din("s5c", [L, 128, 2 * 8 * 16])
    wmerge = din("wmerge", [L, D, 4 * D])
    wbranch = din("wbranch", [L, 4 * 256, D])
    wout = din("wout", [L, D, D])
    wple = din("wple", [L, 256, D])
    wpleg = din("wpleg", [L, D, D])
    ropec = din("ropec", [128, S])
    ropes = din("ropes", [128, S])
    cmat = din("cmat", [128, NCM])
    yT = nc.dram_tensor("yT", [D, S], F32, kind="ExternalOutput").ap()
    ysd = nc.dram_tensor("ysd", [D, S], BF16, kind=("ExternalOutput" if debug else "Internal")).ap()
    mgd = nc.dram_tensor("mgd", [D, S], BF16, kind=("ExternalOutput" if debug else "Internal")).ap()
    xs = nc.dram_tensor("xs", [D, S], F32, kind="Internal").ap()
    rotd = nc.dram_tensor("rotd", [128, 16 * TB], F32, kind="Internal").ap()

    st = ExitStack()
    with st:
        s = Sched(nc)
        uid = [0]

        def sb(shape, dt=F32, name=None):
            uid[0] += 1
            return st.enter_context(nc.sbuf_tensor("%s_%d" % (name or "t", uid[0]), list(shape), dt))

        psum = st.enter_context(nc.psum_tensor("psum", [128, 8 * 512], F32))
        psi = [0]

        def nps(nbanks=1):
            NR = NROT[0]
            b = psi[0] % NR
            if nbanks == 2:
                while b % 2 == 1 or b + 2 > NR:
                    psi[0] += 1
                    b = psi[0] % NR
            psi[0] += nbanks
            key = tuple("ps%d" % (b + i) for i in range(nbanks))
            return psum[:, b * 512:(b + nbanks) * 512], key

        acci = [0]
        NROT = [5]
        s5i = [0]

        def ns5():
            b = 4 + 2 * (s5i[0] % 2)
            s5i[0] += 1
            return (psum[:, b * 512:(b + 1) * 512], ("ps%d" % b,)), (psum[:, (b + 1) * 512:(b + 2) * 512], ("ps%d" % (b + 1),))

        def nacc():
            b = 5 + acci[0] % 3
            acci[0] += 1
            return psum[:, b * 512:(b + 1) * 512], ("ps%d" % b,)

        cm = sb([128, NCM], F32, "cm")
        s.dma(REC("dma_start", out=cm[:], in_=cmat), w=["cm"])
        cmb = sb([128, 512], BF16, "cmb")
        s.dve(REC("tensor_copy", out=cmb[:], in_=cm[:, 0:512]), r=["cm"], w=["cmb"])
        ident_f = cm[:, C_IDENT:C_IDENT + 128]
        tri_b = cmb[:, C_TRI:C_TRI + 128]
        swm_b = cmb[:, C_SWM:C_SWM + 256]
        iota_f = cm[:, C_IOTA:C_IOTA + 512]
        bard = nc.dram_tensor("bard", [1, 16], F32, kind="Internal").ap()
        s.bar_dst = bard
        s.bar_src = cm[0:1, 0:16]
        ones_b = sb([128, 128], BF16, "ones")
        s.dve(REC("memset", ones_b[:], 1.0), w=["ones"])
        vec = sb([128, NV], F32, "vec")

        def V(c, n=1):
            return vec[:, c:c + n]

        def mm(ps_ap, pskey, pairs, extra_r=()):
            n = len(pairs)
            for i, (l, r, keys) in enumerate(pairs):
                M_ = l.shape[1]
                o_ap = ps_ap if M_ == 128 else ps_ap[0:M_, :]
                s.pe(REC("matmul", o_ap, lhsT=l, rhs=r, start=(i == 0), stop=(i == n - 1)),
                     r=list(keys) + list(extra_r), w=list(pskey))

        def rsqrt(out, in_, scale, eps, rk, wk, eng="dve"):
            s.dve(REC("tensor_scalar", out=out, in0=in_, scalar1=float(scale), scalar2=float(eps), op0=ALU.mult, op1=ALU.add), r=rk, w=wk)
            s.act(REC("activation", out=out, in_=out, func=AF.Sqrt), r=wk, w=wk)
            s.dve(REC("reciprocal", out=out, in_=out), r=wk, w=wk)

        WSL = {}

        def load_w(tile, dram2d, K, key, bounds=None, order=None, by_k=False):
            nk = max(1, K // 128)
            N = tile.shape[2]
            if by_k:
                WSL[key] = ["k"]
                for kt in range(nk):
                    s.dma(REC("dma_start", out=tile[:, kt, :], in_=dram2d[kt * 128:(kt + 1) * 128, :]), w=["%s@k%d" % (key, kt)], q="pool")
                return
            bounds = list(bounds or [0, N])
            WSL[key] = bounds
            for si in (order if order is not None else range(len(bounds) - 1)):
                c0, c1 = bounds[si], bounds[si + 1]
                for kt in range(nk):
                    s.dma(REC("dma_start", out=tile[:, kt, c0:c1], in_=dram2d[kt * 128:(kt + 1) * 128, c0:c1]), w=["%s@%d" % (key, si)], q="pool")

        def wkey(key, col=0, kt=None):
            bd = WSL[key]
            if bd[0] == "k":
                return "%s@k%d" % (key, kt)
            si = 0
            while si + 1 < len(bd) - 1 and col >= bd[si + 1]:
                si += 1
            return "%s@%d" % (key, si)

        xf = sb([128, 8, TB], F32, "xf")
        xb = sb([128, 8, TB], BF16, "xb")

        def load_x(l, b, xb=xb, xf=xf, cast=True, act_only=False):
            src = xT if l == 0 else xs
            for kt in range(8):
                s.dma(REC("dma_start", out=xf[:, kt, :], in_=src[kt * 128:(kt + 1) * 128, b * TB:(b + 1) * TB]),
                      w=["xf%d" % kt], r=(["xs_%d" % b] if l > 0 else []))
            if not cast:
                return
            for kt in range(8):
                if kt % 2 == 0 or act_only:
                    s.act(REC("activation", out=xb[:, kt, :], in_=xf[:, kt, :], func=AF.Copy), r=["xf%d" % kt], w=["xb%d" % kt])
                else:
                    s.pool(REC("tensor_copy", out=xb[:, kt, :], in_=xf[:, kt, :]), r=["xf%d" % kt], w=["xb%d" % kt])

        XB = ["xb%d" % k for k in range(8)]

        def run_pipe(gen_fn, nblocks, lag, dbl):
            s.dbl = tuple(dbl)
            active = []
            nxt = 0
            while nxt < nblocks or active:
                if nxt < nblocks and len(active) < 2 and (not active or active[-1]["n"] >= lag):
                    active.append(dict(g=gen_fn(nxt), b=nxt, n=0, done=set(), blocked=None))
                    nxt += 1
                for a in list(active):
                    older = active[0] if (a is not active[0]) else None
                    if a["blocked"] is not None:
                        if older is None or a["blocked"] in older["done"]:
                            a["blocked"] = None
                        else:
                            continue
                    s.par = a["b"] % 2
                    try:
                        r = next(a["g"])
                        a["n"] += 1
                        if isinstance(r, tuple):
                            if r[0] == "done":
                                a["done"].add(r[1])
                            elif r[0] == "wait" and older is not None and r[1] not in older["done"]:
                                a["blocked"] = r[1]
                    except StopIteration:
                        active.remove(a)
            s.par = 0
            s.dbl = ()

        for l in range(L):
            s.dma(REC("dma_start", out=vec[:], in_=vecs[l]), w=["vec"])
            scA = ExitStack()
            SW = os.environ.get('KSWEEPS', 'A,B,C1,C2').split(',')
            with scA:
              if 'A' in SW:
                def sa(shape, dt=F32, name=None):
                    uid[0] += 1
                    return scA.enter_context(nc.sbuf_tensor("%s_%d" % (name or "a", uid[0]), list(shape), dt))
                scP = ExitStack()

                def sp2(shape, dt=F32, name=None):
                    uid[0] += 1
                    return scP.enter_context(nc.sbuf_tensor("%s_%d" % (name or "p", uid[0]), list(shape), dt))
                wA = sa([128, 8, 1280], BF16, "wA")
                if 'lw' not in os.environ.get('KA_SKIP', ''):
                    load_w(wA, w1a[l], D, "wA", bounds=[0, 512, 768, 1280], order=[1, 2, 0])
                wp2 = sa([128, 2, 256], BF16, "wp2")
                if 'lw' not in os.environ.get('KA_SKIP', ''):
                    load_w(wp2, wpw2[l], 256, "wp2")
                wgl = sa([128, 2, 512], BF16, "wgl")
                if 'lw' not in os.environ.get('KA_SKIP', ''):
                    load_w(wgl, wglu[l], 256, "wgl")
                dwt = sa([128, 62, 128], BF16, "dwt")
                for cj in (range(62) if 'dwt' not in os.environ.get('KA_SKIP', '') else []):
                    f = s.dve
                    f(REC("tensor_scalar", out=dwt[:, cj, :], in0=ident_f, scalar1=V(V_CONVW + cj), scalar2=None, op0=ALU.mult),
                      r=["cm", "vec"], w=["dwt"])
                sp_ = sa([128, 128], F32, "s5p")

                def P(i, n=8):
                    return sp_[:, i * 8:i * 8 + n]
                DT, MAG, TH, TI_F, THR, T2, LBR, LBI, DEN, FR, FI, NFI, TMP = range(13)
                spi = sa([128, 8], I32, "s5pi")
                CB = sa([128, 16, 128], BF16, "CB")
                BB = sa([128, 16, 128], BF16, "BB")
                K5 = ["s5p"]
                if 'prep' not in os.environ.get('KA_SKIP', ''):
                    s.act(REC("activation", out=P(DT), in_=V(V_LOGDT, 8), func=AF.Exp), r=["vec"], w=K5)
                    s.dve(REC("tensor_tensor", out=P(MAG), in0=V(V_LR, 8), in1=P(DT), op=ALU.mult), r=K5 + ["vec"], w=K5)
                    s.act(REC("activation", out=P(MAG), in_=P(MAG), func=AF.Exp), r=K5, w=K5)
                    s.dve(REC("scalar_tensor_tensor", out=P(TH), in0=V(V_LI, 8), scalar=float(1.0 / TWO_PI), in1=P(DT), op0=ALU.mult, op1=ALU.mult), r=K5 + ["vec"], w=K5)
                    s.dve(REC("tensor_copy", out=spi[:], in_=P(TH)), r=K5, w=["s5pi"])
                    s.dve(REC("tensor_copy", out=P(TI_F), in_=spi[:]), r=["s5pi"], w=K5)
                    s.dve(REC("tensor_tensor", out=P(THR), in0=P(TH), in1=P(TI_F), op=ALU.subtract), r=K5, w=K5)
                    s.act(REC("activation", out=P(LBI), in_=P(THR), func=AF.Sin, scale=6.28318), r=K5, w=K5)
                    s.dve(REC("tensor_scalar", out=P(T2), in0=P(THR), scalar1=0.25, scalar2=None, op0=ALU.add), r=K5, w=K5)
                    s.dve(REC("tensor_copy", out=spi[:], in_=P(T2)), r=K5, w=["s5pi"])
                    s.dve(REC("tensor_copy", out=P(TI_F), in_=spi[:]), r=["s5pi"], w=K5)
                    s.dve(REC("tensor_tensor", out=P(T2), in0=P(T2), in1=P(TI_F), op=ALU.subtract), r=K5, w=K5)
                    s.act(REC("activation", out=P(LBR), in_=P(T2), func=AF.Sin, scale=6.28318), r=K5, w=K5)
                    s.dve(REC("tensor_tensor", out=P(LBR), in0=P(LBR), in1=P(MAG), op=ALU.mult), r=K5, w=K5)
                    s.dve(REC("tensor_tensor", out=P(LBI), in0=P(LBI), in1=P(MAG), op=ALU.mult), r=K5, w=K5)
                    s.dve(REC("tensor_tensor", out=P(DEN), in0=V(V_LR, 8), in1=V(V_LR, 8), op=ALU.mult), r=["vec"] + K5, w=K5)
                    s.dve(REC("tensor_tensor", out=P(TMP), in0=V(V_LI, 8), in1=V(V_LI, 8), op=ALU.mult), r=["vec"] + K5, w=K5)
                    s.dve(REC("tensor_tensor", out=P(DEN), in0=P(DEN), in1=P(TMP), op=ALU.add), r=K5, w=K5)
                    s.dve(REC("reciprocal", out=P(DEN), in_=P(DEN)), r=K5, w=K5)
                    s.dve(REC("tensor_scalar", out=P(T2), in0=P(LBR), scalar1=-1.0, scalar2=None, op0=ALU.add), r=K5, w=K5)
                    s.dve(REC("tensor_tensor", out=P(FR), in0=P(T2), in1=V(V_LR, 8), op=ALU.mult), r=K5 + ["vec"], w=K5)
                    s.dve(REC("tensor_tensor", out=P(TMP), in0=P(LBI), in1=V(V_LI, 8), op=ALU.mult), r=K5 + ["vec"], w=K5)
                    s.dve(REC("tensor_tensor", out=P(FR), in0=P(FR), in1=P(TMP), op=ALU.add), r=K5, w=K5)
                    s.dve(REC("tensor_tensor", out=P(FR), in0=P(FR), in1=P(DEN), op=ALU.mult), r=K5, w=K5)
                    s.dve(REC("tensor_tensor", out=P(FI), in0=P(LBI), in1=V(V_LR, 8), op=ALU.mult), r=K5 + ["vec"], w=K5)
                    s.dve(REC("tensor_tensor", out=P(TMP), in0=P(T2), in1=V(V_LI, 8), op=ALU.mult), r=K5 + ["vec"], w=K5)
                    s.dve(REC("tensor_tensor", out=P(FI), in0=P(FI), in1=P(TMP), op=ALU.subtract), r=K5, w=K5)
                    s.dve(REC("tensor_tensor", out=P(FI), in0=P(FI), in1=P(DEN), op=ALU.mult), r=K5, w=K5)
                    s.dve(REC("tensor_scalar", out=P(NFI), in0=P(FI), scalar1=-1.0, scalar2=None, op0=ALU.mult), r=K5, w=K5)
                bst = sp2([128, 256], F32, "bst")
                cst = sp2([128, 256], F32, "cst")
                s.dma(REC("dma_start", out=bst[:], in_=s5b[l]), w=["bst"])
                s.dma(REC("dma_start", out=cst[:], in_=s5c[l]), w=["cst"])
                bb = sp2([128, 256], F32, "bb")
                tmpb = sp2([128, 128], F32, "tmpb")
                b3 = lambda t, ri: t[:, ri * 128:(ri + 1) * 128].rearrange("p (a h) -> p a h", h=16)
                fr_b = P(FR).unsqueeze(2).broadcast_to([128, 8, 16])
                fi_b = P(FI).unsqueeze(2).broadcast_to([128, 8, 16])
                nfi_b = P(NFI).unsqueeze(2).broadcast_to([128, 8, 16])
                t3 = tmpb[:, :].rearrange("p (a h) -> p a h", h=16)
                if 'bbc' not in os.environ.get('KA_SKIP', ''):
                    s.dve(REC("tensor_tensor", out=b3(bb, 0), in0=b3(bst, 0), in1=fr_b, op=ALU.mult), r=["bst"] + K5, w=["bb"])
                    s.dve(REC("tensor_tensor", out=t3, in0=b3(bst, 1), in1=nfi_b, op=ALU.mult), r=["bst"] + K5, w=["tmpb"])
                    s.dve(REC("tensor_tensor", out=b3(bb, 0), in0=b3(bb, 0), in1=t3, op=ALU.add), r=["bb", "tmpb"], w=["bb"])
                    s.dve(REC("tensor_tensor", out=b3(bb, 1), in0=b3(bst, 1), in1=fr_b, op=ALU.mult), r=["bst"] + K5, w=["bb"])
                    s.dve(REC("tensor_tensor", out=t3, in0=b3(bst, 0), in1=fi_b, op=ALU.mult), r=["bst", "bb"] + K5, w=["tmpb"])
                    s.dve(REC("tensor_tensor", out=b3(bb, 1), in0=b3(bb, 1), in1=t3, op=ALU.add), r=["bb", "tmpb"], w=["bb"])
                BT = sp2([128, 16, 128], F32, "BT")
                if 'ms' not in os.environ.get('KA_SKIP', ''):
                    s.pool(REC("memset", BT[:], 0.0), w=["BT"])
                    s.pool(REC("memset", CB[:], 0.0), w=["CB"])
                if 'scat' not in os.environ.get('KA_SKIP', ''):
                    for ri in range(2):
                        for half in range(2):
                            for grp in range(2):
                                prt = slice(half * 64, half * 64 + 64)
                                dst = BT[prt, ri * 8 + grp * 4: ri * 8 + grp * 4 + 4, :]
                                src = bb[prt, ri * 128 + grp * 64: ri * 128 + grp * 64 + 64].rearrange("p (a h) -> p a h", h=16)
                                for a in range(4):
                                    s.dve(REC("tensor_copy",
                                        out=BT[prt, ri * 8 + grp * 4 + a, a * 32 + half * 16: a * 32 + half * 16 + 16],
                                        in_=bb[prt, ri * 128 + (grp * 4 + a) * 16: ri * 128 + (grp * 4 + a) * 16 + 16]), r=["bb", "BT"], w=["BT"])
                                    if ri == 0:
                                        s.pool(REC("tensor_copy",
                                            out=CB[prt, grp * 4 + a, a * 32 + half * 16: a * 32 + half * 16 + 16],
                                            in_=cst[prt, (grp * 4 + a) * 16:(grp * 4 + a) * 16 + 16]), r=["cst", "CB"], w=["CB"])
                                    else:
                                        s.dve(REC("tensor_scalar",
                                            out=CB[prt, 8 + grp * 4 + a, a * 32 + half * 16: a * 32 + half * 16 + 16],
                                            in0=cst[prt, 128 + (grp * 4 + a) * 16:128 + (grp * 4 + a) * 16 + 16], scalar1=-1.0, scalar2=None, op0=ALU.mult), r=["cst", "CB"], w=["CB"])
                if 'tr' not in os.environ.get('KA_SKIP', ''):
                    for t in range(16):
                        pt, pk = nps()
                        s.pe(REC("transpose", pt[:, 0:128], BT[:, t, :], ident_f), r=["BT", "cm"], w=list(pk))
                        s.act(REC("activation", out=BB[:, t, :], in_=pt[:, 0:128], func=AF.Copy), r=list(pk), w=["BB"])
                rt = sp2([128, TB], F32, "rt")
                rti = sp2([128, TB], I32, "rti")
                rtf = sp2([128, TB], F32, "rtf")
                rto = [sp2([128, TB], F32, "rto%d" % i) for i in range(2)]
                if 'rot' not in os.environ.get('KA_SKIP', ''):
                    for p in range(8):
                        for cs in range(2):
                            ko = "rto%d" % cs
                            s.dve(REC("tensor_scalar", out=rt[:], in0=iota_f, scalar1=sp_[:, THR * 8 + p:THR * 8 + p + 1], scalar2=(0.25 if cs == 0 else 0.0), op0=ALU.mult, op1=ALU.add),
                                  r=["cm"] + K5, w=["rt"])
                            s.dve(REC("tensor_copy", out=rti[:], in_=rt[:]), r=["rt"], w=["rti"])
                            s.dve(REC("tensor_copy", out=rtf[:], in_=rti[:]), r=["rti"], w=["rtf"])
                            s.dve(REC("tensor_tensor", out=rt[:], in0=rt[:], in1=rtf[:], op=ALU.subtract), r=["rt", "rtf"], w=["rt"])
                            s.act(REC("activation", out=rto[cs][:], in_=rt[:], func=AF.Sin, scale=6.28318), r=["rt"], w=[ko])
                            s.dma(REC("dma_start", out=rotd[:, (p * 2 + cs) * TB:(p * 2 + cs + 1) * TB], in_=rto[cs][:]), r=[ko], w=["rotd%d" % p])
                s.barrier()
                scP.close()
                carry = sa([128, 16], F32, "carry")
                s.dve(REC("memset", carry[:], 0.0), w=["carryR%d" % p for p in range(8)] + ["carryI%d" % p for p in range(8)])
                gbuf = sa([128, 2, 32 + TB], BF16, "gbuf")
                s.pool(REC("memset", gbuf[:], 0.0), w=["gbuf0", "gbuf1"])
                szA = sa([128, 4, TB], BF16, "szA")
                sg = [sa([128, TB], F32, "sg%d" % i) for i in range(2)]
                hcf = sa([128, 2, TB], F32, "hcf")
                hcb = sa([128, 2, TB], BF16, "hcb")
                hsq = sa([128, 2, TB], BF16, "hsq")
                mean_sb = sa([128, TB], F32, "mean_sb")
                var_sb = sa([128, TB], F32, "var_sb")
                hnb = sa([128, 2, TB], BF16, "hnb")
                ysA = sa([128, 4, TB], BF16, "ysA")
                uf = sa([128, 2, TB], F32, "uf")
                ub = sa([128, 2, TB], BF16, "ub")
                rot = [sa([128, 2, TB], F32, "rot%d" % i) for i in range(4)]
                rotc = [0, 0]
                w5 = [[sa([128, TB], F32, "w5_%d_%d" % (i, j)) for j in range(6)] for i in range(2)]
                hb = sa([128, 16, TB], BF16, "hb")
                ygf = sa([128, 2, TB], F32, "ygf")
                ygb = sa([128, 2, TB], BF16, "ygb")
                sgl = sa([128, 2, TB], F32, "sgl")
                tmpc = sa([128, 2, TB], F32, "tmpc")

                xbA = [xb, sa([128, 8, TB], BF16, "xbA2")]
                szAs = [szA, sa([128, 4, TB], BF16, "szA2")]
                ufs = [uf, sa([128, 2, TB], F32, "uf2")]
                ubs = [ub, sa([128, 2, TB], BF16, "ub2")]
                ysAs = [ysA, sa([128, 4, TB], BF16, "ysA2")]

                if l == 0 and os.environ.get('KDBG'):
                    print('SBUF remaining after sweep A allocs', nc.sbuf_bytes_remaining)

                NROT[0] = 4

                def blockA(b):
                    xb = xbA[b % 2]
                    szA = szAs[b % 2]
                    uf = ufs[b % 2]
                    ub = ubs[b % 2]
                    ysA = ysAs[b % 2]
                    load_x(l, b, xb, act_only=True)
                    yield
                    c0 = b * TB
                    def proj(mt, wt=wA):
                        pt, pk = nps()
                        mm(pt, pk, [(wt[:, k, mt * 128:(mt + 1) * 128], xb[:, k, :], [wkey("wA", mt * 128), "xb%d" % k]) for k in range(8)])
                        return pt, pk
                    for c in range(2):
                        pz, kz = proj(4 + c)
                        s.act(REC("activation", out=szA[:, c, :], in_=pz, func=AF.Silu), r=list(kz), w=["szA%d" % c])
                        yield
                        pz2, kz2 = proj(8 + c)
                        s.act(REC("activation", out=szA[:, 2 + c, :], in_=pz2, func=AF.Silu), r=list(kz2), w=["szA%d" % (2 + c)])
                        yield
                        pu, ku = proj(6 + c)
                        if 'ufa' not in os.environ.get('KA_SKIP', ''):
                            s.act(REC("activation", out=uf[:, c, :], in_=pu, func=AF.Copy), r=list(ku), w=["uf%d" % c])
                        if 'ubd' not in os.environ.get('KA_SKIP', ''):
                            s.act(REC("activation", out=ub[:, c, :], in_=uf[:, c, :], func=AF.Copy), r=["uf%d" % c], w=["ub%d" % c])
                        yield
                    yield ("wait", "conv")
                    for c in range(2):
                        pg, kg = proj(2 + c)
                        s.act(REC("activation", out=sg[c][:], in_=pg, func=AF.Sigmoid), r=list(kg), w=["sg%d" % c])
                        pv, kv = proj(c)
                        if 'glu' not in os.environ.get('KA_SKIP', ''):
                            s.dve(REC("tensor_tensor", out=gbuf[:, c, 32:32 + TB], in0=pv, in1=sg[c][:], op=ALU.mult), r=list(kv) + ["sg%d" % c], w=["gbuf%d" % c])
                        yield
                    if 'conv' not in os.environ.get('KA_SKIP', ''):
                        pcs = []
                        for c in range(2):
                            pc, kc = nps()
                            mm(pc, kc, [(dwt[:, c * 31 + j, :], gbuf[:, c, 2 + j:2 + j + TB], ["dwt", "gbuf%d" % c]) for j in range(31)])
                            pcs.append((pc, kc))
                            s.act(REC("activation", out=hcf[:, c, :], in_=pc, func=AF.Identity, bias=V(V_CONVB + c)), r=list(kc) + ["vec"], w=["hcf%d" % c])
                            s.act(REC("activation", out=hsq[:, c, :], in_=pc, func=AF.Square, bias=V(V_CONVB + c)), r=list(kc) + ["vec"], w=["hsq%d" % c])
                            s.act(REC("activation", out=hcb[:, c, :], in_=hcf[:, c, :], func=AF.Copy), r=["hcf%d" % c], w=["hcb%d" % c])
                            s.pool(REC("tensor_copy", out=gbuf[:, c, 0:32], in_=gbuf[:, c, TB:TB + 32]), r=["gbuf%d" % c], w=["gbuf%d" % c])
                            yield
                        pm, km = nps()
                        mm(pm, km, [(ones_b[:], hcb[:, c, :], ["ones", "hcb%d" % c]) for c in range(2)])
                        pq, kq = nps()
                        mm(pq, kq, [(ones_b[:], hsq[:, c, :], ["ones", "hsq%d" % c]) for c in range(2)])
                        s.act(REC("activation", out=mean_sb[:], in_=pm, func=AF.Copy, scale=1.0 / 256), r=list(km), w=["mean_sb"])
                        s.dve(REC("tensor_tensor", out=var_sb[:], in0=mean_sb[:], in1=mean_sb[:], op=ALU.mult), r=["mean_sb"], w=["var_sb"])
                        s.dve(REC("scalar_tensor_tensor", out=var_sb[:], in0=pq, scalar=1.0 / 256, in1=var_sb[:], op0=ALU.mult, op1=ALU.subtract), r=list(kq) + ["var_sb"], w=["var_sb"])
                        rsqrt(var_sb[:], var_sb[:], 1.0, LN_EPS, ["var_sb"], ["var_sb"])
                        for c in range(2):
                            s.dve(REC("tensor_tensor", out=hcf[:, c, :], in0=hcf[:, c, :], in1=mean_sb[:], op=ALU.subtract), r=["hcf%d" % c, "mean_sb"], w=["hcf%d" % c])
                            s.dve(REC("tensor_tensor", out=hcf[:, c, :], in0=hcf[:, c, :], in1=var_sb[:], op=ALU.mult), r=["hcf%d" % c, "var_sb"], w=["hcf%d" % c])
                            s.act(REC("activation", out=hnb[:, c, :], in_=hcf[:, c, :], func=AF.Silu, scale=V(V_CONVG + c), bias=V(V_CONVBETA + c)), r=["hcf%d" % c, "vec"], w=["hnb%d" % c])
                        for m in range(2):
                            py, ky = nps()
                            mm(py, ky, [(wp2[:, k, m * 128:(m + 1) * 128], hnb[:, k, :], [wkey("wp2"), "hnb%d" % k]) for k in range(2)])
                            s.dve(REC("tensor_tensor", out=ysA[:, m, :], in0=py, in1=szA[:, m, :], op=ALU.mult), r=list(ky) + ["szA%d" % m], w=["ysA%d" % m])
                            yield
                    yield ("done", "conv")
                    yield ("wait", "cmat")
                    if 's5' not in os.environ.get('KA_SKIP', ''):
                        for p in range(8):
                            eng = s.dve if p % 2 == 0 else s.pool
                            ei = p % 2
                            W_ = w5[ei]
                            WK = ["w5_%d_%d" % (ei, j) for j in range(6)]
                            ri_ = ei * 2 + rotc[ei] % 2
                            rotc[ei] += 1
                            rk = "rot%d" % ri_
                            s.dma(REC("dma_start", out=rot[ri_][:, :, :], in_=rotd[:, p * 2 * TB:(p * 2 + 2) * TB].rearrange("p (c t) -> p c t", c=2)), r=["rotd%d" % p], w=[rk])
                            Cc = rot[ri_][:, 0, :]
                            Ss = rot[ri_][:, 1, :]
                            kt = p // 4
                            (pr, kr), (pi_, ki) = ns5()
                            mm(pr, kr, [(BB[:, p, :], ub[:, kt, :], ["BB", "ub%d" % kt])])
                            mm(pi_, ki, [(BB[:, 8 + p, :], ub[:, kt, :], ["BB", "ub%d" % kt])])
                            if ei == 1:
                                s.act(REC("activation", out=W_[4][:], in_=pr, func=AF.Copy), r=list(kr), w=[WK[4]])
                                s.act(REC("activation", out=W_[5][:], in_=pi_, func=AF.Copy), r=list(ki), w=[WK[5]])
                                pr, kr = W_[4][:], [WK[4]]
                                pi_, ki = W_[5][:], [WK[5]]
                            eng(REC("tensor_tensor", out=W_[0][:], in0=pr, in1=Cc, op=ALU.mult), r=list(kr) + [rk], w=[WK[0]])
                            eng(REC("tensor_tensor", out=W_[1][:], in0=pi_, in1=Ss, op=ALU.mult), r=list(ki) + [rk], w=[WK[1]])
                            eng(REC("tensor_tensor", out=W_[0][:], in0=W_[0][:], in1=W_[1][:], op=ALU.add), r=[WK[0], WK[1]], w=[WK[0]])
                            eng(REC("tensor_tensor", out=W_[2][:], in0=pi_, in1=Cc, op=ALU.mult), r=list(ki) + [rk], w=[WK[2]])
                            eng(REC("tensor_tensor", out=W_[3][:], in0=pr, in1=Ss, op=ALU.mult), r=list(kr) + [rk], w=[WK[3]])
                            eng(REC("tensor_tensor", out=W_[2][:], in0=W_[2][:], in1=W_[3][:], op=ALU.subtract), r=[WK[2], WK[3]], w=[WK[2]])
                            magb = sp_[:, MAG * 8 + p:MAG * 8 + p + 1].broadcast_to([128, TB])
                            s.dve(REC("tensor_tensor_scan", out=W_[4][:], data0=magb, data1=W_[0][:], initial=carry[:, p:p + 1], op0=ALU.mult, op1=ALU.add), r=[WK[0], "carryR%d" % p] + K5, w=[WK[4]])
                            s.dve(REC("tensor_tensor_scan", out=W_[5][:], data0=magb, data1=W_[2][:], initial=carry[:, 8 + p:9 + p], op0=ALU.mult, op1=ALU.add), r=[WK[2], "carryI%d" % p] + K5, w=[WK[5]])
                            yield
                            eng(REC("tensor_tensor", out=W_[0][:], in0=W_[4][:], in1=Cc, op=ALU.mult), r=[WK[4], rk], w=[WK[0]])
                            eng(REC("tensor_tensor", out=W_[1][:], in0=W_[5][:], in1=Ss, op=ALU.mult), r=[WK[5], rk], w=[WK[1]])
                            eng(REC("tensor_tensor", out=W_[0][:], in0=W_[0][:], in1=W_[1][:], op=ALU.subtract), r=[WK[0], WK[1]], w=[WK[0]])
                            eng(REC("tensor_tensor", out=W_[2][:], in0=W_[5][:], in1=Cc, op=ALU.mult), r=[WK[5], rk], w=[WK[2]])
                            eng(REC("tensor_tensor", out=W_[3][:], in0=W_[4][:], in1=Ss, op=ALU.mult), r=[WK[4], rk], w=[WK[3]])
                            eng(REC("tensor_tensor", out=W_[2][:], in0=W_[2][:], in1=W_[3][:], op=ALU.add), r=[WK[2], WK[3]], w=[WK[2]])
                            eng(REC("tensor_copy", out=carry[:, p:p + 1], in_=W_[0][:, TB - 1:TB]), r=[WK[0]], w=["carryR%d" % p])
                            eng(REC("tensor_copy", out=carry[:, 8 + p:9 + p], in_=W_[2][:, TB - 1:TB]), r=[WK[2]], w=["carryI%d" % p])
                            s.act(REC("activation", out=hb[:, p, :], in_=W_[0][:], func=AF.Copy), r=[WK[0]], w=["hb%d" % p])
                            s.act(REC("activation", out=hb[:, 8 + p, :], in_=W_[2][:], func=AF.Copy), r=[WK[2]], w=["hb%d" % (8 + p)])
                            yield
                        for m in range(2):
                            pyc, kyc = nps()
                            prs = []
                            for p in range(4 * m, 4 * m + 4):
                                prs.append((CB[:, p, :], hb[:, p, :], ["CB", "hb%d" % p]))
                                prs.append((CB[:, 8 + p, :], hb[:, 8 + p, :], ["CB", "hb%d" % (8 + p)]))
                            mm(pyc, kyc, prs)
                            s.dve(REC("scalar_tensor_tensor", out=ygf[:, m, :], in0=uf[:, m, :], scalar=V(V_SSMD + m), in1=pyc, op0=ALU.mult, op1=ALU.add), r=list(kyc) + ["uf%d" % m, "vec"], w=["ygf%d" % m])
                            yield
                        yield ("done", "cmat")
                        for m in range(2):
                            s.act(REC("activation", out=tmpc[:, m, :], in_=ygf[:, m, :], func=AF.Square), r=["ygf%d" % m], w=["tmpc%d" % m])
                            s.dve(REC("tensor_scalar", out=tmpc[:, m, :], in0=tmpc[:, m, :], scalar1=0.044715, scalar2=1.0, op0=ALU.mult, op1=ALU.add), r=["tmpc%d" % m], w=["tmpc%d" % m])
                            s.dve(REC("tensor_tensor", out=tmpc[:, m, :], in0=tmpc[:, m, :], in1=ygf[:, m, :], op=ALU.mult), r=["tmpc%d" % m, "ygf%d" % m], w=["tmpc%d" % m])
                            yield
                            s.act(REC("activation", out=sgl[:, m, :], in_=tmpc[:, m, :], func=AF.Sigmoid, scale=1.5957691216057308), r=["tmpc%d" % m], w=["sgl%d" % m])
                            s.pool(REC("tensor_tensor", out=ygb[:, m, :], in0=ygf[:, m, :], in1=sgl[:, m, :], op=ALU.mult), r=["ygf%d" % m, "sgl%d" % m], w=["ygb%d" % m])
                            yield
                        for m in range(2):
                            p2, k2 = nps()
                            mm(p2, k2, [(wgl[:, k, (2 + m) * 128:(3 + m) * 128], ygb[:, k, :], [wkey("wgl"), "ygb%d" % k]) for k in range(2)])
                            s.act(REC("activation", out=sgl[:, m, :], in_=p2, func=AF.Sigmoid), r=list(k2), w=["sgl%d" % m])
                            yield
                            p1, k1 = nps()
                            mm(p1, k1, [(wgl[:, k, m * 128:(m + 1) * 128], ygb[:, k, :], [wkey("wgl"), "ygb%d" % k]) for k in range(2)])
                            s.dve(REC("tensor_tensor", out=tmpc[:, m, :], in0=p1, in1=sgl[:, m, :], op=ALU.mult), r=list(k1) + ["sgl%d" % m], w=["tmpc%d" % m])
                            yield
                            s.pool(REC("tensor_tensor", out=ysA[:, 2 + m, :], in0=tmpc[:, m, :], in1=szA[:, 2 + m, :], op=ALU.mult), r=["tmpc%d" % m, "szA%d" % (2 + m)], w=["ysA%d" % (2 + m)])
                            yield
                    for m in range(4):
                        row = (m if m < 2 else 2 + m) * 128
                        s.dma(REC("dma_start", out=ysd[row:row + 128, c0:c0 + TB], in_=ysA[:, m, :]), r=["ysA%d" % m], w=["ysd_%d_%d" % (b, row // 128)])
                    yield

                run_pipe(blockA, NB, int(os.environ.get('KLAG_A', '4')), ("xb", "szA", "uf", "ub", "ysA"))
                NROT[0] = 5

            s.barrier()
            scB = ExitStack()
            with scB:
              if 'B' in SW:
                def sbb(shape, dt=F32, name=None):
                    uid[0] += 1
                    return scB.enter_context(nc.sbuf_tensor("%s_%d" % (name or "b", uid[0]), list(shape), dt))
                wB = sbb([128, 8, 13 * 128], BF16, "wB")
                load_w(wB, w1b[l], D, "wB", bounds=[0, 640, 1664])
                wq = sbb([128, 2, 1024], BF16, "wq")
                load_w(wq, wuq[l], 256, "wq")
                wkv = sbb([128, 1, 512], BF16, "wkv")
                load_w(wkv, wukv[l], 128, "wkv")
                Kc = sbb([128, 4, S], BF16, "Kc")
                Vc = sbb([128, NT, 4, 128], BF16, "Vc")
                s.pool(REC("memset", Vc[:], 1.0), w=["Vc"] + ["Vc_%d" % t for t in range(NT)])
                Ksw = sbb([128, 128 + TB], BF16, "Ksw")
                s.pool(REC("memset", Ksw[:], 0.0), w=["Ksw"])
                Vsw = sbb([128, 5, 2, 128], BF16, "Vsw")
                s.pool(REC("memset", Vsw[:], 1.0), w=["Vsw"])
                esk = sbb([128, 4], F32, "esk")
                s.act(REC("activation", out=esk[:], in_=V(V_SINK, 4), func=AF.Exp), r=["vec"], w=["esk"])
                eskf = sbb([128, 4, 128], F32, "eskf")
                s.dve(REC("tensor_copy", out=eskf[:], in_=esk[:, :].unsqueeze(2).broadcast_to([128, 4, 128])), r=["esk"], w=["eskf"])
                szB = sbb([128, 4, TB], BF16, "szB")
                cqf = sbb([128, 3, TB], F32, "cqf")
                cqs = sbb([128, 3, TB], BF16, "cqs")
                cqn = sbb([128, 3, TB], BF16, "cqn")
                rstd = [sbb([128, TB], F32, "rstd%d" % i) for i in range(2)]
                rc = sbb([128, TB], F32, "rc")
                rs_ = sbb([128, TB], F32, "rs")
                t1 = sbb([128, TB], F32, "t1")
                t2 = sbb([128, TB], F32, "t2")
                Qh = sbb([128, 4, TB], BF16, "Qh")
                qsw = sbb([128, 2, TB], BF16, "qsw")
                pbuf = [sbb([128, TB], BF16, "pbuf%d" % i) for i in range(3)]
                pbs = [sbb([128, 1024], BF16, "pbs%d" % i) for i in range(2)]
                rden = sbb([128, TB], F32, "rden")
                tmpo = sbb([128, TB], F32, "tmpo")
                ysB = sbb([128, 4, TB], BF16, "ysB")
                pcount = [0]

                xbB = [xb, sbb([128, 8, TB], BF16, "xbB2")]
                szBs = [szB, sbb([128, 4, TB], BF16, "szB2")]
                Qhs = [Qh, sbb([128, 4, TB], BF16, "Qh2")]
                ysBs = [ysB, sbb([128, 4, TB], BF16, "ysB2")]

                def blockB(b):
                    xb = xbB[b % 2]
                    szB = szBs[b % 2]
                    Qh = Qhs[b % 2]
                    ysB = ysBs[b % 2]
                    load_x(l, b, xb)
                    yield
                    yield ("wait", "front")
                    c0 = b * TB
                    s.dma(REC("dma_start", out=rc[:], in_=ropec[:, c0:c0 + TB]), w=["rc"])
                    s.dma(REC("dma_start", out=rs_[:], in_=ropes[:, c0:c0 + TB]), w=["rs"])

                    def projB(mt):
                        pt, pk = nps()
                        mm(pt, pk, [(wB[:, k, mt * 128:(mt + 1) * 128], xb[:, k, :], [wkey("wB", mt * 128), "xb%d" % k]) for k in range(8)])
                        return pt, pk
                    for c in range(3):
                        pc, kc = projB(c)
                        s.act(REC("activation", out=cqf[:, c, :], in_=pc, func=AF.Copy), r=list(kc), w=["cqf%d" % c])
                        s.act(REC("activation", out=cqs[:, c, :], in_=pc, func=AF.Square), r=list(kc), w=["cqs%d" % c])
                        yield
                    pm, km = nps()
                    mm(pm, km, [(ones_b[:], cqs[:, c, :], ["ones", "cqs%d" % c]) for c in range(2)])
                    rsqrt(rstd[0][:], pm, 1.0 / 256, RMS_EPS, list(km), ["rstd0"])
                    yield
                    pm2, km2 = nps()
                    mm(pm2, km2, [(ones_b[:], cqs[:, 2, :], ["ones", "cqs2"])])
                    rsqrt(rstd[1][:], pm2, 1.0 / 128, RMS_EPS, list(km2), ["rstd1"])
                    yield
                    for c in range(3):
                        gcol = V_QNG + c if c < 2 else V_KVNG
                        ri = 0 if c < 2 else 1
                        s.dve(REC("scalar_tensor_tensor", out=cqn[:, c, :], in0=cqf[:, c, :], scalar=V(gcol), in1=rstd[ri][:], op0=ALU.mult, op1=ALU.mult),
                              r=["cqf%d" % c, "rstd%d" % ri, "vec"], w=["cqn%d" % c])
                        yield
                    pa, ka = projB(3)
                    pb_, kb = projB(4)
                    R = slice(64, 96)
                    s.dve(REC("tensor_tensor", out=t1[R, :], in0=pa[R, :], in1=rc[R, :], op=ALU.mult), r=list(ka) + ["rc"], w=["t1"])
                    s.dve(REC("tensor_tensor", out=t2[R, :], in0=pb_[R, :], in1=rs_[R, :], op=ALU.mult), r=list(kb) + ["rs"], w=["t2"])
                    yield
                    for h in range(4):
                        f = s.dve if h % 2 == 0 else s.pool
                        f(REC("tensor_tensor", out=Kc[R, h, c0:c0 + TB], in0=t1[R, :], in1=t2[R, :], op=ALU.add), r=["t1", "t2"], w=["Kc_%d_%d" % (h, b)])
                    for h in range(4):
                        pk_, kk = nps()
                        mm(pk_, kk, [(wkv[:, 0, h * 64:(h + 1) * 64], cqn[:, 2, :], [wkey("wkv"), "cqn2"])])
                        s.act(REC("activation", out=Kc[0:64, h, c0:c0 + TB], in_=pk_[0:64, :], func=AF.Copy), r=list(kk), w=["Kc_%d_%d" % (h, b)])
                        yield
                    for j in range(4):
                        pv, kv = nps()
                        mm(pv[:, 0:256], kv, [(cqn[:, 2, j * 128:(j + 1) * 128], wkv[:, 0, 256:512], [wkey("wkv"), "cqn2"])])
                        s.act(REC("activation", out=Vc[:, b * 4 + j, :, 0:64], in_=pv[:, 0:256].rearrange("p (h d) -> p h d", d=64), func=AF.Copy), r=list(kv) + ["Vc"], w=["Vc_%d" % (b * 4 + j)])
                        yield
                    for h in range(4):
                        pA, kA = nps()
                        mm(pA, kA, [(wq[:, k, (2 * h) * 128:(2 * h) * 128 + 96], cqn[:, k, :], [wkey("wq"), "cqn%d" % k]) for k in range(2)])
                        pB, kB = nps()
                        mm(pB, kB, [(wq[:, k, (2 * h + 1) * 128:(2 * h + 1) * 128 + 96], cqn[:, k, :], [wkey("wq"), "cqn%d" % k]) for k in range(2)])
                        s.act(REC("activation", out=Qh[0:64, h, :], in_=pA[0:64, :], func=AF.Copy), r=list(kA), w=["Qh%d" % h])
                        s.dve(REC("tensor_tensor", out=t1[R, :], in0=pA[R, :], in1=rc[R, :], op=ALU.mult), r=list(kA) + ["rc"], w=["t1"])
                        s.dve(REC("tensor_tensor", out=t2[R, :], in0=pB[R, :], in1=rs_[R, :], op=ALU.mult), r=list(kB) + ["rs"], w=["t2"])
                        s.dve(REC("tensor_tensor", out=Qh[R, h, :], in0=t1[R, :], in1=t2[R, :], op=ALU.add), r=["t1", "t2", "Qh%d" % h], w=["Qh%d" % h])
                        yield
                    yield ("done", "front")
                    yield ("wait", "end")
                    for c in range(2):
                        pz, kz = projB(5 + c)
                        s.act(REC("activation", out=szB[:, c, :], in_=pz, func=AF.Silu), r=list(kz), w=["szB%d" % c])
                        pz2, kz2 = projB(11 + c)
                        s.act(REC("activation", out=szB[:, 2 + c, :], in_=pz2, func=AF.Silu), r=list(kz2), w=["szB%d" % (2 + c)])
                        yield
                    scale = (64 + 32) ** -0.5
                    nkt = 4 * b + 4
                    SK = 2
                    accs = {}

                    def emit_S(h, kt):
                        if kt == 0:
                            bk = 5 + (h % 2)
                            accs[h] = (psum[:, bk * 512:(bk + 1) * 512], ("ps%d" % bk,))
                        jl = kt - 4 * b
                        qs = max(0, jl) * 128
                        ps_, ks = nps()
                        kb_ = kt // 4
                        s.pe(REC("matmul", ps_[:, qs:TB], lhsT=Kc[0:96, h, kt * 128:(kt + 1) * 128], rhs=Qh[0:96, h, qs:TB], start=True, stop=True),
                             r=["Kc_%d_%d" % (h, kb_), "Qh%d" % h], w=list(ks))
                        pi = pcount[0] % 3
                        pcount[0] += 1
                        s.act(REC("activation", out=pbuf[pi][:, qs:TB], in_=ps_[:, qs:TB], func=AF.Exp, scale=float(scale)), r=list(ks), w=["pbuf%d" % pi])
                        if jl >= 0:
                            s.pool(REC("tensor_tensor", out=pbuf[pi][:, qs:qs + 128], in0=pbuf[pi][:, qs:qs + 128], in1=tri_b, op=ALU.mult), r=["pbuf%d" % pi, "cmb"], w=["pbuf%d" % pi])
                        return pi, qs

                    def emit_PV(h, kt, pi, qs):
                        po, ko = accs[h]
                        s.pe(REC("matmul", po[:, qs:TB], lhsT=Vc[:, kt, h, :], rhs=pbuf[pi][:, qs:TB], start=(kt == 0), stop=(kt == nkt - 1)),
                             r=["Vc_%d" % kt, "pbuf%d" % pi], w=list(ko))
                        if kt == nkt - 1:
                            hp = slice((h % 2) * 64, (h % 2) * 64 + 64)
                            s.dve(REC("reciprocal", out=rden[64:128, :], in_=po[64:128, :]), r=list(ko), w=["rden"])
                            s.dve(REC("tensor_tensor", out=tmpo[hp, :], in0=po[0:64, :], in1=rden[64:128, :], op=ALU.mult), r=list(ko) + ["rden"], w=["tmpo"])
                            s.pool(REC("tensor_tensor", out=ysB[hp, h // 2, :], in0=tmpo[hp, :], in1=szB[hp, h // 2, :], op=ALU.mult), r=["tmpo", "szB%d" % (h // 2)], w=["ysB%d" % (h // 2)])

                    def swa_gen():
                        for c in range(2):
                            pq_, kq_ = projB(7 + c)
                            s.act(REC("activation", out=qsw[:, c, :], in_=pq_, func=AF.Copy), r=list(kq_), w=["qsw%d" % c])
                            yield
                        pk2, kk2 = projB(9)
                        s.act(REC("activation", out=Ksw[:, 128:128 + TB], in_=pk2, func=AF.Copy), r=list(kk2), w=["Ksw"])
                        yield
                        for j in range(4):
                            pv, kv = nps()
                            mm(pv[:, 0:128], kv, [(xb[:, k, j * 128:(j + 1) * 128], wB[:, k, 10 * 128:11 * 128], [wkey("wB", 10 * 128), "xb%d" % k]) for k in range(8)])
                            s.act(REC("activation", out=Vsw[:, 1 + j, :, 0:64], in_=pv[:, 0:128].rearrange("p (g d) -> p g d", d=64), func=AF.Copy), r=list(kv), w=["Vsw"])
                            yield
                        sscale = 64 ** -0.5
                        for h in range(4):
                            g = h // 2
                            qt_ = h % 2
                            gp = slice(g * 64, g * 64 + 64)
                            p2b, k2b = nps(2)
                            for i in range(4):
                                for wch in range(2):
                                    col = (i * 2 + wch) * 128
                                    kcol = (i + wch) * 128
                                    s.pe(REC("matmul", p2b[:, col:col + 128], lhsT=Ksw[gp, kcol:kcol + 128], rhs=qsw[gp, qt_, i * 128:(i + 1) * 128], start=True, stop=True),
                                         r=["Ksw", "qsw%d" % qt_], w=list(k2b))
                            pi = h % 2
                            for hh in range(2):
                                s.act(REC("activation", out=pbs[pi][:, hh * 512:(hh + 1) * 512], in_=p2b[:, hh * 512:(hh + 1) * 512], func=AF.Exp, scale=float(sscale)), r=list(k2b), w=["pbs%d" % pi])
                            s.pool(REC("tensor_tensor", out=pbs[pi][:, :].rearrange("p (i m) -> p i m", m=256), in0=pbs[pi][:, :].rearrange("p (i m) -> p i m", m=256),
                                                                in1=swm_b.unsqueeze(1).broadcast_to([128, 4, 256]), op=ALU.mult), r=["pbs%d" % pi, "cmb"], w=["pbs%d" % pi])
                            po, ko = psum[:, 7 * 512:8 * 512], ("ps7",)
                            for i in range(4):
                                first = True
                                for wch in range(2):
                                    if b == 0 and i == 0 and wch == 0:
                                        continue
                                    col = (i * 2 + wch) * 128
                                    s.pe(REC("matmul", po[:, i * 128:(i + 1) * 128], lhsT=Vsw[:, i + wch, g, :], rhs=pbs[pi][:, col:col + 128], start=first, stop=(wch == 1)),
                                         r=["Vsw", "pbs%d" % pi], w=list(ko))
                                    first = False
                            hp = slice((h % 2) * 64, (h % 2) * 64 + 64)
                            s.dve(REC("tensor_tensor", out=rden[64:128, :].rearrange("p (i m) -> p i m", m=128), in0=po[64:128, :].rearrange("p (i m) -> p i m", m=128),
                                                                    in1=eskf[64:128, h:h + 1, :].broadcast_to([64, 4, 128]), op=ALU.add), r=list(ko) + ["eskf"], w=["rden"])
                            s.dve(REC("reciprocal", out=rden[64:128, :], in_=rden[64:128, :]), r=["rden"], w=["rden"])
                            s.dve(REC("tensor_tensor", out=tmpo[hp, :], in0=po[0:64, :], in1=rden[64:128, :], op=ALU.mult), r=list(ko) + ["rden"], w=["tmpo"])
                            s.pool(REC("tensor_tensor", out=ysB[hp, 2 + h // 2, :], in0=tmpo[hp, :], in1=szB[hp, 2 + h // 2, :], op=ALU.mult), r=["tmpo", "szB%d" % (2 + h // 2)], w=["ysB%d" % (2 + h // 2)])
                            yield

                    sg_ = swa_gen()
                    n_items = 4 * nkt
                    every = max(1, n_items // 20)
                    fifo = []
                    it = 0
                    for h in range(4):
                        for kt in range(nkt):
                            fifo.append((h, kt) + emit_S(h, kt))
                            if len(fifo) > SK:
                                emit_PV(*fifo.pop(0))
                            it += 1
                            if it % every == 0:
                                next(sg_, None)
                            yield
                    while fifo:
                        emit_PV(*fifo.pop(0))
                        yield
                    for _ in sg_:
                        yield
                    s.pool(REC("tensor_copy", out=Ksw[:, 0:128], in_=Ksw[:, TB:TB + 128]), r=["Ksw"], w=["Ksw"])
                    s.pool(REC("tensor_copy", out=Vsw[:, 0, :, 0:64], in_=Vsw[:, 4, :, 0:64]), r=["Vsw"], w=["Vsw"])
                    for m in range(4):
                        row = (2 + m if m < 2 else 4 + m) * 128
                        s.dma(REC("dma_start", out=ysd[row:row + 128, c0:c0 + TB], in_=ysB[:, m, :]), r=["ysB%d" % m], w=["ysd_%d_%d" % (b, row // 128)])
                    yield ("done", "end")

                run_pipe(blockB, NB, 1, ("xb", "szB", "Qh", "ysB"))

            s.barrier()
            scC = ExitStack()
            with scC:
              if 'C1' in SW:
                def sc(shape, dt=F32, name=None):
                    uid[0] += 1
                    return scC.enter_context(nc.sbuf_tensor("%s_%d" % (name or "c", uid[0]), list(shape), dt))
                wm = sc([128, 8, 4 * D], BF16, "wm")
                load_w(wm, wmerge[l], D, "wm", bounds=[0, 1024, 2048, 3072, 4096], order=[0])
                wbr = sc([128, 8, D], BF16, "wbr")
                load_w(wbr, wbranch[l], 1024, "wbr", by_k=True)
                load_w(wm, wmerge[l], D, "wm", bounds=[0, 1024, 2048, 3072, 4096], order=[1, 2, 3])
                ysl = sc([128, 8, TB], BF16, "ysl")
                gs = [sc([128, TB], F32, "gs%d" % i) for i in range(2)]
                tp = [sc([128, TB], F32, "tp%d" % i) for i in range(2)]
                mg = sc([128, 8, TB], F32, "mg")
                mgb = sc([128, 8, TB], BF16, "mgb")
                cnt = [0]
                for b in range(NB):
                    load_x(l, b)
                    c0 = b * TB
                    for t in range(8):
                        s.dma(REC("dma_start", out=ysl[:, t, :], in_=ysd[t * 128:(t + 1) * 128, c0:c0 + TB]), r=["ysd_%d_%d" % (b, t)], w=["ysl%d" % t])
                    for n in range(4):
                        for m in range(8):
                            i = cnt[0] % 2
                            cnt[0] += 1
                            pg, kg = nps()
                            mm(pg, kg, [(wm[:, k, n * D + m * 128:n * D + (m + 1) * 128], xb[:, k, :], [wkey("wm", n * D + m * 128), "xb%d" % k]) for k in range(8)])
                            s.act(REC("activation", out=gs[i][:], in_=pg, func=AF.Sigmoid, bias=V(V_BMERGE + n * 8 + m)), r=list(kg) + ["vec"], w=["gs%d" % i])
                            pb2, kb2 = nps()
                            mm(pb2, kb2, [(wbr[:, 2 * n + k, m * 128:(m + 1) * 128], ysl[:, 2 * n + k, :], [wkey("wbr", kt=2 * n + k), "ysl%d" % (2 * n + k)]) for k in range(2)])
                            if n == 0:
                                s.dve(REC("tensor_tensor", out=mg[:, m, :], in0=pb2, in1=gs[i][:], op=ALU.mult), r=list(kb2) + ["gs%d" % i], w=["mg%d" % m])
                            else:
                                s.dve(REC("tensor_tensor", out=tp[i][:], in0=pb2, in1=gs[i][:], op=ALU.mult), r=list(kb2) + ["gs%d" % i], w=["tp%d" % i])
                                if n < 3:
                                    s.pool(REC("tensor_tensor", out=mg[:, m, :], in0=mg[:, m, :], in1=tp[i][:], op=ALU.add), r=["mg%d" % m, "tp%d" % i], w=["mg%d" % m])
                                else:
                                    s.pool(REC("tensor_tensor", out=mgb[:, m, :], in0=mg[:, m, :], in1=tp[i][:], op=ALU.add), r=["mg%d" % m, "tp%d" % i], w=["mgb%d" % m])
                                    s.dma(REC("dma_start", out=mgd[m * 128:(m + 1) * 128, c0:c0 + TB], in_=mgb[:, m, :]), r=["mgb%d" % m], w=["mgd_%d_%d" % (b, m)])

            s.barrier()
            scD = ExitStack()
            with scD:
              if 'C2' in SW:
                def sd(shape, dt=F32, name=None):
                    uid[0] += 1
                    return scD.enter_context(nc.sbuf_tensor("%s_%d" % (name or "d", uid[0]), list(shape), dt))
                wo = sd([128, 8, D], BF16, "wo")
                load_w(wo, wout[l], D, "wo", bounds=[0, 256, 1024])
                wpg = sd([128, 8, D], BF16, "wpg")
                wpl = sd([128, 2, D], BF16, "wpl")
                load_w(wpl, wple[l], 256, "wpl")
                load_w(wpg, wpleg[l], D, "wpg")
                mgl = sd([128, 8, TB], BF16, "mgl")
                pb16 = sd([128, 2, TB], BF16, "pb16")
                zb = [sd([128, TB], BF16, "zb%d" % i) for i in range(2)]
                zq = [sd([128, TB], BF16, "zq%d" % i) for i in range(2)]
                mean2 = sd([128, TB], F32, "mean2")
                var2 = sd([128, TB], F32, "var2")
                xlb = sd([128, 8, TB], BF16, "xlb")
                ef = sd([128, 8, TB], F32, "ef")
                sge = [sd([128, TB], F32, "sge%d" % i) for i in range(2)]
                rse = sd([128, TB], F32, "rse")
                xo = [sd([128, TB], F32, "xo%d" % i) for i in range(2)]
                alpha = (2.0 * 4) ** 0.25
                dst = yT if l == L - 1 else xs
                xfs = [xf, sd([128, 8, TB], F32, "xfD2")]
                mgls = [mgl, sd([128, 8, TB], BF16, "mgl2")]
                pb16s = [pb16, sd([128, 2, TB], BF16, "pb16b")]
                xlbs = [xlb, sd([128, 8, TB], BF16, "xlb2")]
                efs = [ef, sd([128, 8, TB], F32, "ef2")]
                mean2s = [mean2, sd([128, TB], F32, "mean2b")]
                var2s = [var2, sd([128, TB], F32, "var2b")]
                rses = [rse, sd([128, TB], F32, "rseb")]

                def blockD(b):
                    xf = xfs[b % 2]
                    mgl = mgls[b % 2]
                    pb16 = pb16s[b % 2]
                    xlb = xlbs[b % 2]
                    ef = efs[b % 2]
                    mean2 = mean2s[b % 2]
                    var2 = var2s[b % 2]
                    rse = rses[b % 2]
                    load_x(l, b, xf=xf, cast=False)
                    c0 = b * TB
                    for t in range(8):
                        s.dma(REC("dma_start", out=mgl[:, t, :], in_=mgd[t * 128:(t + 1) * 128, c0:c0 + TB]), r=["mgd_%d_%d" % (b, t)], w=["mgl%d" % t])
                    for t in range(2):
                        s.dma(REC("dma_start", out=pb16[:, t, :], in_=pT[l, t * 128:(t + 1) * 128, c0:c0 + TB]), w=["pb16_%d" % t], q="pool")
                    yield
                    pmean, kmean = nacc()
                    pmsq, kmsq = nacc()
                    for m in range(8):
                        pz, kz = nps()
                        mm(pz, kz, [(wo[:, k, m * 128:(m + 1) * 128], mgl[:, k, :], [wkey("wo", m * 128), "mgl%d" % k]) for k in range(8)])
                        s.dve(REC("scalar_tensor_tensor", out=xf[:, m, :], in0=xf[:, m, :], scalar=float(alpha), in1=pz, op0=ALU.mult, op1=ALU.add), r=list(kz) + ["xf%d" % m], w=["xf%d" % m])
                        i = m % 2
                        s.act(REC("activation", out=zb[i][:], in_=xf[:, m, :], func=AF.Copy), r=["xf%d" % m], w=["zb%d" % i])
                        s.act(REC("activation", out=zq[i][:], in_=xf[:, m, :], func=AF.Square), r=["xf%d" % m], w=["zq%d" % i])
                        s.pe(REC("matmul", pmean, lhsT=ones_b[:], rhs=zb[i][:], start=(m == 0), stop=(m == 7)), r=["ones", "zb%d" % i], w=list(kmean))
                        s.pe(REC("matmul", pmsq, lhsT=ones_b[:], rhs=zq[i][:], start=(m == 0), stop=(m == 7)), r=["ones", "zq%d" % i], w=list(kmsq))
                        yield
                    s.act(REC("activation", out=mean2[:], in_=pmean, func=AF.Copy, scale=1.0 / D), r=list(kmean), w=["mean2"])
                    s.dve(REC("tensor_tensor", out=var2[:], in0=mean2[:], in1=mean2[:], op=ALU.mult), r=["mean2"], w=["var2"])
                    s.dve(REC("scalar_tensor_tensor", out=var2[:], in0=pmsq, scalar=1.0 / D, in1=var2[:], op0=ALU.mult, op1=ALU.subtract), r=list(kmsq) + ["var2"], w=["var2"])
                    rsqrt(var2[:], var2[:], 1.0, LN_EPS, ["var2"], ["var2"])
                    yield
                    for m in range(8):
                        f = s.dve if m % 2 == 0 else s.pool
                        f(REC("tensor_tensor", out=xf[:, m, :], in0=xf[:, m, :], in1=mean2[:], op=ALU.subtract), r=["xf%d" % m, "mean2"], w=["xf%d" % m])
                        f(REC("tensor_tensor", out=xf[:, m, :], in0=xf[:, m, :], in1=var2[:], op=ALU.mult), r=["xf%d" % m, "var2"], w=["xf%d" % m])
                        s.act(REC("activation", out=xf[:, m, :], in_=xf[:, m, :], func=AF.Identity, scale=V(V_LNG + m), bias=V(V_LNB + m)), r=["xf%d" % m, "vec"], w=["xf%d" % m])
                        s.act(REC("activation", out=xlb[:, m, :], in_=xf[:, m, :], func=AF.Copy), r=["xf%d" % m], w=["xlb%d" % m])
                        yield
                    pms, kms = nacc()
                    for m in range(8):
                        i = m % 2
                        pgt, kgt = nps()
                        mm(pgt, kgt, [(wpg[:, k, m * 128:(m + 1) * 128], xlb[:, k, :], [wkey("wpg"), "xlb%d" % k]) for k in range(8)])
                        s.act(REC("activation", out=sge[i][:], in_=pgt, func=AF.Sigmoid), r=list(kgt), w=["sge%d" % i])
                        ppe, kpe = nps()
                        mm(ppe, kpe, [(wpl[:, k, m * 128:(m + 1) * 128], pb16[:, k, :], [wkey("wpl"), "pb16_%d" % k]) for k in range(2)])
                        s.dve(REC("tensor_tensor", out=ef[:, m, :], in0=ppe, in1=sge[i][:], op=ALU.mult), r=list(kpe) + ["sge%d" % i], w=["ef%d" % m])
                        s.act(REC("activation", out=zq[i][:], in_=ef[:, m, :], func=AF.Square), r=["ef%d" % m], w=["zq%d" % i])
                        s.pe(REC("matmul", pms, lhsT=ones_b[:], rhs=zq[i][:], start=(m == 0), stop=(m == 7)), r=["ones", "zq%d" % i], w=list(kms))
                        yield
                    rsqrt(rse[:], pms, 1.0 / D, RMS_EPS, list(kms), ["rse"])
                    yield
                    for m in range(8):
                        i = m % 2
                        f = s.dve if m % 2 == 0 else s.pool
                        s.dve(REC("scalar_tensor_tensor", out=ef[:, m, :], in0=ef[:, m, :], scalar=V(V_PLEG + m), in1=rse[:], op0=ALU.mult, op1=ALU.mult), r=["ef%d" % m, "rse", "vec"], w=["ef%d" % m])
                        f(REC("tensor_tensor", out=xo[i][:], in0=ef[:, m, :], in1=xf[:, m, :], op=ALU.add), r=["ef%d" % m, "xf%d" % m], w=["xo%d" % i])
                        s.dma(REC("dma_start", out=dst[m * 128:(m + 1) * 128, c0:c0 + TB], in_=xo[i][:]), r=["xo%d" % i], w=["xs_%d" % b], final=(l == L - 1))
                        yield
                run_pipe(blockD, NB, int(os.environ.get('KLAG_D', '11')), ("xf", "mgl", "pb", "xlb", "ef", "mean", "var", "rse"))
            s.barrier()
        s.emit(st)
    return nc


def pack_weights(inp, L, S):
    f = np.float32
    W = {}
    w_in = np.asarray(inp["w_in"], f)
    offs = np.cumsum([0, 256, 256, 256, 256, 128, 32, 256, 256, 256, 256, 128, 128, 256])
    seg = {n: (offs[i], offs[i + 1]) for i, n in enumerate(
        ["a_val", "a_gate", "a_z", "c_q", "c_kv", "k_r", "b_z", "u", "c_z", "q", "k", "v", "d_z"])}

    def cols(n):
        a, b = seg[n]
        return w_in[:, :, a:b]
    w1a = np.concatenate([cols("a_val"), cols("a_gate"), cols("a_z"), cols("u"), cols("c_z")], axis=2)
    z = lambda *sh: np.zeros(sh, f)
    kr = cols("k_r")
    krA = z(L, D, 128); krA[:, :, 64:96] = kr
    krB = z(L, D, 128); krB[:, :, 64:80] = kr[:, :, 16:32]; krB[:, :, 80:96] = kr[:, :, 0:16]
    q = cols("q")
    q02 = np.concatenate([q[:, :, 0:64], q[:, :, 128:192]], axis=2)
    q13 = np.concatenate([q[:, :, 64:128], q[:, :, 192:256]], axis=2)
    w1b = np.concatenate([cols("c_q"), cols("c_kv"), krA, krB, cols("b_z"), q02, q13, cols("k"), cols("v"), cols("d_z")], axis=2)
    W["w1a"] = np.ascontiguousarray(w1a)
    W["w1b"] = np.ascontiguousarray(w1b)
    wuq_ = np.asarray(inp["w_uq"], f)
    wq = z(L, 256, 8 * 128)
    for h in range(4):
        wq[:, :, (2 * h) * 128:(2 * h) * 128 + 96] = wuq_[:, :, h * 96:(h + 1) * 96]
        wq[:, :, (2 * h + 1) * 128 + 64:(2 * h + 1) * 128 + 80] = wuq_[:, :, h * 96 + 80:h * 96 + 96]
        wq[:, :, (2 * h + 1) * 128 + 80:(2 * h + 1) * 128 + 96] = wuq_[:, :, h * 96 + 64:h * 96 + 80]
    W["wuq"] = wq
    wukv_ = np.asarray(inp["w_ukv"], f).reshape(L, 128, 4, 128)
    W["wukv"] = np.ascontiguousarray(np.concatenate([wukv_[:, :, :, 0:64].reshape(L, 128, 256), wukv_[:, :, :, 64:128].reshape(L, 128, 256)], axis=2))
    W["wpw2"] = np.asarray(inp["w_pw2"], f)
    W["wglu"] = np.asarray(inp["w_glu"], f)
    vec = z(L, 128, NV)
    cw = np.asarray(inp["conv_w"], f)
    for c in range(2):
        vec[:, :, V_CONVW + c * 31:V_CONVW + (c + 1) * 31] = cw[:, :, c * 128:(c + 1) * 128].transpose(0, 2, 1)

    def pv(name, col, nt):
        a = np.asarray(inp[name], f).reshape(L, nt, 128)
        vec[:, :, col:col + nt] = a.transpose(0, 2, 1)
    pv("conv_b", V_CONVB, 2); pv("conv_norm_g", V_CONVG, 2); pv("conv_norm_b", V_CONVBETA, 2)
    pv("mla_q_norm_g", V_QNG, 2); pv("mla_kv_norm_g", V_KVNG, 1); pv("ssm_d", V_SSMD, 2)
    pv("b_merge", V_BMERGE, 32); pv("ln_g", V_LNG, 8); pv("ln_b", V_LNB, 8); pv("ple_norm_g", V_PLEG, 8)

    def sm(a):
        return a.reshape(L, 8, 2, 64).transpose(0, 2, 3, 1).reshape(L, 128, 8)
    vec[:, :, V_LR:V_LR + 8] = sm(np.asarray(inp["ssm_a_re"], f))
    vec[:, :, V_LI:V_LI + 8] = sm(np.asarray(inp["ssm_a_im"], f))
    vec[:, :, V_LOGDT:V_LOGDT + 8] = sm(np.repeat(np.asarray(inp["ssm_log_dt"], f)[:, :, None], 64, axis=2))
    vec[:, :, V_SINK:V_SINK + 4] = np.asarray(inp["attn_sinks"], f)[:, None, :]
    W["vecs"] = vec

    def smB(a):
        return a.reshape(L, 8, 2, 64, 16).transpose(0, 2, 3, 1, 4).reshape(L, 128, 128)

    def smC(a):
        return a.reshape(L, 8, 2, 16, 64).transpose(0, 2, 4, 1, 3).reshape(L, 128, 128)
    W["s5b"] = np.ascontiguousarray(np.concatenate([smB(np.asarray(inp["ssm_b_re"], f)), smB(np.asarray(inp["ssm_b_im"], f))], axis=2))
    W["s5c"] = np.ascontiguousarray(np.concatenate([smC(np.asarray(inp["ssm_c_re"], f)), smC(np.asarray(inp["ssm_c_im"], f))], axis=2))
    W["wmerge"] = np.asarray(inp["w_merge"], f)
    W["wbranch"] = np.asarray(inp["w_branch"], f).reshape(L, 1024, D)
    W["wout"] = np.asarray(inp["w_out"], f)
    W["wple"] = np.asarray(inp["w_ple"], f)
    W["wpleg"] = np.asarray(inp["w_ple_gate"], f)
    pos = np.arange(S, dtype=np.float64)
    inv = 10000.0 ** (-np.arange(0, 32, 2, dtype=np.float64) / 32)
    rc = np.zeros((128, S), f); rs = np.zeros((128, S), f)
    for p in range(128):
        i = p % 32
        ang = pos * inv[i % 16]
        ang = (pos.astype(f) * inv.astype(f)[i % 16]).astype(np.float64)
        rc[p] = np.cos(ang)
        rs[p] = (-1.0 if i < 16 else 1.0) * np.sin(ang)
    W["ropec"] = rc; W["ropes"] = rs
    cmv = np.zeros((128, NCM), f)
    cmv[:, C_IDENT:C_IDENT + 128] = np.eye(128, dtype=f)
    k = np.arange(128)[:, None]; qq = np.arange(128)[None, :]
    cmv[:, C_TRI:C_TRI + 128] = (k <= qq)
    cmv[:, C_SWM:C_SWM + 128] = (k > qq)
    cmv[:, C_SWM + 128:C_SWM + 256] = (k <= qq)
    cmv[:, C_IOTA:C_IOTA + 512] = np.arange(1, 513, dtype=f)[None, :]
    W["cmat"] = cmv
    return W


_NC_CACHE = {}


def run(inputs, S, L, ncores, debug=False):
    x = np.asarray(inputs["x"], np.float32)
    p = np.asarray(inputs["p"], np.float32)
    W = pack_weights(inputs, L, S)
    key = (S, L, debug)
    if key not in _NC_CACHE:
        _NC_CACHE[key] = build(S, L, debug)
    nc = _NC_CACHE[key]
    in_maps = []
    for c in range(ncores):
        m = dict(W)
        m["xT"] = np.ascontiguousarray(x[c].T)
        m["pT"] = np.ascontiguousarray(p[:, c].transpose(0, 2, 1))
        in_maps.append(m)
    res = run_bass_kernel_spmd(nc, in_maps, core_ids=list(range(ncores)))
    out = np.stack([np.ascontiguousarray(r["yT"].T) for r in res.results], axis=0)
    if debug:
        return out.astype(np.float32), [dict(r) for r in res.results]
    return out.astype(np.float32)


def kernel(**inputs):
    return run(inputs, 4096, 4, 8)
```
